# Optimizing a Trainium2 kernel written in Bass

```python
import math
import jax, jax.numpy as jnp
from jax import lax
import numpy as np

D_MODEL = 1024
BATCH = 8
SEQ = 2048
DEPTH = 2
DEC_BATCH = 8
DEC_SEQ = 64
PAST_LEN = 4096

CHUNK = 64
HEAD_DIM = 64
N_EVEN = (DEPTH + 1) // 2
N_ODD = DEPTH // 2
EPS = 1e-6
A_HEADS = 8
A_KV_HEADS = 2
A_GROUP = A_HEADS // A_KV_HEADS
WINDOW = 128
A_PREV_CHUNKS = WINDOW // CHUNK
T5_BUCKETS = 32
T5_MAX_DIST = 128
B_HEADS = 8
B_PREV_CHUNKS = 8
B_REACH = B_PREV_CHUNKS * CHUNK
B_MAX_REL = 128
C_WIDTH = 512
CONV_W = 3
D_HEADS = 8
D_Q_LORA = 256
D_KV_LORA = 128
D_NOPE = 64
D_ROPE = 32
D_V = 64
ROPE_THETA = 10000.0
Q_BLOCK = 128
FFN_DIM = 2816
A_Q = A_HEADS * HEAD_DIM
A_KV = A_KV_HEADS * HEAD_DIM
B_QKV = B_HEADS * HEAD_DIM
EVEN_IN = A_Q + 2 * A_KV + 3 * B_QKV
EVEN_MIX = A_Q + B_QKV
ODD_IN = 3 * C_WIDTH + D_Q_LORA + D_KV_LORA + D_ROPE
ODD_MIX = C_WIDTH + D_HEADS * D_V

kernel_name = "hybrid_streaming_encoder_step"


def rms_norm(x, g):
    xf = x.astype(jnp.float32)
    y = xf * lax.rsqrt(jnp.mean(xf * xf, axis=-1, keepdims=True) + EPS)
    return (y * g.astype(jnp.float32)).astype(x.dtype)


def split_cols(z, sizes):
    idx, acc = [], 0
    for s in sizes[:-1]:
        acc += s
        idx.append(acc)
    return jnp.split(z, idx, axis=-1)


def swiglu(h, w_gu, w_down):
    g, u = jnp.split(h @ w_gu, 2, axis=-1)
    return (jax.nn.silu(g) * u) @ w_down


def rope(x, pos):
    half = D_ROPE // 2
    inv = 1.0 / (ROPE_THETA ** (jnp.arange(half, dtype=jnp.float32) / half))
    ang = pos.astype(jnp.float32)[:, None] * inv[None, :]
    cos = jnp.cos(ang)[:, None, :]
    sin = jnp.sin(ang)[:, None, :]
    xf = x.astype(jnp.float32)
    x1, x2 = xf[..., :half], xf[..., half:]
    return jnp.concatenate([x1 * cos - x2 * sin, x1 * sin + x2 * cos], axis=-1).astype(x.dtype)


def t5_bucket(rel):
    nb = T5_BUCKETS // 2
    max_exact = nb // 2
    n = -rel
    ret = jnp.where(n < 0, nb, 0)
    n = jnp.abs(n)
    nf = jnp.maximum(n, 1).astype(jnp.float32)
    large = max_exact + (jnp.log(nf / max_exact) / math.log(T5_MAX_DIST / max_exact) * (nb - max_exact)).astype(jnp.int32)
    large = jnp.minimum(large, nb - 1)
    return ret + jnp.where(n < max_exact, n, large)


def t5_bias(table, q_pos, k_pos):
    b = table.astype(jnp.float32)[t5_bucket(k_pos[None, :] - q_pos[:, None])]
    return b.transpose(2, 0, 1).reshape(A_KV_HEADS, A_GROUP, q_pos.shape[0], k_pos.shape[0])


def rel_bias(table, q_pos, k_pos):
    idx = jnp.clip(k_pos[None, :] - q_pos[:, None], -B_MAX_REL, B_MAX_REL) + B_MAX_REL
    return table.astype(jnp.float32)[:, idx][:, None]


def chunk_visible(q_pos, k_pos, n_prev):
    qc = q_pos[:, None] // CHUNK
    kc = k_pos[None, :] // CHUNK
    m = (kc <= qc) & (k_pos[None, :] >= 0)
    if n_prev is not None:
        m = m & (kc >= qc - n_prev)
    return m


def attend(q, k, v, q_pos, k_pos, n_prev, bias=None, sinks=None):
    b, tq = q.shape[0], q.shape[1]
    s = jnp.einsum('bqkgd,bskd->bkgqs', q, k, preferred_element_type=jnp.float32) * (q.shape[-1] ** -0.5)
    if bias is not None:
        s = s + bias
    s = jnp.where(chunk_visible(q_pos, k_pos, n_prev), s, -jnp.inf)
    if sinks is None:
        p = jax.nn.softmax(s, axis=-1)
    else:
        sk = sinks.astype(jnp.float32)[None, :, :, None, None]
        m = jnp.maximum(jnp.max(s, axis=-1, keepdims=True), sk)
        e = jnp.exp(s - m)
        p = e / (jnp.sum(e, axis=-1, keepdims=True) + jnp.exp(sk - m))
    o = jnp.einsum('bkgqs,bskd->bqkgd', p.astype(v.dtype), v)
    return o.reshape(b, tq, -1)


def band_attention_prompt(q, k, v, n_prev, bias_fn, sinks):
    b, s = q.shape[0], q.shape[1]
    pad = n_prev * CHUNK
    band = pad + CHUNK
    kp = jnp.pad(k, ((0, 0), (pad, 0), (0, 0), (0, 0)))
    vp = jnp.pad(v, ((0, 0), (pad, 0), (0, 0), (0, 0)))

    def one_chunk(c):
        start = c * CHUNK
        qb = lax.dynamic_slice_in_dim(q, start, CHUNK, axis=1)
        kb = lax.dynamic_slice_in_dim(kp, start, band, axis=1)
        vb = lax.dynamic_slice_in_dim(vp, start, band, axis=1)
        q_pos = start + jnp.arange(CHUNK)
        k_pos = start - pad + jnp.arange(band)
        return attend(qb, kb, vb, q_pos, k_pos, n_prev, bias_fn(q_pos, k_pos), sinks)

    out = lax.map(one_chunk, jnp.arange(s // CHUNK))
    return out.transpose(1, 0, 2, 3).reshape(b, s, -1)


def even_qkv(h, p):
    b, t, _ = h.shape
    aq, ak, av, bq, bk, bv = split_cols(h @ p['w_in'], (A_Q, A_KV, A_KV, B_QKV, B_QKV, B_QKV))
    aq = rms_norm(aq.reshape(b, t, A_KV_HEADS, A_GROUP, HEAD_DIM), p['a_qn'])
    ak = rms_norm(ak.reshape(b, t, A_KV_HEADS, HEAD_DIM), p['a_kn'])
    av = av.reshape(b, t, A_KV_HEADS, HEAD_DIM)
    bq = rms_norm(bq.reshape(b, t, B_HEADS, 1, HEAD_DIM), p['b_qn'])
    bk = rms_norm(bk.reshape(b, t, B_HEADS, HEAD_DIM), p['b_kn'])
    bv = bv.reshape(b, t, B_HEADS, HEAD_DIM)
    return aq, ak, av, bq, bk, bv


def even_prompt(h, p, t5_table):
    s = h.shape[1]
    aq, ak, av, bq, bk, bv = even_qkv(h, p)
    sinks = p['a_sinks'].reshape(A_KV_HEADS, A_GROUP)
    ya = band_attention_prompt(aq, ak, av, A_PREV_CHUNKS, lambda qp, kp: t5_bias(t5_table, qp, kp), sinks)
    yb = band_attention_prompt(bq, bk, bv, B_PREV_CHUNKS, lambda qp, kp: rel_bias(p['b_rel'], qp, kp), None)
    y = jnp.concatenate([ya, yb], axis=-1) @ p['w_out']
    la, lb = min(WINDOW, s), min(B_REACH, s)
    return y, (ak[:, s - la:], av[:, s - la:], bk[:, s - lb:], bv[:, s - lb:])


def even_sample(h, p, t5_table, ck_a, cv_a, ck_b, cv_b):
    t = h.shape[1]
    la, lb = ck_a.shape[1], ck_b.shape[1]
    aq, ak, av, bq, bk, bv = even_qkv(h, p)
    q_pos = PAST_LEN + jnp.arange(t)
    ka = jnp.concatenate([ck_a, ak], axis=1)
    va = jnp.concatenate([cv_a, av], axis=1)
    kpa = jnp.concatenate([PAST_LEN - la + jnp.arange(la), q_pos])
    ya = attend(aq, ka, va, q_pos, kpa, A_PREV_CHUNKS, t5_bias(t5_table, q_pos, kpa), p['a_sinks'].reshape(A_KV_HEADS, A_GROUP))
    kb = jnp.concatenate([ck_b, bk], axis=1)
    vb = jnp.concatenate([cv_b, bv], axis=1)
    kpb = jnp.concatenate([PAST_LEN - lb + jnp.arange(lb), q_pos])
    yb = attend(bq, kb, vb, q_pos, kpb, B_PREV_CHUNKS, rel_bias(p['b_rel'], q_pos, kpb), None)
    y = jnp.concatenate([ya, yb], axis=-1) @ p['w_out']
    return y, (ka[:, -la:], va[:, -la:], kb[:, -lb:], vb[:, -lb:])


def odd_project(h, p, pos):
    b, t, _ = h.shape
    cb, cc, ch, qa, kva, kr = split_cols(h @ p['w_in'], (C_WIDTH, C_WIDTH, C_WIDTH, D_Q_LORA, D_KV_LORA, D_ROPE))
    u = cc * ch
    q = (rms_norm(qa, p['q_a_norm']) @ p['w_q_b']).reshape(b, t, D_HEADS, D_NOPE + D_ROPE)
    q = jnp.concatenate([rms_norm(q[..., :D_NOPE], p['qn_nope']),
                         rope(rms_norm(q[..., D_NOPE:], p['qn_rope']), pos)], axis=-1)
    ckv = rms_norm(kva, p['kv_a_norm'])
    kpe = rope(rms_norm(kr, p['kn_rope'])[:, :, None, :], pos)[:, :, 0, :]
    return cb, u, q[:, :, :, None, :], ckv, kpe


def short_conv(u, prev, w):
    t = u.shape[1]
    up = jnp.concatenate([prev, u], axis=1)
    y = w[0] * up[:, 0:t]
    for j in range(1, CONV_W):
        y = y + w[j] * up[:, j:j + t]
    return y, up[:, -(CONV_W - 1):]


def mla_keys(ckv, kpe, p):
    b, tk, _ = ckv.shape
    kv = (ckv @ p['w_kv_b']).reshape(b, tk, D_HEADS, D_NOPE + D_V)
    k = jnp.concatenate([rms_norm(kv[..., :D_NOPE], p['kn_nope']),
                         jnp.broadcast_to(kpe[:, :, None, :], (b, tk, D_HEADS, D_ROPE))], axis=-1)
    return k, kv[..., D_NOPE:]


def odd_prompt(h, p):
    b, s, _ = h.shape
    pos = jnp.arange(s)
    cb, u, q, ckv, kpe = odd_project(h, p, pos)
    yconv, conv_state = short_conv(u, jnp.zeros((b, CONV_W - 1, C_WIDTH), u.dtype), p['conv_w'])
    yc = cb * yconv
    k, v = mla_keys(ckv, kpe, p)

    def one_block(i):
        start = i * Q_BLOCK
        qb = lax.dynamic_slice_in_dim(q, start, Q_BLOCK, axis=1)
        return attend(qb, k, v, start + jnp.arange(Q_BLOCK), pos, None)

    yd = lax.map(one_block, jnp.arange(s // Q_BLOCK))
    yd = yd.transpose(1, 0, 2, 3).reshape(b, s, -1)
    y = jnp.concatenate([yc, yd], axis=-1) @ p['w_out']
    return y, (conv_state, ckv, kpe)


def odd_sample(h, p, conv_prev, c_ckv, c_kpe):
    t = h.shape[1]
    pos = PAST_LEN + jnp.arange(t)
    cb, u, q, ckv, kpe = odd_project(h, p, pos)
    yconv, conv_state = short_conv(u, conv_prev, p['conv_w'])
    yc = cb * yconv
    k, v = mla_keys(jnp.concatenate([c_ckv, ckv], axis=1), jnp.concatenate([c_kpe, kpe], axis=1), p)
    yd = attend(q, k, v, pos, jnp.arange(PAST_LEN + t), None)
    y = jnp.concatenate([yc, yd], axis=-1) @ p['w_out']
    return y, (conv_state, ckv, kpe)


def setup_inputs(seed: int = 0) -> dict:
    key = jax.random.key(seed)
    ks = jax.random.split(key, 48)
    cnt = [0]

    def nk():
        cnt[0] += 1
        return ks[cnt[0] - 1]

    def nrm(shape, scale):
        return jax.random.normal(nk(), shape, jnp.float32) * scale

    def gain(shape):
        return 1.0 + 0.02 * jax.random.normal(nk(), shape, jnp.float32)

    la = min(WINDOW, PAST_LEN)
    lb = min(B_REACH, PAST_LEN)
    return {
        'x_prompt': nrm((BATCH, SEQ, D_MODEL), 1.0),
        'x_sample': nrm((DEC_BATCH, DEC_SEQ, D_MODEL), 1.0),
        'cache_a_k': nrm((N_EVEN, DEC_BATCH, la, A_KV_HEADS, HEAD_DIM), 1.0),
        'cache_a_v': nrm((N_EVEN, DEC_BATCH, la, A_KV_HEADS, HEAD_DIM), 1.0),
        'cache_b_k': nrm((N_EVEN, DEC_BATCH, lb, B_HEADS, HEAD_DIM), 1.0),
        'cache_b_v': nrm((N_EVEN, DEC_BATCH, lb, B_HEADS, HEAD_DIM), 1.0),
        'state_c_conv': nrm((N_ODD, DEC_BATCH, CONV_W - 1, C_WIDTH), 1.0),
        'cache_d_ckv': nrm((N_ODD, DEC_BATCH, PAST_LEN, D_KV_LORA), 1.0),
        'cache_d_kpe': nrm((N_ODD, DEC_BATCH, PAST_LEN, D_ROPE), 1.0),
        'ff1_norm': gain((DEPTH, D_MODEL)),
        'ff1_w_gu': nrm((DEPTH, D_MODEL, 2 * FFN_DIM), D_MODEL ** -0.5),
        'ff1_w_down': nrm((DEPTH, FFN_DIM, D_MODEL), FFN_DIM ** -0.5),
        'mix_norm': gain((DEPTH, D_MODEL)),
        'ff2_norm': gain((DEPTH, D_MODEL)),
        'ff2_w_gu': nrm((DEPTH, D_MODEL, 2 * FFN_DIM), D_MODEL ** -0.5),
        'ff2_w_down': nrm((DEPTH, FFN_DIM, D_MODEL), FFN_DIM ** -0.5),
        't5_bias_table': nrm((T5_BUCKETS, A_HEADS), 0.5),
        'ev_w_in': nrm((N_EVEN, D_MODEL, EVEN_IN), D_MODEL ** -0.5),
        'ev_w_out': nrm((N_EVEN, EVEN_MIX, D_MODEL), EVEN_MIX ** -0.5),
        'a_q_norm': gain((N_EVEN, HEAD_DIM)),
        'a_k_norm': gain((N_EVEN, HEAD_DIM)),
        'a_sinks': nrm((N_EVEN, A_HEADS), 1.0),
        'b_q_norm': gain((N_EVEN, HEAD_DIM)),
        'b_k_norm': gain((N_EVEN, HEAD_DIM)),
        'b_rel_bias': nrm((N_EVEN, B_HEADS, 2 * B_MAX_REL + 1), 0.5),
        'od_w_in': nrm((N_ODD, D_MODEL, ODD_IN), D_MODEL ** -0.5),
        'od_w_out': nrm((N_ODD, ODD_MIX, D_MODEL), ODD_MIX ** -0.5),
        'c_conv_w': nrm((N_ODD, CONV_W, C_WIDTH), CONV_W ** -0.5),
        'd_q_a_norm': gain((N_ODD, D_Q_LORA)),
        'd_w_q_b': nrm((N_ODD, D_Q_LORA, D_HEADS * (D_NOPE + D_ROPE)), D_Q_LORA ** -0.5),
        'd_kv_a_norm': gain((N_ODD, D_KV_LORA)),
        'd_w_kv_b': nrm((N_ODD, D_KV_LORA, D_HEADS * (D_NOPE + D_V)), D_KV_LORA ** -0.5),
        'd_q_nope_norm': gain((N_ODD, D_NOPE)),
        'd_q_rope_norm': gain((N_ODD, D_ROPE)),
        'd_k_nope_norm': gain((N_ODD, D_NOPE)),
        'd_k_rope_norm': gain((N_ODD, D_ROPE)),
    }


def reference(x_prompt, x_sample, cache_a_k, cache_a_v, cache_b_k, cache_b_v, state_c_conv, cache_d_ckv, cache_d_kpe,
              ff1_norm, ff1_w_gu, ff1_w_down, mix_norm, ff2_norm, ff2_w_gu, ff2_w_down, t5_bias_table,
              ev_w_in, ev_w_out, a_q_norm, a_k_norm, a_sinks, b_q_norm, b_k_norm, b_rel_bias,
              od_w_in, od_w_out, c_conv_w, d_q_a_norm, d_w_q_b, d_kv_a_norm, d_w_kv_b,
              d_q_nope_norm, d_q_rope_norm, d_k_nope_norm, d_k_rope_norm):
    xp, xs = x_prompt, x_sample
    st = {n: [] for n in ('akp', 'avp', 'bkp', 'bvp', 'cp', 'ckvp', 'kpep',
                          'aks', 'avs', 'bks', 'bvs', 'cs', 'ckvs', 'kpes')}
    for l in range(DEPTH):
        i = l // 2
        xp = xp + 0.5 * swiglu(rms_norm(xp, ff1_norm[l]), ff1_w_gu[l], ff1_w_down[l])
        xs = xs + 0.5 * swiglu(rms_norm(xs, ff1_norm[l]), ff1_w_gu[l], ff1_w_down[l])
        hp = rms_norm(xp, mix_norm[l])
        hs = rms_norm(xs, mix_norm[l])
        if l % 2 == 0:
            p = {'w_in': ev_w_in[i], 'w_out': ev_w_out[i], 'a_qn': a_q_norm[i], 'a_kn': a_k_norm[i],
                 'a_sinks': a_sinks[i], 'b_qn': b_q_norm[i], 'b_kn': b_k_norm[i], 'b_rel': b_rel_bias[i]}
            yp, (akp, avp, bkp, bvp) = even_prompt(hp, p, t5_bias_table)
            ys, (aks, avs, bks, bvs) = even_sample(hs, p, t5_bias_table, cache_a_k[i], cache_a_v[i],
                                                   cache_b_k[i], cache_b_v[i])
            for n, a in (('akp', akp), ('avp', avp), ('bkp', bkp), ('bvp', bvp),
                         ('aks', aks), ('avs', avs), ('bks', bks), ('bvs', bvs)):
                st[n].append(a)
        else:
            p = {'w_in': od_w_in[i], 'w_out': od_w_out[i], 'conv_w': c_conv_w[i], 'q_a_norm': d_q_a_norm[i],
                 'w_q_b': d_w_q_b[i], 'kv_a_norm': d_kv_a_norm[i], 'w_kv_b': d_w_kv_b[i],
                 'qn_nope': d_q_nope_norm[i], 'qn_rope': d_q_rope_norm[i],
                 'kn_nope': d_k_nope_norm[i], 'kn_rope': d_k_rope_norm[i]}
            yp, (cp, ckvp, kpep) = odd_prompt(hp, p)
            ys, (cs, ckvs, kpes) = odd_sample(hs, p, state_c_conv[i], cache_d_ckv[i], cache_d_kpe[i])
            for n, a in (('cp', cp), ('ckvp', ckvp), ('kpep', kpep), ('cs', cs), ('ckvs', ckvs), ('kpes', kpes)):
                st[n].append(a)
        xp = xp + yp
        xs = xs + ys
        xp = xp + 0.5 * swiglu(rms_norm(xp, ff2_norm[l]), ff2_w_gu[l], ff2_w_down[l])
        xs = xs + 0.5 * swiglu(rms_norm(xs, ff2_norm[l]), ff2_w_gu[l], ff2_w_down[l])
    return (xp, xs,
            jnp.stack(st['akp']), jnp.stack(st['avp']), jnp.stack(st['bkp']), jnp.stack(st['bvp']),
            jnp.stack(st['cp']), jnp.stack(st['ckvp']), jnp.stack(st['kpep']),
            jnp.stack(st['aks']), jnp.stack(st['avs']), jnp.stack(st['bks']), jnp.stack(st['bvs']),
            jnp.stack(st['cs']), jnp.stack(st['ckvs']), jnp.stack(st['kpes']))
```

```python
import contextlib
import os
import numpy as np
import ml_dtypes
import concourse.bass as bass
import concourse.mybir as mybir
from concourse.bass_utils import run_bass_kernel_spmd

F32 = mybir.dt.float32
BF16 = mybir.dt.bfloat16
ALU = mybir.AluOpType
AF = mybir.ActivationFunctionType

EPS = 1e-6
T = 2112
BLKS = [(0, 512), (512, 512), (1024, 512), (1536, 512), (2048, 64)]
NTILE = 17


def tile_rows(i):
    return (i * 128, 128) if i < 16 else (2048, 64)


class Buf:
    __slots__ = ("name", "writers", "readers", "sem", "semcount", "psum")

    def __init__(self, name, psum=False):
        self.name = name
        self.writers = []
        self.readers = []
        self.sem = None
        self.semcount = 0
        self.psum = psum


class Op:
    __slots__ = ("eng", "fn", "deps", "is_dma", "sem", "val", "signals", "idx")

    def __init__(self, eng, fn):
        self.eng = eng
        self.fn = fn
        self.deps = []
        self.is_dma = False
        self.sem = None
        self.val = 0
        self.signals = False


ENGS = ("pe", "act", "dve", "pool", "sp")
STRICT_SAME_ENGINE = False


class Prog:
    def __init__(self, nc):
        self.nc = nc
        self.ops = []
        self.out_dmas = []
        self.last = {}
        self.pending_dma = []

    def add(self, eng, fn, reads=(), writes=(), dma_buf=None, is_out=False):
        op = Op(eng, fn)
        op.idx = len(self.ops)
        deps = []
        for b in reads:
            deps.extend((d, 0) for d in b.writers)
            if b.psum:
                deps.extend((d, 2) for d in b.readers if d.eng != eng)
        for b in writes:
            deps.extend((d, 1) for d in b.writers)
            deps.extend((d, 1) for d in b.readers)
        seen = set()
        for d, kind in deps:
            if d is op or id(d) in seen:
                continue
            if d.eng == eng and not d.is_dma:
                if eng == "pe" or (kind != 0 and not STRICT_SAME_ENGINE):
                    continue
            seen.add(id(d))
            op.deps.append(d)
        for b in writes:
            b.writers = [op]
            b.readers = []
        for b in reads:
            if b not in writes:
                b.readers.append(op)
        if dma_buf is not None:
            op.is_dma = True
            op.signals = True
            op.sem = dma_buf
            self.pending_dma.append(op)
        self.ops.append(op)
        if not op.is_dma:
            self.last[eng] = op
        if is_out:
            self.out_dmas.append(op)
        return op

    def barrier(self):
        lasts = [o for o in self.last.values() if not o.is_dma]
        dmas = list(self.pending_dma)
        self.pending_dma = []
        for e in ENGS:
            op = Op(e, None)
            op.idx = len(self.ops)
            op.deps = [o for o in lasts if o.eng != e] + dmas
            self.ops.append(op)

    def finalize_and_emit(self):
        nc = self.nc
        fin = Op("sp", None)
        fin.deps = list(self.out_dmas)
        fin.idx = len(self.ops)
        self.ops.append(fin)
        for op in self.ops:
            for d in op.deps:
                d.signals = True
        with contextlib.ExitStack() as es:
            engsem = {}
            for e in ("pe", "act", "dve", "pool"):
                engsem[e] = es.enter_context(nc.semaphore("c_" + e))
            counters = {e: 0 for e in engsem}
            for op in self.ops:
                if op.is_dma:
                    b = op.sem
                    if b.sem is None:
                        b.sem = es.enter_context(nc.semaphore("d_" + b.name))
                        b.semcount = 0
                    b.semcount += 16
                    op.sem = b.sem
                    op.val = b.semcount
                elif op.signals and op.eng in engsem:
                    counters[op.eng] += 1
                    op.sem = engsem[op.eng]
                    op.val = counters[op.eng]
            block = es.enter_context(nc.Block())
            engobj = {"pe": "tensor", "act": "scalar", "dve": "vector",
                      "pool": "gpsimd", "sp": "sync"}

            def make(ename):
                def body(eng):
                    known = {}
                    for op in self.ops:
                        if op.eng != ename:
                            continue
                        need = {}
                        for d in op.deps:
                            if d.sem is None:
                                continue
                            k = id(d.sem)
                            if k not in need or need[k][1] < d.val:
                                need[k] = (d.sem, d.val)
                        for k, (s, v) in need.items():
                            if known.get(k, 0) >= v:
                                continue
                            eng.wait_ge(s, v)
                            known[k] = v
                        if op.fn is None:
                            continue
                        ins = op.fn(eng)
                        if op.signals:
                            ins.then_inc(op.sem, 16 if op.is_dma else 1)
                return body

            for ename in ENGS:
                getattr(block, engobj[ename])(make(ename))


class Arena:
    def __init__(self, ap, words):
        self.ap = ap
        self.words = words
        self.off = 0
        self.n = 0

    def mark(self):
        return self.off

    def reset(self, m=0):
        self.off = m

    def alloc(self, shape, dt, name=None):
        n = int(np.prod(shape[1:]))
        nw = n if dt == F32 else (n + 1) // 2
        assert self.off + nw <= self.words, (name, self.off, nw, self.words)
        v = self.ap[:, self.off:self.off + nw]
        if dt != F32:
            v = v.bitcast(dt)[:, 0:n]
        self.off += nw
        if len(shape) == 3:
            v = v.rearrange("p (a b) -> p a b", a=shape[1])
        elif len(shape) == 4:
            v = v.rearrange("p (a b c) -> p a b c", a=shape[1], b=shape[2])
        self.n += 1
        return v


def _t5_bucket_np(rel):
    import jax.numpy as jnp
    import math
    import jax
    with jax.default_device(jax.devices("cpu")[0]):
        return _t5_bucket_cpu(rel)


def _t5_bucket_cpu(rel):
    import jax.numpy as jnp
    import math
    rel = jnp.asarray(rel, dtype=jnp.int32)
    nb = 16
    max_exact = 8
    n = -rel
    ret = jnp.where(n < 0, nb, 0)
    n = jnp.abs(n)
    nf = jnp.maximum(n, 1).astype(jnp.float32)
    large = max_exact + (jnp.log(nf / max_exact) / math.log(128 / max_exact) * (nb - max_exact)).astype(jnp.int32)
    large = jnp.minimum(large, nb - 1)
    return np.asarray(ret + jnp.where(n < max_exact, n, large))


def host_consts():
    c = {}
    c["ident"] = np.eye(128, dtype=np.float32)
    cm = np.zeros((128, 6, 128), np.float32)
    cm[:, 0, :] = 1.0 / 1024
    cm[0:64, 1, 0:64] = 1.0 / 64
    cm[64:128, 1, 64:128] = 1.0 / 64
    cm[:, 2, :] = 1.0 / 256
    cm[:, 3, :] = 1.0 / 128
    cm[0:64, 4, 0:64] = 1.0 / 64
    cm[64:96, 4, 64:96] = 1.0 / 32
    cm[96:128, 4, 96:128] = 1.0 / 32
    for k in range(128):
        m = k if k < 96 else k - 32
        cm[k, 5, m] = 1.0
    c["cmat"] = cm.astype(ml_dtypes.bfloat16)
    j = np.arange(384)
    bk = _t5_bucket_np(127 - j)
    oha = np.zeros((32, 384), np.float32)
    oha[bk, j] = 1.0
    c["oh_a"] = oha
    j = np.arange(768)
    idx = np.clip(127 - j, -128, 128) + 128
    ohb = np.zeros((256, 768), np.float32)
    ohb[idx, j] = 1.0
    c["oh_b"] = np.ascontiguousarray(ohb.reshape(2, 128, 768).transpose(1, 0, 2))
    p = np.arange(128)[:, None] // 64
    ya = np.arange(256)[None, :] // 64
    c["vis_a"] = ((ya - p >= 0) & (ya - p <= 2)).astype(np.float32)
    yb = np.arange(640)[None, :] // 64
    c["vis_b"] = ((yb - p >= 0) & (yb - p <= 8)).astype(np.float32)
    pos = np.concatenate([np.arange(2048), 4096 + np.arange(64)]).astype(np.float32)
    half = 16
    inv = (1.0 / (np.float32(10000.0) ** (np.arange(half, dtype=np.float32) / np.float32(half)))).astype(np.float32)
    ang = (pos[None, :] * inv[:, None]).astype(np.float32)
    cos = np.cos(ang).astype(np.float32)
    sin = np.sin(ang).astype(np.float32)
    tab = np.ones((128, T), np.float32)
    tab[64:80] = cos
    tab[80:96] = cos
    tab[96:112] = -sin
    tab[112:128] = sin
    c["tabq"] = tab
    sk = np.zeros((128, T), np.float32)
    sk[64:80] = -sin
    sk[80:96] = sin
    c["sink"] = sk
    return c


NVEC = 70


def pack_vec(inp):
    v = np.zeros((128, NVEC), np.float32)

    def col8(a):
        return np.asarray(a, np.float32).reshape(8, 128).T

    for l in range(2):
        v[:, 24 * l + 0:24 * l + 8] = col8(inp["ff1_norm"][l])
        v[:, 24 * l + 8:24 * l + 16] = col8(inp["mix_norm"][l])
        v[:, 24 * l + 16:24 * l + 24] = col8(inp["ff2_norm"][l])
    for k, nm in enumerate(["a_q_norm", "a_k_norm", "b_q_norm", "b_k_norm"]):
        g = np.asarray(inp[nm][0], np.float32)
        v[:, 48 + k] = np.concatenate([g, g])
    v[:, 52:54] = np.asarray(inp["d_q_a_norm"][0], np.float32).reshape(2, 128).T
    v[:, 54] = np.asarray(inp["d_kv_a_norm"][0], np.float32)
    qn = np.asarray(inp["d_q_nope_norm"][0], np.float32)
    qr = np.asarray(inp["d_q_rope_norm"][0], np.float32)
    kn = np.asarray(inp["d_k_nope_norm"][0], np.float32)
    kr = np.asarray(inp["d_k_rope_norm"][0], np.float32)
    sw = lambda a: np.concatenate([a[16:], a[:16]])
    v[:, 55] = np.concatenate([qn, qr, sw(qr)])
    v[0:64, 56] = kn
    v[64:96, 56] = kr
    v[96:128, 56] = sw(kr)
    v[:, 57] = np.concatenate([kn, kn])
    cw = np.asarray(inp["c_conv_w"][0], np.float32)
    for jj in range(3):
        v[:, 58 + 4 * jj:58 + 4 * jj + 4] = cw[jj].reshape(4, 128).T
    return v


def build(stop=None):
    nc = bass.Bass("TRN2", target_bir_lowering=False)

    def din(name, shape, dt=F32):
        return nc.dram_tensor(name, list(shape), dt, kind="ExternalInput").ap()

    def dout(name, shape):
        return nc.dram_tensor(name, list(shape), F32, kind="ExternalOutput").ap()

    xp_d = din("xp", [2048, 1024])
    xs_d = din("xs", [64, 1024])
    cak_d = din("cak", [128, 128])
    cav_d = din("cav", [128, 128])
    cbk_d = din("cbk", [512, 512])
    cbv_d = din("cbv", [512, 512])
    ccv_d = din("ccv", [2, 512])
    ckv_d = din("ckv", [4096, 128])
    ckp_d = din("ckp", [4096, 32])
    wgu_d = [din("ff1_w_gu", [2, 1024, 5632]), din("ff2_w_gu", [2, 1024, 5632])]
    wdn_d = [din("ff1_w_down", [2, 2816, 1024]), din("ff2_w_down", [2, 2816, 1024])]
    evin_d = din("ev_w_in", [1024, 2304])
    evout_d = din("ev_w_out", [1024, 1024])
    odin_d = din("od_w_in", [1024, 1952])
    odout_d = din("od_w_out", [1024, 1024])
    wqb_d = din("wqb", [256, 8, 128])
    wkvb_d = din("wkvb", [128, 1024])
    vec_d = din("vec", [128, NVEC])
    t5_d = din("t5", [32, 8])
    brel_d = din("brel", [8, 257])
    sinks_d = din("sinks", [1, 8])
    ident_d = din("ident", [128, 128])
    cmat_d = din("cmat", [128, 6, 128], BF16)
    oha_d = din("oh_a", [32, 384])
    ohb_d = din("oh_b", [128, 2, 768])
    visa_d = din("vis_a", [128, 256])
    visb_d = din("vis_b", [128, 640])
    tabq_d = din("tabq", [128, T])
    sink_d = din("sink", [128, T])

    o_yp = dout("o_yp", [2048, 1024])
    o_ys = dout("o_ys", [64, 1024])
    o_akp = dout("o_akp", [128, 128])
    o_avp = dout("o_avp", [128, 128])
    o_bkp = dout("o_bkp", [512, 512])
    o_bvp = dout("o_bvp", [512, 512])
    o_cp = dout("o_cp", [2, 512])
    o_ckvp = dout("o_ckvp", [2048, 128])
    o_kpep = dout("o_kpep", [2048, 32])
    o_aks = dout("o_aks", [128, 128])
    o_avs = dout("o_avs", [128, 128])
    o_bks = dout("o_bks", [512, 512])
    o_bvs = dout("o_bvs", [512, 512])
    o_cs = dout("o_cs", [2, 512])
    o_ckvs = dout("o_ckvs", [64, 128])
    o_kpes = dout("o_kpes", [64, 32])
    gscr = nc.dram_tensor("gscr", [16, 128, 768], F32, kind="Internal").ap()

    P = Prog(nc)
    es = contextlib.ExitStack()
    with es:
        def sbt(name, shape, dt):
            return es.enter_context(nc.sbuf_tensor("s_" + name, shape, dt))

        xT = sbt("xT", [128, 8, T], F32)
        hn = sbt("hn", [128, 8, T], BF16)
        ident = sbt("ident", [128, 128], F32)
        cmat = sbt("cmat", [128, 6, 128], BF16)
        vec = sbt("vec", [128, NVEC], F32)
        AW = 27000
        arena_t = sbt("arena", [128, AW], F32)
        ar = Arena(arena_t, AW)
        psb = [es.enter_context(nc.psum_tensor("ps%d" % i, [128, 512], F32)) for i in range(8)]
        PB = [Buf("ps%d" % i, psum=True) for i in range(8)]

        bufn = [0]

        def nb(name="b"):
            bufn[0] += 1
            return Buf("%s%d" % (name, bufn[0]))

        B_x = [nb("x") for _ in BLKS]
        B_hn = [nb("hn") for _ in BLKS]
        B_const = nb("const")
        B_hnall = None

        def mm(out, lhsT, rhs, start, stop, R, W):
            P.add("pe", lambda e: e.matmul(out, lhsT=lhsT, rhs=rhs, start=start, stop=stop,
                                           skip_group_check=True), reads=R, writes=W)

        def tr(out, in_, idn, R, W):
            P.add("pe", lambda e: e.transpose(out=out, in_=in_, identity=idn), reads=R, writes=W)

        def act(out, in_, func, R, W, scale=1.0, bias=0.0):
            P.add("act", lambda e: e.activation(out=out, in_=in_, func=func, bias=bias, scale=scale),
                  reads=R, writes=W)

        def cp(eng, out, in_, R, W):
            if eng == "act":
                P.add("act", lambda e: e.copy(out=out, in_=in_), reads=R, writes=W)
            else:
                P.add(eng, lambda e: e.tensor_copy(out=out, in_=in_), reads=R, writes=W)

        def tt(eng, out, in0, in1, op, R, W):
            P.add(eng, lambda e: e.tensor_tensor(out=out, in0=in0, in1=in1, op=op), reads=R, writes=W)

        def ts(eng, out, in0, s1, s2, op0, op1, R, W):
            if s2 is None:
                P.add(eng, lambda e: e.tensor_scalar(out=out, in0=in0, scalar1=s1, scalar2=None, op0=op0),
                      reads=R, writes=W)
            else:
                P.add(eng, lambda e: e.tensor_scalar(out=out, in0=in0, scalar1=s1, scalar2=s2, op0=op0, op1=op1),
                      reads=R, writes=W)

        def stt(eng, out, in0, scalar, in1, op0, op1, R, W):
            P.add(eng, lambda e: e.scalar_tensor_tensor(out=out, in0=in0, scalar=scalar, in1=in1, op0=op0, op1=op1),
                  reads=R, writes=W)

        def recip(out, in_, R, W):
            P.add("dve", lambda e: e.reciprocal(out=out, in_=in_), reads=R, writes=W)

        def rstd_from(out, ss_psum, R, B_out):
            act(out, ss_psum, AF.Ln, R, [B_out], bias=EPS)
            act(out, out, AF.Exp, [B_out], [B_out], scale=-0.5)

        def memset(eng, ap, val, W):
            P.add(eng, lambda e: e.memset(ap, val), writes=W)

        def dma(q, out, in_, R, W, buf, is_out=False):
            P.add(q, lambda e: e.dma_start(out=out, in_=in_), reads=R, writes=W, dma_buf=buf, is_out=is_out)

        def dma_nc(q, out, in_, R, W, buf, is_out=False):
            P.add(q, lambda e: e.dma_start(out=out, in_=in_, allow_slow_non_contiguous=True),
                  reads=R, writes=W, dma_buf=buf, is_out=is_out)

        rr = [0]

        def ev_eng():
            rr[0] += 1
            return "act" if rr[0] % 2 else "dve"

        dma("sp", ident[:], ident_d, [], [B_const], B_const)
        dma("sp", cmat[:], cmat_d, [], [B_const], B_const)
        dma("sp", vec[:], vec_d, [], [B_const], B_const)
        CM_ONES1024, CM_BD64, CM_ONES256, CM_ONES128, CM_BD3, CM_AMAT = range(6)

        FFN_END = 24512
        ar.reset(FFN_END)
        stg = [ar.alloc([128, 1024], F32) for _ in range(2)]
        B_stg = [nb("stg") for _ in range(2)]
        for i in range(NTILE):
            r0, nr = tile_rows(i)
            s = i % 2
            src = xp_d[r0:r0 + nr, :] if i < 16 else xs_d[:, :]
            dma("sp", stg[s][0:nr, :], src, [], [B_stg[s]], B_stg[s])
            blk = min(i // 4, 4)
            for hf in range(2):
                bk = (2 * i + hf) % 4
                for k in range(4):
                    c = 4 * hf + k
                    tr(psb[bk][:, k * 128:k * 128 + nr], stg[s][0:nr, c * 128:(c + 1) * 128], ident[0:nr, 0:nr],
                       [B_stg[s], B_const], [PB[bk]])
                src_ps = psb[bk][:, :].rearrange("p (c t) -> p c t", c=4)[:, :, 0:nr]
                cp(ev_eng(), xT[:, 4 * hf:4 * hf + 4, r0:r0 + nr], src_ps, [PB[bk]], [B_x[blk]])
        pass

        def norm_block(bi, gcol0, sq, rs, B_sq, B_rs, part="ab"):
            c0, n = BLKS[bi]
            if "a" in part:
                P.add("pool", lambda e: e.tensor_tensor(out=sq[:, :, 0:n], in0=xT[:, :, c0:c0 + n],
                                                        in1=xT[:, :, c0:c0 + n], op=ALU.mult),
                      reads=[B_x[bi]], writes=[B_sq])
            if "b" not in part:
                return
            for c in range(8):
                mm(psb[6][:, 0:n], cmat[:, CM_ONES1024, :], sq[:, c, 0:n], c == 0, c == 7, [B_sq, B_const], [PB[6]])
            rstd_from(rs[:, 0:n], psb[6][:, 0:n], [PB[6]], B_rs)
            for c in range(8):
                stt("dve", hn[:, c, c0:c0 + n], xT[:, c, c0:c0 + n], vec[:, gcol0 + c:gcol0 + c + 1], rs[:, 0:n],
                    ALU.mult, ALU.mult, [B_x[bi], B_rs, B_const], [B_hn[bi]])

        FB = {}

        def ffn(l, which, skip_norm01=False, next_gcol=None, end_barrier=True):
            wgu = wgu_d[which]
            wdn = wdn_d[which]
            gcol0 = 24 * l + (0 if which == 0 else 16)
            ar.reset()
            h = ar.alloc([128, 22, 1088], BF16)
            NWS = 3
            wg = [ar.alloc([128, 8, 256], BF16) for _ in range(NWS)]
            wu = [ar.alloc([128, 8, 256], BF16) for _ in range(NWS)]
            wd = [ar.alloc([128, 22, 128], BF16) for _ in range(2)]
            sq = ar.alloc([128, 8, 512], BF16)
            rs = ar.alloc([128, 512], F32)
            sg = [ar.alloc([128, 512], F32) for _ in range(2)]
            assert ar.off == FFN_END
            if not FB:
                FB.update(h=nb("h"), sq=nb("sq"), rs=nb("rs"), wgu=[nb("wgu") for _ in range(NWS)],
                          wd=[nb("wd") for _ in range(2)], sg=[nb("sg") for _ in range(2)])
            B_h, B_sq, B_rs, B_wgu, B_wd, B_sg = FB["h"], FB["sq"], FB["rs"], FB["wgu"], FB["wd"], FB["sg"]
            gi = 0
            yi = 0
            wslot = 0
            dslot = 0
            for half in ([0, 1], [2, 3, 4]):
                h0 = BLKS[half[0]][0]
                if half[0] == 0 and not skip_norm01:
                    for bi in half:
                        norm_block(bi, gcol0, sq, rs, B_sq, B_rs)
                for jg in range(11):
                    if half[0] == 0 and jg in (3, 5, 7):
                        norm_block({3: 2, 5: 3, 7: 4}[jg], gcol0, sq, rs, B_sq, B_rs)
                    if half[0] == 2 and next_gcol is not None and jg in (4, 7):
                        norm_block({4: 0, 7: 1}[jg], next_gcol, sq, rs, B_sq, B_rs)
                    s = wslot % NWS
                    wslot += 1
                    dma("pool", wg[s][:], wgu[l, :, jg * 256:(jg + 1) * 256].rearrange("(c p) f -> p c f", p=128),
                        [], [B_wgu[s]], B_wgu[s])
                    dma("pool", wu[s][:], wgu[l, :, 2816 + jg * 256:2816 + (jg + 1) * 256].rearrange("(c p) f -> p c f", p=128),
                        [], [B_wgu[s]], B_wgu[s])
                    for fc in range(2):
                        j = 2 * jg + fc
                        for bi in half:
                            c0, n = BLKS[bi]
                            gb, ub = gi % 2, 2 + gi % 2
                            sgi = gi % 2
                            gi += 1
                            for c in range(8):
                                mm(psb[gb][:, 0:n], wg[s][:, c, fc * 128:(fc + 1) * 128], hn[:, c, c0:c0 + n],
                                   c == 0, c == 7, [B_wgu[s], B_hn[bi]], [PB[gb]])
                            for c in range(8):
                                mm(psb[ub][:, 0:n], wu[s][:, c, fc * 128:(fc + 1) * 128], hn[:, c, c0:c0 + n],
                                   c == 0, c == 7, [B_wgu[s], B_hn[bi]], [PB[ub]])
                            act(sg[sgi][:, 0:n], psb[gb][:, 0:n], AF.Silu, [PB[gb]], [B_sg[sgi]])
                            tt("dve", h[:, j, c0 - h0:c0 - h0 + n], psb[ub][:, 0:n], sg[sgi][:, 0:n], ALU.mult,
                               [PB[ub], B_sg[sgi]], [B_h])
                for o in range(8):
                    s = dslot % 2
                    dslot += 1
                    dma("pool", wd[s][:], wdn[l, :, o * 128:(o + 1) * 128].rearrange("(j p) o -> p j o", p=128),
                        [], [B_wd[s]], B_wd[s])
                    for bi in half:
                        c0, n = BLKS[bi]
                        yb = 4 + yi % 2
                        yi += 1
                        for j in range(22):
                            mm(psb[yb][:, 0:n], wd[s][:, j, :], h[:, j, c0 - h0:c0 - h0 + n], j == 0, j == 21,
                               [B_wd[s], B_h], [PB[yb]])
                        stt("dve", xT[:, o, c0:c0 + n], psb[yb][:, 0:n], 0.5, xT[:, o, c0:c0 + n], ALU.mult, ALU.add,
                            [PB[yb], B_x[bi]], [B_x[bi]])
            if end_barrier:
                P.barrier()

        def final_phase():
            ar.reset(FFN_END)
            st2 = [ar.alloc([128, 1024], F32) for _ in range(2)]
            B_st2 = [nb("st2") for _ in range(2)]
            for i in range(NTILE):
                r0, nr = tile_rows(i)
                s = i % 2
                blk = min(i // 4, 4)
                for hf in range(2):
                    bk = (2 * i + hf) % 4
                    for k in range(4):
                        c = 4 * hf + k
                        tr(psb[bk][0:nr, k * 128:(k + 1) * 128], xT[:, c, r0:r0 + nr], ident[:, :],
                           [B_x[blk], B_const], [PB[bk]])
                    cp(ev_eng(), st2[s][0:nr, hf * 512:(hf + 1) * 512], psb[bk][0:nr, :], [PB[bk]], [B_st2[s]])
                dst = o_yp[r0:r0 + nr, :] if i < 16 else o_ys[:, :]
                dma("sp", dst, st2[s][0:nr, :], [B_st2[s]], [], B_st2[s], is_out=True)


        def sw_pipeline(items, stage_fns, lag=1):
            n = len(items)
            K = len(stage_fns)
            for s in range(n + (K - 1) * lag):
                for k in range(K):
                    idx = s - k * lag
                    if 0 <= idx < n:
                        stage_fns[k](items[idx], idx)
                yield

        def drain(*gens):
            gens = [g for g in gens if g is not None]
            while gens:
                for g in list(gens):
                    try:
                        next(g)
                    except StopIteration:
                        gens.remove(g)

        def norm_all(gcol0, blocks=(0, 1, 2, 3, 4)):
            m = ar.mark()
            drain(norm_all_gen(gcol0, blocks))
            P.barrier()
            ar.reset(m)

        def norm_all_gen(gcol0, blocks=(0, 1, 2, 3, 4)):
            sq = ar.alloc([128, 8, 512], BF16)
            rs = ar.alloc([128, 512], F32)
            B_sq, B_rs = nb("sq"), nb("rs")
            for bi in blocks:
                norm_block(bi, gcol0, sq, rs, B_sq, B_rs)
                yield

        class Chain:
            def __init__(self):
                self.sqc = [ar.alloc([128, 512], BF16) for _ in range(3)]
                self.rsc = [ar.alloc([128, 512], F32) for _ in range(2)]
                self.B_sqc = [nb("sqc") for _ in range(3)]
                self.B_rsc = [nb("rsc") for _ in range(2)]

            def item(self, pmm, rows, n, bd_lhsT, gcol, dests, PJ, SSB, post=None):
                r0, r1 = rows

                def s0(idx):
                    b = PJ[idx % len(PJ)]
                    sl = idx % 3
                    pmm(b)
                    act(self.sqc[sl][r0:r1, 0:n], psb[b][r0:r1, 0:n], AF.Square, [PB[b]], [self.B_sqc[sl]])

                def s1(idx):
                    b = PJ[idx % len(PJ)]
                    sl = idx % 3
                    rl = idx % 2
                    ssb = SSB[idx % len(SSB)]
                    mm(psb[ssb][r0:r1, 0:n], bd_lhsT, self.sqc[sl][r0:r1, 0:n], True, True,
                       [self.B_sqc[sl], B_const], [PB[ssb]])
                    rstd_from(self.rsc[rl][r0:r1, 0:n], psb[ssb][r0:r1, 0:n], [PB[ssb]], self.B_rsc[rl])
                    for d in dests:
                        ap, bf = d[0], d[1]
                        a, b_ = d[2] if len(d) > 2 else (r0, r1)
                        g = d[3] if len(d) > 3 else gcol
                        stt("dve", ap, psb[b][a:b_, 0:n], g, self.rsc[rl][a:b_, 0:n], ALU.mult, ALU.mult,
                            [PB[b], self.B_rsc[rl], B_const], [bf])
                    if post is not None:
                        post()

                return (s0, s1)

        def simple_item(f0, f1, PJ):
            return (lambda idx: f0(PJ[idx % len(PJ)]), lambda idx: f1(PJ[idx % len(PJ)]))

        def run_items(items, lag=1):
            return sw_pipeline(items, [lambda it, i: it[0](i), lambda it, i: it[1](i)], lag)

        def out_proj(mix, nk, wo, B_mix, B_wo, banks=(0, 1, 2)):
            cnt = 0
            for bi in range(5):
                c0, n = BLKS[bi]
                for o in range(8):
                    bk = banks[cnt % len(banks)]
                    cnt += 1
                    for k in range(nk):
                        mm(psb[bk][:, 0:n], wo[:, k, o * 128:(o + 1) * 128], mix[:, k, c0:c0 + n], k == 0, k == nk - 1,
                           [B_wo, B_mix], [PB[bk]])
                    tt("dve", xT[:, o, c0:c0 + n], psb[bk][:, 0:n], xT[:, o, c0:c0 + n], ALU.add,
                       [PB[bk], B_x[bi]], [B_x[bi]])

        NPB = 6

        class AttnScratch:
            def __init__(self, need_pexp=True):
                self.pexp = [ar.alloc([128, 512], F32) for _ in range(3)] if need_pexp else None
                self.pb = [ar.alloc([128, 512], BF16) for _ in range(NPB)]
                self.B_pexp = [nb("pexp") for _ in range(3)]
                self.B_pb = [nb("pb") for _ in range(NPB)]
                self.rcp = ar.alloc([128, 512], F32)
                self.rcp2 = self.rcp
                self.B_rcp, self.B_rcp2 = nb("rcp"), nb("rcp2")

        def attention_gen(A, groups, scale, SB, OB, lag=3):
            items = []
            ob0 = 0
            for gi, g in enumerate(groups):
                g["ob0"] = ob0
                ob0 += g["nO"]
                for ti, t in enumerate(g["tiles"]):
                    items.append((gi, t, ti == len(g["tiles"]) - 1))
            started = set()

            def s0(it, idx):
                gi, t, last = it
                sl3, sl4 = idx % 3, idx % NPB
                sbk = SB[idx % len(SB)]
                nk, W = t["nk"], t["W"]
                ov = psb[sbk][0:nk, 0:W]
                pb_v = A.pb[sl4][0:nk, 0:W]
                view = t.get("view")
                if view is not None:
                    ov, pb_v = view(ov), view(pb_v)
                mm(ov, t["KT"], t["qrhs"], True, True, t["R"], [PB[sbk]])
                if t.get("Gap") is None:
                    act(pb_v, ov, AF.Exp, [PB[sbk]], [A.B_pb[sl4]], scale=scale)
                else:
                    pe_v = A.pexp[sl3][0:nk, 0:W]
                    if view is not None:
                        pe_v = view(pe_v)
                    act(pe_v, ov, AF.Exp, [PB[sbk]], [A.B_pexp[sl3]], scale=scale)
                    tt("pool" if idx % 3 == 2 else "dve", pb_v, pe_v, t["Gap"], ALU.mult,
                       [A.B_pexp[sl3], B_G], [A.B_pb[sl4]])

            def s1(it, idx):
                gi, t, last = it
                g = groups[gi]
                sl4 = idx % NPB
                obs = [OB[(g["ob0"] + j) % len(OB)] for j in range(g["nO"])]
                for (j, lo, hi, VEap, nkr, plo, phi, Rk) in t["pv"]:
                    first = (gi, j) not in started
                    started.add((gi, j))
                    mm(psb[obs[j]][:, lo:hi], VEap, A.pb[sl4][0:nkr, plo:phi], first, False,
                       [A.B_pb[sl4]] + list(Rk), [PB[obs[j]]])
                if last:
                    g["finish"](obs)

            return sw_pipeline(items, [s0, s1], lag)

        def attn_finish(A, ob, ncols, esink_ap, dest, B_dest, R_extra=(), use_dve=False):
            if use_dve:
                recip(A.rcp2[0:64, 0:ncols], psb[ob][64:128, 0:ncols], [PB[ob]], [A.B_rcp2])
            else:
                if esink_ap is not None:
                    act(A.rcp[64:128, 0:ncols], psb[ob][64:128, 0:ncols], AF.Ln, [PB[ob], B_G], [A.B_rcp], bias=esink_ap)
                else:
                    act(A.rcp[64:128, 0:ncols], psb[ob][64:128, 0:ncols], AF.Ln, [PB[ob]], [A.B_rcp])
                act(A.rcp2[0:64, 0:ncols], A.rcp[64:128, 0:ncols], AF.Exp, [A.B_rcp], [A.B_rcp2], scale=-1.0)
            tt("dve", dest, psb[ob][0:64, 0:ncols], A.rcp2[0:64, 0:ncols], ALU.mult,
               [PB[ob], A.B_rcp2] + list(R_extra), [B_dest])

        B_G = nb("G")

        class OutStage:
            def __init__(self, width=256):
                self.ostg = [ar.alloc([128, width], F32) for _ in range(2)]
                self.B = [nb("ostg") for _ in range(2)]
                self.cnt = 0

            def store_rows(self, src_bank, nr, ncols, dst):
                s = self.cnt % 2
                self.cnt += 1
                cp(ev_eng(), self.ostg[s][0:nr, 0:ncols], psb[src_bank][0:nr, 0:ncols], [PB[src_bank]], [self.B[s]])
                dma("sp", dst, self.ostg[s][0:nr, 0:ncols], [self.B[s]], [], self.B[s], is_out=True)

        def mix_even():
            ar.reset()
            G_A = ar.alloc([128, 8, 256], BF16)
            G_B = ar.alloc([128, 8, 640], BF16)
            esink = ar.alloc([128, 8], F32)
            mG = ar.mark()
            gnorm = norm_all_gen(8, (2, 3, 4))
            t5s = ar.alloc([128, 8], F32)
            tcb = ar.alloc([128, 2, 8], F32)
            oha = ar.alloc([128, 384], F32)
            ohb = ar.alloc([128, 2, 768], F32)
            visa = ar.alloc([128, 256], F32)
            visb = ar.alloc([128, 640], F32)
            sraw = ar.alloc([128, 8], F32)
            lbA = [ar.alloc([128, 128], F32) for _ in range(8)]
            lbB = [ar.alloc([128, 2, 128], F32) for _ in range(8)]
            rrA = [ar.alloc([128, 384], F32) for _ in range(8)]
            rrB = [ar.alloc([128, 768], F32) for _ in range(8)]
            gpA = [ar.alloc([128, 256], F32) for _ in range(4)]
            gpB = [ar.alloc([128, 640], F32) for _ in range(4)]
            B_gc = nb("gconst")
            B_lbA = [nb("lbA") for _ in range(8)]
            B_lbB = [nb("lbB") for _ in range(8)]
            B_rrA = [nb("rrA") for _ in range(8)]
            B_rrB = [nb("rrB") for _ in range(8)]
            B_gpA = [nb("gpA") for _ in range(4)]
            B_gpB = [nb("gpB") for _ in range(4)]
            B_gscr = [nb("gscr") for _ in range(16)]
            dma("sp", t5s[0:32, :], t5_d, [], [B_gc], B_gc)
            for kc in range(2):
                dma_nc("sp", tcb[:, kc, :], brel_d[:, kc * 128:(kc + 1) * 128].rearrange("h p -> p h"), [], [B_gc], B_gc)
            dma("sp", oha[0:32, :], oha_d, [], [B_gc], B_gc)
            dma("sp", ohb[:], ohb_d, [], [B_gc], B_gc)
            dma("sp", visa[:], visa_d, [], [B_gc], B_gc)
            dma("sp", visb[:], visb_d, [], [B_gc], B_gc)
            dma("sp", sraw[:], sinks_d.partition_broadcast(128), [], [B_gc], B_gc)
            act(esink[:], sraw[:], AF.Exp, [B_gc], [B_G])
            next(gnorm)
            for h in range(8):
                cp("dve", lbA[h][0:32, :], t5s[0:32, h:h + 1].to_broadcast([32, 128]), [B_gc], [B_lbA[h]])
                for kc in range(2):
                    cp("dve", lbB[h][:, kc, :], tcb[:, kc, h:h + 1].to_broadcast([128, 128]), [B_gc], [B_lbB[h]])
            next(gnorm)
            for h in range(8):
                ba = (0, 3)[h % 2]
                bb = (1, 4)[h % 2]
                bc = (2, 5)[h % 2]
                mm(psb[ba][:, 0:384], lbA[h][0:32, :], oha[0:32, 0:384], True, True, [B_lbA[h], B_gc], [PB[ba]])
                act(rrA[h][:, :], psb[ba][:, 0:384], AF.Exp, [PB[ba]], [B_rrA[h]])
                for kc in range(2):
                    mm(psb[bb][:, 0:256], lbB[h][:, kc, :], ohb[:, kc, 0:256], kc == 0, kc == 1, [B_lbB[h], B_gc], [PB[bb]])
                act(rrB[h][:, 0:256], psb[bb][:, 0:256], AF.Exp, [PB[bb]], [B_rrB[h]])
                act(rrB[h][:, 256:768], psb[bb][:, 255:256].to_broadcast([128, 512]), AF.Exp, [PB[bb]], [B_rrB[h]])
                scrA = bass.AP(gscr.tensor, h * 128 * 768, [[384, 128], [1, 384]])
                dma("sp", scrA, rrA[h][:, :], [B_rrA[h]], [B_gscr[h]], B_gscr[h])
                scrB = bass.AP(gscr.tensor, (8 + h) * 128 * 768, [[768, 128], [1, 768]])
                dma("sp", scrB, rrB[h][:, :], [B_rrB[h]], [B_gscr[8 + h]], B_gscr[8 + h])
            next(gnorm)
            for h in range(8):
                s = h % 4
                skA = bass.AP(gscr.tensor, h * 128 * 768 + 127, [[383, 128], [1, 256]])
                dma("sp", gpA[s][:, :], skA, [B_gscr[h]], [B_gpA[s]], B_gpA[s])
                skB = bass.AP(gscr.tensor, (8 + h) * 128 * 768 + 127, [[767, 128], [1, 640]])
                dma("sp", gpB[s][:, :], skB, [B_gscr[8 + h]], [B_gpB[s]], B_gpB[s])
                kvh, g = h // 4, h % 4
                slot = kvh * 4 + (g % 2) * 2 + g // 2
                tt("dve", G_A[:, slot, :], gpA[s][:, :], visa[:], ALU.mult, [B_gpA[s], B_gc], [B_G])
                tt("dve", G_B[:, h, :], gpB[s][:, :], visb[:], ALU.mult, [B_gpB[s], B_gc], [B_G])
            P.barrier()
            ar.reset(mG)
            CH = Chain()
            AS = AttnScratch()
            k32 = [ar.alloc([128, 512], F32) for _ in range(1)]
            B_k32 = [nb("k32") for _ in range(1)]
            OSt = OutStage(256)
            mS = ar.mark()
            PJ, SSB = (0, 1, 2), (3, 4)

            def k_out(k32cols, B_src, nr, dst, bank):
                tr(psb[bank][0:nr, 0:128], k32cols, ident[:, :], [B_src, B_const], [PB[bank]])
                OSt.store_rows(bank, nr, 128, dst)

            wsrc = lambda a, b: evin_d[:, a:b].rearrange("(c p) f -> p c f", p=128)

            wbuf = ar.alloc([128, 8, 896], BF16)
            B_w = nb("wA")
            aqT = ar.alloc([128, 4, T], BF16)
            akT = ar.alloc([128, 2, T], BF16)
            aVE = ar.alloc([128, NTILE, 2, 128], BF16)
            cstg = ar.alloc([128, 256], F32)
            akcT = ar.alloc([128, 2, 128], BF16)
            aVEc = ar.alloc([128, 2, 128], BF16)
            mixA = ar.alloc([128, 4, T], BF16)
            B_aq, B_ak, B_aVE, B_cstg, B_akc, B_aVEc, B_mixA = (nb("aq"), nb("ak"), nb("aVE"), nb("cstg"),
                                                               nb("akc"), nb("aVEc"), nb("mixA"))
            dma("pool", wbuf[:, :, 0:512], wsrc(0, 512), [], [B_w], B_w)
            dma("pool", wbuf[:, :, 512:640], wsrc(512, 640), [], [B_w], B_w)
            dma("pool", wbuf[:, :, 640:704], wsrc(576, 640), [], [B_w], B_w)
            dma("pool", wbuf[:, :, 704:768], wsrc(512, 576), [], [B_w], B_w)
            dma("pool", wbuf[:, :, 768:896], wsrc(640, 768), [], [B_w], B_w)
            dma("pool", aVEc[:, :, 0:64], cav_d.rearrange("k (h d) -> k h d", h=2), [], [B_aVEc], B_aVEc)
            memset("pool", aVE[:, :, :, 64:128], 1.0, [B_aVE])
            memset("pool", aVEc[:, :, 64:128], 1.0, [B_aVEc])
            dma("sp", cstg[:, 0:128], cak_d, [], [B_cstg], B_cstg)
            dma("sp", cstg[:, 128:192], cak_d[:, 64:128], [], [B_cstg], B_cstg)
            dma("sp", cstg[:, 192:256], cak_d[:, 0:64], [], [B_cstg], B_cstg)
            for sel in range(2):
                tr(psb[6 + sel][:, 0:128], cstg[:, sel * 128:(sel + 1) * 128], ident[:, :], [B_cstg, B_const], [PB[6 + sel]])
                cp("act", akcT[:, sel, :], psb[6 + sel][:, 0:128], [PB[6 + sel]], [B_akc])
            dma("sp", o_aks[0:64, :], cak_d[64:128, :], [], [], nb("oc"), is_out=True)
            dma("sp", o_avs[0:64, :], cav_d[64:128, :], [], [], nb("oc"), is_out=True)
            dma("sp", o_bks[0:448, :], cbk_d[64:512, :], [], [], nb("oc"), is_out=True)
            dma("sp", o_bvs[0:448, :], cbv_d[64:512, :], [], [], nb("oc"), is_out=True)
            items = []
            for bi in range(5):
                c0, n = BLKS[bi]
                for oc in range(6):
                    def pmm(b, oc=oc, bi=bi, c0=c0, n=n):
                        for c in range(8):
                            mm(psb[b][:, 0:n], wbuf[:, c, oc * 128:(oc + 1) * 128], hn[:, c, c0:c0 + n], c == 0, c == 7,
                               [B_w, B_hn[bi]], [PB[b]])
                    post = None
                    if oc < 4:
                        dests = [(aqT[:, oc, c0:c0 + n], B_aq)]
                        g = vec[:, 48:49]
                    else:
                        dests = [(akT[:, oc - 4, c0:c0 + n], B_ak)]
                        g = vec[:, 49:50]
                        if oc == 4 and bi >= 3:
                            dests.append((k32[0][:, 0:n], B_k32[0]))
                            if bi == 3:
                                post = lambda: k_out(k32[0][:, 384:512], B_k32[0], 128, o_akp[:, :], 6)
                            else:
                                post = lambda: k_out(k32[0][:, 0:64], B_k32[0], 64, o_aks[64:128, :], 6)
                    items.append(CH.item(pmm, (0, 128), n, cmat[:, CM_BD64, :], g, dests, PJ, SSB, post))
                for i in range(NTILE):
                    r0, nr = tile_rows(i)
                    if not (c0 <= r0 < c0 + n):
                        continue

                    def f0(b, i=i, r0=r0, nr=nr, bi=bi):
                        for c in range(8):
                            mm(psb[b][0:nr, 0:128], hn[:, c, r0:r0 + nr], wbuf[:, c, 768:896], c == 0, c == 7,
                               [B_w, B_hn[bi]], [PB[b]])

                    def f1(b, i=i, nr=nr):
                        cp("dve", aVE[0:nr, i, :, 0:64], psb[b][0:nr, 0:128].rearrange("p (k d) -> p k d", k=2),
                           [PB[b]], [B_aVE])
                        if i == 15:
                            OSt.store_rows(b, 128, 128, o_avp[:, :])
                        if i == 16:
                            OSt.store_rows(b, 64, 128, o_avs[64:128, :])
                    items.append(simple_item(f0, f1, PJ))
            drain(run_items(items))
            woA = wbuf[:, :, :].rearrange("p c f -> p (c f)")[:, 0:4096].rearrange("p (k o) -> p k o", k=4)
            dma("pool", woA, evout_d[0:512, :].rearrange("(k p) o -> p k o", p=128), [], [B_w], B_w)
            ACR = [(0, 128), (0, 256), (128, 384), (256, 512), (384, 512)]
            groups = []
            for kvh in range(2):
                for par in range(2):
                    sel = 0 if kvh == par else 1
                    pr = slice(par * 64, par * 64 + 64)
                    slot0 = kvh * 4 + par * 2
                    qgroups = [(i * 512, 512, [(4 * i - 1 + r, ACR[r], None) for r in range(5) if 4 * i - 1 + r >= 0])
                               for i in range(4)]
                    qgroups.append((2048, 64, [("cache", (0, 64), 128), ("new", (0, 64), 0)]))
                    for q0, qn, kbl in qgroups:
                        tiles = []
                        for kb, (ca, cb_), yoff in kbl:
                            ncols = cb_ - ca
                            if kb == "cache":
                                KT, nk, VEs, y0 = akcT[pr, sel, :], 128, aVEc[:, kvh, :], yoff + ca
                                Rk = [B_akc, B_aVEc]
                            elif kb == "new":
                                KT, nk, VEs, y0 = akT[pr, sel, 2048:2112], 64, aVE[0:64, 16, kvh, :], ca
                                Rk = [B_ak, B_aVE]
                            else:
                                KT, nk, VEs = akT[pr, sel, kb * 128:(kb + 1) * 128], 128, aVE[:, kb, kvh, :]
                                r = kb - (4 * (q0 // 512) - 1)
                                y0 = ca - 128 * (r - 1)
                                Rk = [B_ak, B_aVE]
                            tiles.append(dict(
                                KT=KT, qrhs=aqT[pr, kvh * 2:kvh * 2 + 2, q0 + ca:q0 + cb_], nk=nk, W=2 * ncols,
                                Gap=G_A[0:nk, slot0:slot0 + 2, y0:y0 + ncols],
                                view=lambda a: a.rearrange("p (c n) -> p c n", c=2),
                                R=[B_aq] + Rk,
                                pv=[(cc, ca, cb_, VEs, nk, cc * ncols, (cc + 1) * ncols, Rk) for cc in range(2)]))

                        def fin(obs, kvh=kvh, par=par, q0=q0, qn=qn):
                            for cc in range(2):
                                g = 2 * cc + par
                                h = kvh * 4 + g
                                chunk = kvh * 2 + g // 2
                                drow = slice((g % 2) * 64, (g % 2) * 64 + 64)
                                attn_finish(AS, obs[cc], qn, esink[64:128, h:h + 1], mixA[drow, chunk, q0:q0 + qn], B_mixA)
                        groups.append(dict(tiles=tiles, nO=2, finish=fin))
            drain(attention_gen(AS, groups, 0.125, (3, 4, 5), (0, 1, 2, 6, 7, 0, 1, 2, 6, 7)[0:4]))
            out_proj(mixA, 4, woA, B_mixA, B_w)
            P.barrier()

            BCR = [(0, 128), (0, 256), (0, 384), (0, 512), (0, 512), (128, 512), (256, 512), (384, 512)]
            for hb in range(2):
                ar.reset(mS)
                wbuf = ar.alloc([128, 8, 768], BF16)
                B_w = nb("wB")
                bqT = ar.alloc([128, 2, T], BF16)
                bkT = ar.alloc([128, 2, T], BF16)
                bVE = ar.alloc([128, NTILE, 4, 128], BF16)
                cstg = ar.alloc([128, 4, 256], F32)
                bkcT = ar.alloc([128, 2, 512], BF16)
                bVEc = ar.alloc([128, 4, 4, 128], BF16)
                mixB = ar.alloc([128, 2, T], BF16)
                kb32 = ar.alloc([128, 512], F32)
                B_kb32 = nb("kb32")
                B_bq, B_bk, B_bVE, B_cstg, B_bkc, B_bVEc, B_mixB = (nb("bq"), nb("bk"), nb("bVE"), nb("cstgb"),
                                                                   nb("bkc"), nb("bVEc"), nb("mixB"))
                for k3, base in enumerate((768, 1280, 1792)):
                    dma("pool", wbuf[:, :, k3 * 256:(k3 + 1) * 256], wsrc(base + hb * 256, base + hb * 256 + 256),
                        [], [B_w], B_w)
                for t4 in range(4):
                    dma("pool", bVEc[:, t4, :, 0:64],
                        cbv_d[t4 * 128:(t4 + 1) * 128, hb * 256:(hb + 1) * 256].rearrange("p (h d) -> p h d", h=4),
                        [], [B_bVEc], B_bVEc)
                memset("pool", bVE[:, :, :, 64:128], 1.0, [B_bVE])
                memset("pool", bVEc[:, :, :, 64:128], 1.0, [B_bVEc])
                dma("sp", cstg[:], cbk_d[:, hb * 256:(hb + 1) * 256].rearrange("(t p) f -> p t f", p=128),
                    [], [B_cstg], B_cstg)
                for t4 in range(4):
                    for cl in range(2):
                        bk = 6 + (t4 * 2 + cl) % 2
                        tr(psb[bk][:, 0:128], cstg[:, t4, cl * 128:(cl + 1) * 128], ident[:, :], [B_cstg, B_const], [PB[bk]])
                        cp(ev_eng(), bkcT[:, cl, t4 * 128:(t4 + 1) * 128], psb[bk][:, 0:128], [PB[bk]], [B_bkc])
                items = []
                for bi in range(5):
                    c0, n = BLKS[bi]
                    for oc in range(4):
                        def pmm(b, oc=oc, bi=bi, c0=c0, n=n):
                            for c in range(8):
                                mm(psb[b][:, 0:n], wbuf[:, c, oc * 128:(oc + 1) * 128], hn[:, c, c0:c0 + n], c == 0, c == 7,
                                   [B_w, B_hn[bi]], [PB[b]])
                        post = None
                        if oc < 2:
                            dests = [(bqT[:, oc, c0:c0 + n], B_bq)]
                            g = vec[:, 50:51]
                        else:
                            dests = [(bkT[:, oc - 2, c0:c0 + n], B_bk)]
                            g = vec[:, 51:52]
                            if bi >= 3:
                                kk = k32[0] if oc == 2 else kb32
                                Bkk = B_k32[0] if oc == 2 else B_kb32
                                dests.append((kk[:, 0:n], Bkk))
                                fcol = hb * 256 + (oc - 2) * 128
                                if bi == 3:
                                    def post(kk=kk, Bkk=Bkk, fcol=fcol):
                                        for t4 in range(4):
                                            k_out(kk[:, t4 * 128:(t4 + 1) * 128], Bkk, 128,
                                                  o_bkp[t4 * 128:(t4 + 1) * 128, fcol:fcol + 128], 6 + t4 % 2)
                                else:
                                    def post(kk=kk, Bkk=Bkk, fcol=fcol):
                                        k_out(kk[:, 0:64], Bkk, 64, o_bks[448:512, fcol:fcol + 128], 6)
                        items.append(CH.item(pmm, (0, 128), n, cmat[:, CM_BD64, :], g, dests, PJ, SSB, post))
                    for i in range(NTILE):
                        r0, nr = tile_rows(i)
                        if not (c0 <= r0 < c0 + n):
                            continue

                        def f0(b, i=i, r0=r0, nr=nr, bi=bi):
                            for c in range(8):
                                mm(psb[b][0:nr, 0:256], hn[:, c, r0:r0 + nr], wbuf[:, c, 512:768], c == 0, c == 7,
                                   [B_w, B_hn[bi]], [PB[b]])

                        def f1(b, i=i, nr=nr, hb=hb):
                            cp(ev_eng(), bVE[0:nr, i, :, 0:64], psb[b][0:nr, 0:256].rearrange("p (k d) -> p k d", k=4),
                               [PB[b]], [B_bVE])
                            if 12 <= i < 16:
                                OSt.store_rows(b, 128, 256, o_bvp[(i - 12) * 128:(i - 11) * 128, hb * 256:(hb + 1) * 256])
                            if i == 16:
                                OSt.store_rows(b, 64, 256, o_bvs[448:512, hb * 256:(hb + 1) * 256])
                        items.append(simple_item(f0, f1, PJ))
                drain(run_items(items))
                woB = wbuf[:, :, :].rearrange("p c f -> p (c f)")[:, 0:2048].rearrange("p (k o) -> p k o", k=2)
                dma("pool", woB, evout_d[512 + hb * 256:512 + (hb + 1) * 256, :].rearrange("(k p) o -> p k o", p=128),
                    [], [B_w], B_w)
                groups = []
                for hl in range(4):
                    h = hb * 4 + hl
                    cl = hl // 2
                    pr = slice((hl % 2) * 64, (hl % 2) * 64 + 64)
                    qgroups = [(i * 512, 512, [(4 * i - 4 + r, BCR[r], 512 - 128 * r) for r in range(8) if 4 * i - 4 + r >= 0])
                               for i in range(4)]
                    qgroups.append((2048, 64, [("c%d" % m, (0, 64), 512 - 128 * m) for m in range(4)] + [("new", (0, 64), 0)]))
                    for q0, qn, kbl in qgroups:
                        tiles = []
                        for kb, (ca, cb_), yoff in kbl:
                            ncols = cb_ - ca
                            y0 = yoff + ca
                            if isinstance(kb, str) and kb[0] == "c":
                                m = int(kb[1:])
                                KT, nk, VEs = bkcT[pr, cl, m * 128:(m + 1) * 128], 128, bVEc[:, m, hl, :]
                                Rk = [B_bkc, B_bVEc]
                            elif kb == "new":
                                KT, nk, VEs = bkT[pr, cl, 2048:2112], 64, bVE[0:64, 16, hl, :]
                                Rk = [B_bk, B_bVE]
                            else:
                                KT, nk, VEs = bkT[pr, cl, kb * 128:(kb + 1) * 128], 128, bVE[:, kb, hl, :]
                                Rk = [B_bk, B_bVE]
                            tiles.append(dict(KT=KT, qrhs=bqT[pr, cl, q0 + ca:q0 + cb_], nk=nk, W=ncols,
                                              Gap=G_B[0:nk, h, y0:y0 + ncols], R=[B_bq] + Rk,
                                              pv=[(0, ca, cb_, VEs, nk, 0, ncols, Rk)]))

                        def fin(obs, pr=pr, cl=cl, q0=q0, qn=qn):
                            attn_finish(AS, obs[0], qn, None, mixB[pr, cl, q0:q0 + qn], B_mixB)
                        groups.append(dict(tiles=tiles, nO=1, finish=fin))
                drain(attention_gen(AS, groups, 0.125, (3, 4, 5), (0, 1, 2, 6, 7)))
                out_proj(mixB, 2, woB, B_mixB, B_w)
                P.barrier()

        def mix_odd():
            ar.reset()
            CH = Chain()
            AS = AttnScratch(need_pexp=False)
            OSt = OutStage(128)
            mS = ar.mark()
            osrc = lambda a, b: odin_d[:, a:b].rearrange("(c p) f -> p c f", p=128)
            mixC = ar.alloc([128, 4, T], BF16)
            woC = ar.alloc([128, 4, 1024], BF16)
            wC = [ar.alloc([128, 8, 384], BF16) for _ in range(2)]
            ub = [ar.alloc([128, 516], F32) for _ in range(3)]
            ccs = [ar.alloc([128, 512], F32) for _ in range(2)]
            t1 = [ar.alloc([128, 512], F32) for _ in range(2)]
            cvs = ar.alloc([128, 4, 2], F32)
            B_mixC, B_woC, B_cvs = nb("mixC"), nb("woC"), nb("cvs")
            B_wC = [nb("wC") for _ in range(2)]
            B_ub = [nb("ub") for _ in range(3)]
            B_ccs = [nb("ccs") for _ in range(2)]
            B_t1 = [nb("t1") for _ in range(2)]

            def load_wC(c):
                s = c % 2
                for k3 in range(3):
                    dma("pool", wC[s][:, :, k3 * 128:(k3 + 1) * 128], osrc(k3 * 512 + c * 128, k3 * 512 + (c + 1) * 128),
                        [], [B_wC[s]], B_wC[s])
            load_wC(0)
            load_wC(1)
            dma("pool", woC[:], odout_d[0:512, :].rearrange("(k p) o -> p k o", p=128), [], [B_woC], B_woC)
            for c in range(4):
                dma_nc("sp", cvs[:, c, :], ccv_d[:, c * 128:(c + 1) * 128].rearrange("j p -> p j"), [], [B_cvs], B_cvs)
            drain(norm_all_gen(32, (2, 3, 4)))
            items = []
            cnt = 0
            for c in range(4):
                for bi in range(5):
                    c0, n = BLKS[bi]
                    us = cnt % 3
                    ups = (cnt - 1) % 3
                    es_ = cnt % 2
                    cnt += 1

                    def f0(bset, c=c, bi=bi, c0=c0, n=n):
                        s = c % 2
                        for k3 in range(3):
                            for kc in range(8):
                                mm(psb[bset[k3]][:, 0:n], wC[s][:, kc, k3 * 128:(k3 + 1) * 128], hn[:, kc, c0:c0 + n],
                                   kc == 0, kc == 7, [B_wC[s], B_hn[bi]], [PB[bset[k3]]])
                        if bi == 4 and c + 2 < 4:
                            load_wC(c + 2)

                    def f1(bset, c=c, bi=bi, c0=c0, n=n, us=us, ups=ups, es_=es_):
                        u = ub[us]
                        if bi == 0:
                            memset("dve", u[:, 0:2], 0.0, [B_ub[us]])
                        elif bi == 4:
                            cp("dve", u[:, 0:2], cvs[:, c, :], [B_cvs], [B_ub[us]])
                        else:
                            cp("dve", u[:, 0:2], ub[ups][:, 512:514], [B_ub[ups]], [B_ub[us]])
                        cp("act", ccs[es_][:, 0:n], psb[bset[1]][:, 0:n], [PB[bset[1]]], [B_ccs[es_]])
                        tt("dve", u[:, 2:2 + n], psb[bset[2]][:, 0:n], ccs[es_][:, 0:n], ALU.mult,
                           [PB[bset[2]], B_ccs[es_]], [B_ub[us]])
                        ts("dve", t1[es_][:, 0:n], u[:, 2:2 + n], vec[:, 58 + 8 + c:58 + 8 + c + 1], None, ALU.mult, None,
                           [B_ub[us], B_const], [B_t1[es_]])
                        stt("dve", t1[es_][:, 0:n], u[:, 1:1 + n], vec[:, 58 + 4 + c:58 + 4 + c + 1], t1[es_][:, 0:n],
                            ALU.mult, ALU.add, [B_ub[us], B_const, B_t1[es_]], [B_t1[es_]])
                        stt("dve", t1[es_][:, 0:n], u[:, 0:n], vec[:, 58 + c:58 + c + 1], t1[es_][:, 0:n],
                            ALU.mult, ALU.add, [B_ub[us], B_const, B_t1[es_]], [B_t1[es_]])
                        tt("dve", mixC[:, c, c0:c0 + n], psb[bset[0]][:, 0:n], t1[es_][:, 0:n], ALU.mult,
                           [PB[bset[0]], B_t1[es_]], [B_mixC])
                        if bi == 3:
                            dma_nc("sp", o_cp[:, c * 128:(c + 1) * 128].rearrange("j p -> p j"), u[:, 512:514],
                                   [B_ub[us]], [], nb("ocp"), is_out=True)
                        if bi == 4:
                            dma_nc("sp", o_cs[:, c * 128:(c + 1) * 128].rearrange("j p -> p j"), u[:, 64:66],
                                   [B_ub[us]], [], nb("ocs"), is_out=True)
                    items.append((lambda idx, f0=f0: f0(((0, 1, 2), (3, 4, 5))[idx % 2]),
                                  lambda idx, f1=f1: f1(((0, 1, 2), (3, 4, 5))[idx % 2])))
            drain(run_items(items))
            out_proj(mixC, 4, woC, B_mixC, B_woC, banks=(6, 7))
            P.barrier()
            ar.reset(mS)
            wkvb = ar.alloc([128, 1024], BF16)
            QsT = ar.alloc([128, 8, 64], BF16)
            KnT = ar.alloc([128, 8, 64], BF16)
            VE16 = ar.alloc([128, 8, 128], BF16)
            mKeep2 = ar.mark()
            ckvT = ar.alloc([128, T], BF16)
            kpeT = ar.alloc([128, T], BF16)
            VE = ar.alloc([128, 16, 8, 128], BF16)
            mKeep = ar.mark()
            qan = ar.alloc([128, 2, T], BF16)
            tabq = ar.alloc([128, T], F32)
            wD = ar.alloc([128, 8, 448], BF16)
            wqb = ar.alloc([128, 2, 8, 128], BF16)
            zk = [ar.alloc([128, 512], F32) for _ in range(2)]
            zk2 = ar.alloc([128, 512], F32)
            c32 = ar.alloc([128, 512], F32)
            zb = [ar.alloc([128, 512], BF16) for _ in range(2)]
            sq2 = ar.alloc([128, 512], BF16)
            (B_ckvT, B_kpeT, B_VE, B_wkvb, B_QsT, B_KnT, B_qan, B_tabq, B_wD, B_wqb, B_zk2, B_c32, B_sq2, B_VE16) = \
                [nb(x) for x in ("ckvT", "kpeT", "VE", "wkvb", "QsT", "KnT", "qan", "tabq", "wD", "wqb", "zk2",
                                 "c32", "sq2", "VE16")]
            B_zk = [nb("zk") for _ in range(2)]
            B_zb = [nb("zb") for _ in range(2)]
            dma("pool", wD[:, :, 0:416], osrc(1536, 1952), [], [B_wD], B_wD)
            dma("pool", wD[:, :, 416:432], osrc(1936, 1952), [], [B_wD], B_wD)
            dma("pool", wD[:, :, 432:448], osrc(1920, 1936), [], [B_wD], B_wD)
            dma("pool", wkvb[:], wkvb_d, [], [B_wkvb], B_wkvb)
            for kc in range(2):
                dma("pool", wqb[:, kc, :, :], wqb_d[kc * 128:(kc + 1) * 128, :, :], [], [B_wqb], B_wqb)
            dma("sp", tabq[:], tabq_d, [], [B_tabq], B_tabq)
            memset("pool", VE[:, :, :, 64:128], 1.0, [B_VE])
            memset("pool", VE16[:, :, 64:128], 1.0, [B_VE16])
            wkv_v = wkvb[:, 512:1024]

            def VEt(i):
                return VE[:, i] if i < 16 else VE16

            for bi in range(5):
                c0, n = BLKS[bi]
                for kc2 in range(2):
                    for kc in range(8):
                        mm(psb[kc2][:, 0:n], wD[:, kc, kc2 * 128:(kc2 + 1) * 128], hn[:, kc, c0:c0 + n], kc == 0, kc == 7,
                           [B_wD, B_hn[bi]], [PB[kc2]])
                for kc in range(8):
                    mm(psb[3][:, 0:n], wD[:, kc, 256:384], hn[:, kc, c0:c0 + n], kc == 0, kc == 7, [B_wD, B_hn[bi]], [PB[3]])
                for kc in range(8):
                    mm(psb[4][64:128, 0:n], wD[:, kc, 384:448], hn[:, kc, c0:c0 + n], kc == 0, kc == 7, [B_wD, B_hn[bi]], [PB[4]])
                act(CH.sqc[0][:, 0:n], psb[0][:, 0:n], AF.Square, [PB[0]], [CH.B_sqc[0]])
                act(sq2[:, 0:n], psb[1][:, 0:n], AF.Square, [PB[1]], [B_sq2])
                act(CH.sqc[1][:, 0:n], psb[3][:, 0:n], AF.Square, [PB[3]], [CH.B_sqc[1]])
                act(CH.sqc[2][64:96, 0:n], psb[4][64:96, 0:n], AF.Square, [PB[4]], [CH.B_sqc[2]])
                mm(psb[2][:, 0:n], cmat[:, CM_ONES256, :], CH.sqc[0][:, 0:n], True, False, [CH.B_sqc[0], B_const], [PB[2]])
                mm(psb[2][:, 0:n], cmat[:, CM_ONES256, :], sq2[:, 0:n], False, True, [B_sq2, B_const], [PB[2]])
                mm(psb[6][:, 0:n], cmat[:, CM_ONES128, :], CH.sqc[1][:, 0:n], True, True, [CH.B_sqc[1], B_const], [PB[6]])
                mm(psb[5][64:96, 0:n], cmat[64:96, CM_BD3, 64:96], CH.sqc[2][64:96, 0:n], True, True, [CH.B_sqc[2], B_const], [PB[5]])
                rstd_from(CH.rsc[0][:, 0:n], psb[2][:, 0:n], [PB[2]], CH.B_rsc[0])
                for kc2 in range(2):
                    stt("dve", qan[:, kc2, c0:c0 + n], psb[kc2][:, 0:n], vec[:, 52 + kc2:53 + kc2], CH.rsc[0][:, 0:n],
                        ALU.mult, ALU.mult, [PB[kc2], CH.B_rsc[0], B_const], [B_qan])
                rstd_from(CH.rsc[1][:, 0:n], psb[6][:, 0:n], [PB[6]], CH.B_rsc[1])
                stt("dve", c32[:, 0:n], psb[3][:, 0:n], vec[:, 54:55], CH.rsc[1][:, 0:n], ALU.mult, ALU.mult,
                    [PB[3], CH.B_rsc[1], B_const], [B_c32])
                cp("pool", ckvT[:, c0:c0 + n], c32[:, 0:n], [B_c32], [B_ckvT])
                rstd_from(zk2[64:96, 0:n], psb[5][64:96, 0:n], [PB[5]], B_zk2)
                stt("dve", zk[0][64:128, 0:n], psb[4][64:128, 0:n], vec[64:128, 56:57], tabq[64:128, c0:c0 + n],
                    ALU.mult, ALU.mult, [PB[4], B_const, B_tabq], [B_zk[0]])
                cp("act", zk[1][64:96, 0:n], zk[0][96:128, 0:n], [B_zk[0]], [B_zk[1]])
                tt("dve", zk[1][64:96, 0:n], zk[0][64:96, 0:n], zk[1][64:96, 0:n], ALU.add, [B_zk[0], B_zk[1]], [B_zk[1]])
                tt("dve", zk[0][64:96, 0:n], zk[1][64:96, 0:n], zk2[64:96, 0:n], ALU.mult, [B_zk[1], B_zk2], [B_zk[0]])
                cp("pool", kpeT[64:96, c0:c0 + n], zk[0][64:96, 0:n], [B_zk[0]], [B_kpeT])
                for i in range(NTILE):
                    r0, nr = tile_rows(i)
                    if not (c0 <= r0 < c0 + n):
                        continue
                    bk = 6 + i % 2
                    tr(psb[bk][0:nr, 0:128], c32[:, r0 - c0:r0 - c0 + nr], ident[:, :], [B_c32, B_const], [PB[bk]])
                    tr(psb[bk][0:nr, 128:160], zk[0][64:96, r0 - c0:r0 - c0 + nr], ident[64:96, 64:96], [B_zk[0], B_const], [PB[bk]])
                    s = OSt.cnt % 2
                    OSt.cnt += 1
                    cp(ev_eng(), OSt.ostg[s][0:nr, 0:128], psb[bk][0:nr, 0:128], [PB[bk]], [OSt.B[s]])
                    dma("sp", o_ckvp[r0:r0 + nr, :] if i < 16 else o_ckvs[:, :], OSt.ostg[s][0:nr, 0:128], [OSt.B[s]], [],
                        OSt.B[s], is_out=True)
                    s = OSt.cnt % 2
                    OSt.cnt += 1
                    cp(ev_eng(), OSt.ostg[s][0:nr, 0:32], psb[bk][0:nr, 128:160], [PB[bk]], [OSt.B[s]])
                    dma("sp", o_kpep[r0:r0 + nr, :] if i < 16 else o_kpes[:, :], OSt.ostg[s][0:nr, 0:32], [OSt.B[s]], [],
                        OSt.B[s], is_out=True)
                for i in range(NTILE):
                    r0, nr = tile_rows(i)
                    if not (c0 <= r0 < c0 + n):
                        continue
                    bk = i % 2
                    mm(psb[bk][0:nr, 0:512], ckvT[:, r0:r0 + nr], wkv_v, True, True, [B_ckvT, B_wkvb], [PB[bk]])
                    cp("dve", VEt(i)[0:nr, :, 0:64], psb[bk][0:nr, 0:512].rearrange("p (h d) -> p h d", h=8),
                       [PB[bk]], [B_VE if i < 16 else B_VE16])
            P.barrier()
            mixD = hn
            B_mixD = nb("mixD")
            B_Qh = [nb("Qh") for _ in range(2)]
            B_Kh = [nb("Kh") for _ in range(2)]
            sc_d = float(96.0 ** -0.5)
            PJ, SSB, AMB = (0, 1), (2,), 3

            def head_chain_gen(h):
                s = h % 2
                QhT = hn[:, 4 + s, :]
                KhT = hn[:, 6 + s, :]
                items = []
                for bi in range(5):
                    c0, n = BLKS[bi]

                    def q0f(idx, c0=c0, n=n):
                        b = PJ[idx % 2]
                        for kc in range(2):
                            mm(psb[b][:, 0:n], wqb[:, kc, h, :], qan[:, kc, c0:c0 + n], kc == 0, kc == 1, [B_wqb, B_qan], [PB[b]])
                        act(CH.sqc[idx % 3][:, 0:n], psb[b][:, 0:n], AF.Square, [PB[b]], [CH.B_sqc[idx % 3]])

                    def q1f(idx, c0=c0, n=n):
                        b = PJ[idx % 2]
                        sl, rl, zl = idx % 3, idx % 2, (idx // 2) % 2
                        mm(psb[SSB[0]][:, 0:n], cmat[:, CM_BD3, :], CH.sqc[sl][:, 0:n], True, True, [CH.B_sqc[sl], B_const], [PB[SSB[0]]])
                        rstd_from(CH.rsc[rl][:, 0:n], psb[SSB[0]][:, 0:n], [PB[SSB[0]]], CH.B_rsc[rl])
                        stt("dve", zk[zl][:, 0:n], psb[b][:, 0:n], vec[:, 55:56], tabq[:, c0:c0 + n], ALU.mult, ALU.mult,
                            [PB[b], B_const, B_tabq], [B_zk[zl]])
                        tt("dve", zb[zl][:, 0:n], zk[zl][:, 0:n], CH.rsc[rl][:, 0:n], ALU.mult, [B_zk[zl], CH.B_rsc[rl]], [B_zb[zl]])

                    def q2f(idx, bi=bi, c0=c0, n=n):
                        zl = (idx // 2) % 2
                        mm(psb[AMB][0:96, 0:n], cmat[:, CM_AMAT, 0:96], zb[zl][:, 0:n], True, True, [B_zb[zl], B_const], [PB[AMB]])
                        if bi < 4:
                            cp("dve", QhT[0:96, c0:c0 + n], psb[AMB][0:96, 0:n], [PB[AMB]], [B_Qh[s]])
                        else:
                            cp("dve", QsT[0:96, h, :], psb[AMB][0:96, 0:n], [PB[AMB]], [B_QsT])
                    items.append((q0f, q1f, q2f))

                    def kpm(b, c0=c0, n=n):
                        mm(psb[b][0:64, 0:n], wkvb[:, h * 64:(h + 1) * 64], ckvT[:, c0:c0 + n], True, True, [B_wkvb, B_ckvT], [PB[b]])
                    if bi < 4:
                        dests = [(KhT[0:64, c0:c0 + n], B_Kh[s])]
                        post = lambda c0=c0, n=n: cp("pool", KhT[64:96, c0:c0 + n], kpeT[64:96, c0:c0 + n], [B_kpeT], [B_Kh[s]])
                    else:
                        dests = [(KnT[0:64, h, :], B_KnT)]
                        post = lambda c0=c0, n=n: cp("pool", KnT[64:96, h, :], kpeT[64:96, c0:c0 + n], [B_kpeT], [B_KnT])
                    k0f, k1f = CH.item(kpm, (0, 64), n, cmat[0:64, CM_BD64, 0:64], vec[0:64, 57:58], dests, PJ, SSB, post)
                    items.append((k0f, k1f, lambda idx: None))
                return sw_pipeline(items, [lambda it, i: it[0](i), lambda it, i: it[1](i), lambda it, i: it[2](i)], 1)

            def head_attn_gen(h):
                s = h % 2
                QhT = hn[:, 4 + s, :]
                KhT = hn[:, 6 + s, :]
                groups = []
                for i in range(4):
                    q0 = 512 * i
                    tiles = []
                    for kb in range(4 * i + 4):
                        ca = max(0, 128 * kb - 512 * i)
                        ncols = 512 - ca
                        diag = kb >= 4 * i
                        if diag:
                            pv = [(0, ca, ca + 64, VE[0:64, kb, h, :], 64, 0, 64, [B_VE])]
                            if ncols > 64:
                                pv.append((0, ca + 64, 512, VE[:, kb, h, :], 128, 64, ncols, [B_VE]))
                        else:
                            pv = [(0, ca, 512, VE[:, kb, h, :], 128, 0, ncols, [B_VE])]
                        tiles.append(dict(KT=KhT[0:96, kb * 128:(kb + 1) * 128], qrhs=QhT[0:96, q0 + ca:q0 + 512], nk=128,
                                          W=ncols, Gap=None, R=[B_Kh[s], B_Qh[s]], pv=pv))

                    def fin(obs, q0=q0):
                        drow = slice((h % 2) * 64, (h % 2) * 64 + 64)
                        attn_finish(AS, obs[0], 512, None, mixD[drow, h // 2, q0:q0 + 512], B_mixD, use_dve=True)
                    groups.append(dict(tiles=tiles, nO=1, finish=fin))
                return attention_gen(AS, groups, sc_d, (4, 5), (6, 7))

            drain(head_chain_gen(0))
            for h in range(8):
                drain(head_attn_gen(h), head_chain_gen(h + 1) if h < 7 else None)
            P.barrier()
            ar.reset(mKeep2)
            woD = ar.alloc([128, 4, 1024], BF16)
            B_woD = nb("woD")
            dma("pool", woD[:], odout_d[512:1024, :].rearrange("(k p) o -> p k o", p=128), [], [B_woD], B_woD)
            NS = 3
            cst = [ar.alloc([128, 4, 128], F32) for _ in range(NS)]
            kst = [ar.alloc([128, 4, 32], F32) for _ in range(NS)]
            ckc = [ar.alloc([128, 512], BF16) for _ in range(NS)]
            kpc = [ar.alloc([128, 512], BF16) for _ in range(NS)]
            KcT = [ar.alloc([128, 8, 512], BF16) for _ in range(NS)]
            VEc = [ar.alloc([128, 4, 8, 128], BF16) for _ in range(NS)]
            B_cst = [nb("cst") for _ in range(NS)]
            B_kst = [nb("kst") for _ in range(NS)]
            B_ckc = [nb("ckc") for _ in range(NS)]
            B_kpc = [nb("kpc") for _ in range(NS)]
            B_KcT = [nb("KcT") for _ in range(NS)]
            B_VEc = [nb("VEc") for _ in range(NS)]
            for s in range(NS):
                memset("pool", VEc[s][:, :, :, 64:128], 1.0, [B_VEc[s]])
            OS = 7
            PJs, SSBs = (0, 1, 2), (3,)

            def load_grp(g):
                s = g % NS
                dma("sp", cst[s][:], ckv_d[g * 512:(g + 1) * 512, :].rearrange("(t p) l -> p t l", p=128), [], [B_cst[s]], B_cst[s])
                dma("sp", kst[s][:], ckp_d[g * 512:(g + 1) * 512, :].rearrange("(t p) l -> p t l", p=128), [], [B_kst[s]], B_kst[s])

            def prep_gen(g):
                s = g % NS
                if g + 2 < 8:
                    load_grp(g + 2)
                for t4 in range(4):
                    tr(psb[0][:, t4 * 128:(t4 + 1) * 128], cst[s][:, t4, :], ident[:, :], [B_cst[s], B_const], [PB[0]])
                cp("act", ckc[s][:, :], psb[0][:, :], [PB[0]], [B_ckc[s]])
                for t4 in range(4):
                    tr(psb[1][0:32, t4 * 128:(t4 + 1) * 128], kst[s][:, t4, :], ident[:, :], [B_kst[s], B_const], [PB[1]])
                cp("dve", kpc[s][64:96, :], psb[1][0:32, :], [PB[1]], [B_kpc[s]])
                yield
                items = []
                for t4 in range(4):
                    def f0(b, t4=t4):
                        mm(psb[b][:, 0:512], ckc[s][:, t4 * 128:(t4 + 1) * 128], wkv_v, True, True, [B_ckc[s], B_wkvb], [PB[b]])

                    def f1(b, t4=t4):
                        cp("dve", VEc[s][:, t4, :, 0:64], psb[b][:, 0:512].rearrange("p (h d) -> p h d", h=8), [PB[b]], [B_VEc[s]])
                    items.append(simple_item(f0, f1, PJs))
                for hp in range(4):
                    def kpm(b, hp=hp):
                        mm(psb[b][:, 0:512], wkvb[:, hp * 128:(hp + 1) * 128], ckc[s][:, :], True, True, [B_wkvb, B_ckc[s]], [PB[b]])
                    dests = [(KcT[s][0:64, 2 * hp, :], B_KcT[s], (0, 64), vec[0:64, 57:58]),
                             (KcT[s][0:64, 2 * hp + 1, :], B_KcT[s], (64, 128), vec[64:128, 57:58])]

                    items.append(CH.item(kpm, (0, 128), 512, cmat[:, CM_BD64, :], vec[:, 57:58], dests, PJs, SSBs, None))
                for _ in run_items(items):
                    yield

            started = [False]

            def tiles_gen(g):
                s = g % NS
                items = list(range(4))

                def s0(t4, idx):
                    sl4 = (4 * g + t4) % 4
                    sbk = (4, 5)[(4 * g + t4) % 2]
                    mm(psb[sbk][:, 0:512].rearrange("p (h q) -> p h q", h=8), kpc[s][64:96, t4 * 128:(t4 + 1) * 128],
                       QsT[64:96, :, :], True, False, [B_kpc[s], B_QsT], [PB[sbk]])
                    for h in range(8):
                        mm(psb[sbk][:, h * 64:(h + 1) * 64], KcT[s][0:64, h, t4 * 128:(t4 + 1) * 128], QsT[0:64, h, :], False, False,
                           [B_KcT[s], B_QsT], [PB[sbk]])
                    act(AS.pb[sl4][:, 0:512], psb[sbk][:, 0:512], AF.Exp, [PB[sbk]], [AS.B_pb[sl4]], scale=sc_d)

                def s1(t4, idx):
                    sl4 = (4 * g + t4) % 4
                    for h in range(8):
                        mm(psb[OS][:, h * 64:(h + 1) * 64], VEc[s][:, t4, h, :], AS.pb[sl4][:, h * 64:(h + 1) * 64],
                           not started[0], False, [AS.B_pb[sl4], B_VEc[s]], [PB[OS]])
                        started[0] = True
                return sw_pipeline(items, [s0, s1], 1)

            load_grp(0)
            load_grp(1)
            drain(prep_gen(0))
            drain(prep_gen(1))
            for g in range(8):
                drain(tiles_gen(g), prep_gen(g + 2) if g + 2 < 8 else None)
            sbk = 4
            for h in range(8):
                mm(psb[sbk][0:64, h * 64:(h + 1) * 64], KnT[0:96, h, :], QsT[0:96, h, :], True, True, [B_KnT, B_QsT], [PB[sbk]])
            act(AS.pb[0][0:64, 0:512], psb[sbk][0:64, 0:512], AF.Exp, [PB[sbk]], [AS.B_pb[0]], scale=sc_d)
            for h in range(8):
                mm(psb[OS][:, h * 64:(h + 1) * 64], VE16[0:64, h, :], AS.pb[0][0:64, h * 64:(h + 1) * 64], False, False,
                   [AS.B_pb[0], B_VE16], [PB[OS]])
            act(AS.rcp[64:128, 0:512], psb[OS][64:128, 0:512], AF.Ln, [PB[OS]], [AS.B_rcp])
            act(AS.rcp2[0:64, 0:512], AS.rcp[64:128, 0:512], AF.Exp, [AS.B_rcp], [AS.B_rcp2], scale=-1.0)
            for h in range(8):
                drow = slice((h % 2) * 64, (h % 2) * 64 + 64)
                tt("dve", mixD[drow, h // 2, 2048:2112], psb[OS][0:64, h * 64:(h + 1) * 64], AS.rcp2[0:64, h * 64:(h + 1) * 64],
                   ALU.mult, [PB[OS], AS.B_rcp2], [B_mixD])
            out_proj(mixD, 4, woD, B_mixD, B_woD)
            P.barrier()

        stages = ["ffn1_0", "mix_0", "ffn2_0", "ffn1_1", "mix_1", "ffn2_1"]
        for st in stages:
            if stop is not None and st == stop:
                break
            if st == "ffn1_0":
                ffn(0, 0, next_gcol=8)
            elif st == "ffn2_0":
                ffn(0, 1, next_gcol=24, end_barrier=False)
            elif st == "ffn1_1":
                ffn(1, 0, skip_norm01=True, next_gcol=32)
            elif st == "ffn2_1":
                ffn(1, 1, end_barrier=False)
            elif st == "mix_0":
                mix_even()
            elif st == "mix_1":
                mix_odd()
        final_phase()
        P.finalize_and_emit()
    return nc


OUT_NAMES = ["o_yp", "o_ys", "o_akp", "o_avp", "o_bkp", "o_bvp", "o_cp", "o_ckvp", "o_kpep",
             "o_aks", "o_avs", "o_bks", "o_bvs", "o_cs", "o_ckvs", "o_kpes"]


def make_in_maps(inputs, cores):
    f = lambda a: np.ascontiguousarray(np.asarray(a, dtype=np.float32))
    consts = host_consts()
    vec = pack_vec(inputs)
    wqb = f(inputs["d_w_q_b"][0]).reshape(256, 8, 96)
    nope, rope = wqb[:, :, 0:64], wqb[:, :, 64:96]
    ropesw = np.concatenate([rope[:, :, 16:32], rope[:, :, 0:16]], axis=2)
    wqb_p = np.ascontiguousarray(np.concatenate([nope, rope, ropesw], axis=2))
    wkv = f(inputs["d_w_kv_b"][0]).reshape(128, 8, 128)
    wkvb_p = np.ascontiguousarray(np.concatenate([wkv[:, :, 0:64].reshape(128, 512), wkv[:, :, 64:128].reshape(128, 512)], axis=1))
    shared = {
        "ff1_w_gu": f(inputs["ff1_w_gu"]), "ff2_w_gu": f(inputs["ff2_w_gu"]),
        "ff1_w_down": f(inputs["ff1_w_down"]), "ff2_w_down": f(inputs["ff2_w_down"]),
        "ev_w_in": f(inputs["ev_w_in"][0]), "ev_w_out": f(inputs["ev_w_out"][0]),
        "od_w_in": f(inputs["od_w_in"][0]), "od_w_out": f(inputs["od_w_out"][0]),
        "wqb": wqb_p, "wkvb": wkvb_p, "vec": vec,
        "t5": f(inputs["t5_bias_table"]), "brel": f(inputs["b_rel_bias"][0]),
        "sinks": f(inputs["a_sinks"]).reshape(1, 8),
    }
    shared.update(consts)
    maps = []
    for b in cores:
        m = dict(shared)
        m["xp"] = f(inputs["x_prompt"][b])
        m["xs"] = f(inputs["x_sample"][b])
        m["cak"] = f(inputs["cache_a_k"][0, b]).reshape(128, 128)
        m["cav"] = f(inputs["cache_a_v"][0, b]).reshape(128, 128)
        m["cbk"] = f(inputs["cache_b_k"][0, b]).reshape(512, 512)
        m["cbv"] = f(inputs["cache_b_v"][0, b]).reshape(512, 512)
        m["ccv"] = f(inputs["state_c_conv"][0, b])
        m["ckv"] = f(inputs["cache_d_ckv"][0, b])
        m["ckp"] = f(inputs["cache_d_kpe"][0, b])
        maps.append(m)
    return maps


def assemble(results):
    def st(name, shape):
        return np.stack([np.asarray(r[name], np.float32).reshape(shape) for r in results])[None] \
            if name not in ("o_yp", "o_ys") else np.stack([np.asarray(r[name], np.float32) for r in results])

    return (
        st("o_yp", None), st("o_ys", None),
        st("o_akp", (128, 2, 64)), st("o_avp", (128, 2, 64)),
        st("o_bkp", (512, 8, 64)), st("o_bvp", (512, 8, 64)),
        st("o_cp", (2, 512)), st("o_ckvp", (2048, 128)), st("o_kpep", (2048, 32)),
        st("o_aks", (128, 2, 64)), st("o_avs", (128, 2, 64)),
        st("o_bks", (512, 8, 64)), st("o_bvs", (512, 8, 64)),
        st("o_cs", (2, 512)), st("o_ckvs", (64, 128)), st("o_kpes", (64, 32)),
    )


def kernel(**inputs):
    nc = build(stop=os.environ.get("MK_STOP"))
    maps = make_in_maps(inputs, list(range(8)))
    res = run_bass_kernel_spmd(nc, maps, core_ids=list(range(8)))
    return assemble(res.results)
```

```python
import contextlib
import os
import numpy as np
import ml_dtypes
import concourse.bass as bass
import concourse.mybir as mybir
from concourse.bass_utils import run_bass_kernel_spmd

F32 = mybir.dt.float32
BF16 = mybir.dt.bfloat16
ALU = mybir.AluOpType
AF = mybir.ActivationFunctionType

EPS = 1e-6
T = 2112
BLKS = [(0, 512), (512, 512), (1024, 512), (1536, 512), (2048, 64)]
NTILE = 17


def tile_rows(i):
    return (i * 128, 128) if i < 16 else (2048, 64)


class Buf:
    __slots__ = ("name", "writers", "readers", "sem", "semcount", "psum")

    def __init__(self, name, psum=False):
        self.name = name
        self.writers = []
        self.readers = []
        self.sem = None
        self.semcount = 0
        self.psum = psum


class Op:
    __slots__ = ("eng", "fn", "deps", "is_dma", "sem", "val", "signals", "idx")

    def __init__(self, eng, fn):
        self.eng = eng
        self.fn = fn
        self.deps = []
        self.is_dma = False
        self.sem = None
        self.val = 0
        self.signals = False


ENGS = ("pe", "act", "dve", "pool", "sp")
STRICT_SAME_ENGINE = False


class Prog:
    def __init__(self, nc):
        self.nc = nc
        self.ops = []
        self.out_dmas = []
        self.last = {}
        self.pending_dma = []

    def add(self, eng, fn, reads=(), writes=(), dma_buf=None, is_out=False):
        op = Op(eng, fn)
        op.idx = len(self.ops)
        deps = []
        for b in reads:
            deps.extend((d, 0) for d in b.writers)
            if b.psum:
                deps.extend((d, 2) for d in b.readers if d.eng != eng)
        for b in writes:
            deps.extend((d, 1) for d in b.writers)
            deps.extend((d, 1) for d in b.readers)
        seen = set()
        for d, kind in deps:
            if d is op or id(d) in seen:
                continue
            if d.eng == eng and not d.is_dma:
                if eng == "pe" or (kind != 0 and not STRICT_SAME_ENGINE):
                    continue
            seen.add(id(d))
            op.deps.append(d)
        for b in writes:
            b.writers = [op]
            b.readers = []
        for b in reads:
            if b not in writes:
                b.readers.append(op)
        if dma_buf is not None:
            op.is_dma = True
            op.signals = True
            op.sem = dma_buf
            self.pending_dma.append(op)
        self.ops.append(op)
        if not op.is_dma:
            self.last[eng] = op
        if is_out:
            self.out_dmas.append(op)
        return op

    def barrier(self):
        lasts = [o for o in self.last.values() if not o.is_dma]
        dmas = list(self.pending_dma)
        self.pending_dma = []
        for e in ENGS:
            op = Op(e, None)
            op.idx = len(self.ops)
            op.deps = [o for o in lasts if o.eng != e] + dmas
            self.ops.append(op)

    def finalize_and_emit(self):
        nc = self.nc
        fin = Op("sp", None)
        fin.deps = list(self.out_dmas)
        fin.idx = len(self.ops)
        self.ops.append(fin)
        for op in self.ops:
            for d in op.deps:
                d.signals = True
        with contextlib.ExitStack() as es:
            engsem = {}
            for e in ("pe", "act", "dve", "pool"):
                engsem[e] = es.enter_context(nc.semaphore("c_" + e))
            counters = {e: 0 for e in engsem}
            for op in self.ops:
                if op.is_dma:
                    b = op.sem
                    if b.sem is None:
                        b.sem = es.enter_context(nc.semaphore("d_" + b.name))
                        b.semcount = 0
                    b.semcount += 16
                    op.sem = b.sem
                    op.val = b.semcount
                elif op.signals and op.eng in engsem:
                    counters[op.eng] += 1
                    op.sem = engsem[op.eng]
                    op.val = counters[op.eng]
            block = es.enter_context(nc.Block())
            engobj = {"pe": "tensor", "act": "scalar", "dve": "vector",
                      "pool": "gpsimd", "sp": "sync"}

            def make(ename):
                def body(eng):
                    known = {}
                    for op in self.ops:
                        if op.eng != ename:
                            continue
                        need = {}
                        for d in op.deps:
                            if d.sem is None:
                                continue
                            k = id(d.sem)
                            if k not in need or need[k][1] < d.val:
                                need[k] = (d.sem, d.val)
                        for k, (s, v) in need.items():
                            if known.get(k, 0) >= v:
                                continue
                            eng.wait_ge(s, v)
                            known[k] = v
                        if op.fn is None:
                            continue
                        ins = op.fn(eng)
                        if op.signals:
                            ins.then_inc(op.sem, 16 if op.is_dma else 1)
                return body

            for ename in ENGS:
                getattr(block, engobj[ename])(make(ename))


class Arena:
    def __init__(self, ap, words):
        self.ap = ap
        self.words = words
        self.off = 0
        self.n = 0

    def mark(self):
        return self.off

    def reset(self, m=0):
        self.off = m

    def alloc(self, shape, dt, name=None):
        n = int(np.prod(shape[1:]))
        nw = n if dt == F32 else (n + 1) // 2
        assert self.off + nw <= self.words, (name, self.off, nw, self.words)
        v = self.ap[:, self.off:self.off + nw]
        if dt != F32:
            v = v.bitcast(dt)[:, 0:n]
        self.off += nw
        if len(shape) == 3:
            v = v.rearrange("p (a b) -> p a b", a=shape[1])
        elif len(shape) == 4:
            v = v.rearrange("p (a b c) -> p a b c", a=shape[1], b=shape[2])
        self.n += 1
        return v


def _t5_bucket_np(rel):
    import jax.numpy as jnp
    import math
    import jax
    with jax.default_device(jax.devices("cpu")[0]):
        return _t5_bucket_cpu(rel)


def _t5_bucket_cpu(rel):
    import jax.numpy as jnp
    import math
    rel = jnp.asarray(rel, dtype=jnp.int32)
    nb = 16
    max_exact = 8
    n = -rel
    ret = jnp.where(n < 0, nb, 0)
    n = jnp.abs(n)
    nf = jnp.maximum(n, 1).astype(jnp.float32)
    large = max_exact + (jnp.log(nf / max_exact) / math.log(128 / max_exact) * (nb - max_exact)).astype(jnp.int32)
    large = jnp.minimum(large, nb - 1)
    return np.asarray(ret + jnp.where(n < max_exact, n, large))


def host_consts():
    c = {}
    c["ident"] = np.eye(128, dtype=np.float32)
    cm = np.zeros((128, 6, 128), np.float32)
    cm[:, 0, :] = 1.0 / 1024
    cm[0:64, 1, 0:64] = 1.0 / 64
    cm[64:128, 1, 64:128] = 1.0 / 64
    cm[:, 2, :] = 1.0 / 256
    cm[:, 3, :] = 1.0 / 128
    cm[0:64, 4, 0:64] = 1.0 / 64
    cm[64:96, 4, 64:96] = 1.0 / 32
    cm[96:128, 4, 96:128] = 1.0 / 32
    for k in range(128):
        m = k if k < 96 else k - 32
        cm[k, 5, m] = 1.0
    c["cmat"] = cm.astype(ml_dtypes.bfloat16)
    j = np.arange(384)
    bk = _t5_bucket_np(127 - j)
    oha = np.zeros((32, 384), np.float32)
    oha[bk, j] = 1.0
    c["oh_a"] = oha
    j = np.arange(768)
    idx = np.clip(127 - j, -128, 128) + 128
    ohb = np.zeros((256, 768), np.float32)
    ohb[idx, j] = 1.0
    c["oh_b"] = np.ascontiguousarray(ohb.reshape(2, 128, 768).transpose(1, 0, 2))
    p = np.arange(128)[:, None] // 64
    ya = np.arange(256)[None, :] // 64
    c["vis_a"] = ((ya - p >= 0) & (ya - p <= 2)).astype(np.float32)
    yb = np.arange(640)[None, :] // 64
    c["vis_b"] = ((yb - p >= 0) & (yb - p <= 8)).astype(np.float32)
    pos = np.concatenate([np.arange(2048), 4096 + np.arange(64)]).astype(np.float32)
    half = 16
    inv = (1.0 / (np.float32(10000.0) ** (np.arange(half, dtype=np.float32) / np.float32(half)))).astype(np.float32)
    ang = (pos[None, :] * inv[:, None]).astype(np.float32)
    cos = np.cos(ang).astype(np.float32)
    sin = np.sin(ang).astype(np.float32)
    tab = np.ones((128, T), np.float32)
    tab[64:80] = cos
    tab[80:96] = cos
    tab[96:112] = -sin
    tab[112:128] = sin
    c["tabq"] = tab
    sk = np.zeros((128, T), np.float32)
    sk[64:80] = -sin
    sk[80:96] = sin
    c["sink"] = sk
    return c


NVEC = 70


def pack_vec(inp):
    v = np.zeros((128, NVEC), np.float32)

    def col8(a):
        return np.asarray(a, np.float32).reshape(8, 128).T

    for l in range(2):
        v[:, 24 * l + 0:24 * l + 8] = col8(inp["ff1_norm"][l])
        v[:, 24 * l + 8:24 * l + 16] = col8(inp["mix_norm"][l])
        v[:, 24 * l + 16:24 * l + 24] = col8(inp["ff2_norm"][l])
    for k, nm in enumerate(["a_q_norm", "a_k_norm", "b_q_norm", "b_k_norm"]):
        g = np.asarray(inp[nm][0], np.float32)
        v[:, 48 + k] = np.concatenate([g, g])
    v[:, 52:54] = np.asarray(inp["d_q_a_norm"][0], np.float32).reshape(2, 128).T
    v[:, 54] = np.asarray(inp["d_kv_a_norm"][0], np.float32)
    qn = np.asarray(inp["d_q_nope_norm"][0], np.float32)
    qr = np.asarray(inp["d_q_rope_norm"][0], np.float32)
    kn = np.asarray(inp["d_k_nope_norm"][0], np.float32)
    kr = np.asarray(inp["d_k_rope_norm"][0], np.float32)
    sw = lambda a: np.concatenate([a[16:], a[:16]])
    v[:, 55] = np.concatenate([qn, qr, sw(qr)])
    v[0:64, 56] = kn
    v[64:96, 56] = kr
    v[96:128, 56] = sw(kr)
    v[:, 57] = np.concatenate([kn, kn])
    cw = np.asarray(inp["c_conv_w"][0], np.float32)
    for jj in range(3):
        v[:, 58 + 4 * jj:58 + 4 * jj + 4] = cw[jj].reshape(4, 128).T
    return v


def build(stop=None):
    nc = bass.Bass("TRN2", target_bir_lowering=False)

    def din(name, shape, dt=F32):
        return nc.dram_tensor(name, list(shape), dt, kind="ExternalInput").ap()

    def dout(name, shape):
        return nc.dram_tensor(name, list(shape), F32, kind="ExternalOutput").ap()

    xp_d = din("xp", [2048, 1024])
    xs_d = din("xs", [64, 1024])
    cak_d = din("cak", [128, 128])
    cav_d = din("cav", [128, 128])
    cbk_d = din("cbk", [512, 512])
    cbv_d = din("cbv", [512, 512])
    ccv_d = din("ccv", [2, 512])
    ckv_d = din("ckv", [4096, 128])
    ckp_d = din("ckp", [4096, 32])
    wgu_d = [din("ff1_w_gu", [2, 1024, 5632]), din("ff2_w_gu", [2, 1024, 5632])]
    wdn_d = [din("ff1_w_down", [2, 2816, 1024]), din("ff2_w_down", [2, 2816, 1024])]
    evin_d = din("ev_w_in", [1024, 2304])
    evout_d = din("ev_w_out", [1024, 1024])
    odin_d = din("od_w_in", [1024, 1952])
    odout_d = din("od_w_out", [1024, 1024])
    wqb_d = din("wqb", [256, 8, 128])
    wkvb_d = din("wkvb", [128, 1024])
    vec_d = din("vec", [128, NVEC])
    t5_d = din("t5", [32, 8])
    brel_d = din("brel", [8, 257])
    sinks_d = din("sinks", [1, 8])
    ident_d = din("ident", [128, 128])
    cmat_d = din("cmat", [128, 6, 128], BF16)
    oha_d = din("oh_a", [32, 384])
    ohb_d = din("oh_b", [128, 2, 768])
    visa_d = din("vis_a", [128, 256])
    visb_d = din("vis_b", [128, 640])
    tabq_d = din("tabq", [128, T])
    sink_d = din("sink", [128, T])

    o_yp = dout("o_yp", [2048, 1024])
    o_ys = dout("o_ys", [64, 1024])
    o_akp = dout("o_akp", [128, 128])
    o_avp = dout("o_avp", [128, 128])
    o_bkp = dout("o_bkp", [512, 512])
    o_bvp = dout("o_bvp", [512, 512])
    o_cp = dout("o_cp", [2, 512])
    o_ckvp = dout("o_ckvp", [2048, 128])
    o_kpep = dout("o_kpep", [2048, 32])
    o_aks = dout("o_aks", [128, 128])
    o_avs = dout("o_avs", [128, 128])
    o_bks = dout("o_bks", [512, 512])
    o_bvs = dout("o_bvs", [512, 512])
    o_cs = dout("o_cs", [2, 512])
    o_ckvs = dout("o_ckvs", [64, 128])
    o_kpes = dout("o_kpes", [64, 32])
    gscr = nc.dram_tensor("gscr", [16, 128, 768], F32, kind="Internal").ap()

    P = Prog(nc)
    es = contextlib.ExitStack()
    with es:
        def sbt(name, shape, dt):
            return es.enter_context(nc.sbuf_tensor("s_" + name, shape, dt))

        xT = sbt("xT", [128, 8, T], F32)
        hn = sbt("hn", [128, 8, T], BF16)
        ident = sbt("ident", [128, 128], F32)
        cmat = sbt("cmat", [128, 6, 128], BF16)
        vec = sbt("vec", [128, NVEC], F32)
        AW = 27000
        arena_t = sbt("arena", [128, AW], F32)
        ar = Arena(arena_t, AW)
        psb = [es.enter_context(nc.psum_tensor("ps%d" % i, [128, 512], F32)) for i in range(8)]
        PB = [Buf("ps%d" % i, psum=True) for i in range(8)]

        bufn = [0]

        def nb(name="b"):
            bufn[0] += 1
            return Buf("%s%d" % (name, bufn[0]))

        B_x = [nb("x") for _ in BLKS]
        B_hn = [nb("hn") for _ in BLKS]
        B_const = nb("const")
        B_hnall = None

        def mm(out, lhsT, rhs, start, stop, R, W):
            P.add("pe", lambda e: e.matmul(out, lhsT=lhsT, rhs=rhs, start=start, stop=stop,
                                           skip_group_check=True), reads=R, writes=W)

        def tr(out, in_, idn, R, W):
            P.add("pe", lambda e: e.transpose(out=out, in_=in_, identity=idn), reads=R, writes=W)

        def act(out, in_, func, R, W, scale=1.0, bias=0.0):
            P.add("act", lambda e: e.activation(out=out, in_=in_, func=func, bias=bias, scale=scale),
                  reads=R, writes=W)

        def cp(eng, out, in_, R, W):
            if eng == "act":
                P.add("act", lambda e: e.copy(out=out, in_=in_), reads=R, writes=W)
            else:
                P.add(eng, lambda e: e.tensor_copy(out=out, in_=in_), reads=R, writes=W)

        def tt(eng, out, in0, in1, op, R, W):
            P.add(eng, lambda e: e.tensor_tensor(out=out, in0=in0, in1=in1, op=op), reads=R, writes=W)

        def ts(eng, out, in0, s1, s2, op0, op1, R, W):
            if s2 is None:
                P.add(eng, lambda e: e.tensor_scalar(out=out, in0=in0, scalar1=s1, scalar2=None, op0=op0),
                      reads=R, writes=W)
            else:
                P.add(eng, lambda e: e.tensor_scalar(out=out, in0=in0, scalar1=s1, scalar2=s2, op0=op0, op1=op1),
                      reads=R, writes=W)

        def stt(eng, out, in0, scalar, in1, op0, op1, R, W):
            P.add(eng, lambda e: e.scalar_tensor_tensor(out=out, in0=in0, scalar=scalar, in1=in1, op0=op0, op1=op1),
                  reads=R, writes=W)

        def recip(out, in_, R, W):
            P.add("dve", lambda e: e.reciprocal(out=out, in_=in_), reads=R, writes=W)

        def rstd_from(out, ss_psum, R, B_out):
            act(out, ss_psum, AF.Ln, R, [B_out], bias=EPS)
            act(out, out, AF.Exp, [B_out], [B_out], scale=-0.5)

        def memset(eng, ap, val, W):
            P.add(eng, lambda e: e.memset(ap, val), writes=W)

        def dma(q, out, in_, R, W, buf, is_out=False):
            P.add(q, lambda e: e.dma_start(out=out, in_=in_), reads=R, writes=W, dma_buf=buf, is_out=is_out)

        def dma_nc(q, out, in_, R, W, buf, is_out=False):
            P.add(q, lambda e: e.dma_start(out=out, in_=in_, allow_slow_non_contiguous=True),
                  reads=R, writes=W, dma_buf=buf, is_out=is_out)

        rr = [0]

        def ev_eng():
            rr[0] += 1
            return "act" if rr[0] % 2 else "dve"

        dma("sp", ident[:], ident_d, [], [B_const], B_const)
        dma("sp", cmat[:], cmat_d, [], [B_const], B_const)
        dma("sp", vec[:], vec_d, [], [B_const], B_const)
        CM_ONES1024, CM_BD64, CM_ONES256, CM_ONES128, CM_BD3, CM_AMAT = range(6)

        FFN_END = 22464
        ar.reset(FFN_END)
        stg = [ar.alloc([128, 1024], F32) for _ in range(2)]
        B_stg = [nb("stg") for _ in range(2)]
        for i in range(NTILE):
            r0, nr = tile_rows(i)
            s = i % 2
            src = xp_d[r0:r0 + nr, :] if i < 16 else xs_d[:, :]
            dma("sp", stg[s][0:nr, :], src, [], [B_stg[s]], B_stg[s])
            blk = min(i // 4, 4)
            for hf in range(2):
                bk = (2 * i + hf) % 4
                for k in range(4):
                    c = 4 * hf + k
                    tr(psb[bk][:, k * 128:k * 128 + nr], stg[s][0:nr, c * 128:(c + 1) * 128], ident[0:nr, 0:nr],
                       [B_stg[s], B_const], [PB[bk]])
                src_ps = psb[bk][:, :].rearrange("p (c t) -> p c t", c=4)[:, :, 0:nr]
                cp(ev_eng(), xT[:, 4 * hf:4 * hf + 4, r0:r0 + nr], src_ps, [PB[bk]], [B_x[blk]])
        pass

        def norm_block(bi, gcol0, sq, rs, B_sq, B_rs, part="ab"):
            c0, n = BLKS[bi]
            if "a" in part:
                P.add("pool", lambda e: e.tensor_tensor(out=sq[:, :, 0:n], in0=xT[:, :, c0:c0 + n],
                                                        in1=xT[:, :, c0:c0 + n], op=ALU.mult),
                      reads=[B_x[bi]], writes=[B_sq])
            if "b" not in part:
                return
            for c in range(8):
                mm(psb[6][:, 0:n], cmat[:, CM_ONES1024, :], sq[:, c, 0:n], c == 0, c == 7, [B_sq, B_const], [PB[6]])
            rstd_from(rs[:, 0:n], psb[6][:, 0:n], [PB[6]], B_rs)
            for c in range(8):
                stt("dve", hn[:, c, c0:c0 + n], xT[:, c, c0:c0 + n], vec[:, gcol0 + c:gcol0 + c + 1], rs[:, 0:n],
                    ALU.mult, ALU.mult, [B_x[bi], B_rs, B_const], [B_hn[bi]])

        FB = {}

        def ffn(l, which, skip_norm01=False, next_gcol=None, end_barrier=True):
            wgu = wgu_d[which]
            wdn = wdn_d[which]
            gcol0 = 24 * l + (0 if which == 0 else 16)
            ar.reset()
            h = ar.alloc([128, 22, 1088], BF16)
            wg = [ar.alloc([128, 8, 256], BF16) for _ in range(2)]
            wu = [ar.alloc([128, 8, 256], BF16) for _ in range(2)]
            wd = [ar.alloc([128, 22, 128], BF16) for _ in range(2)]
            sq = ar.alloc([128, 8, 512], BF16)
            rs = ar.alloc([128, 512], F32)
            sg = [ar.alloc([128, 512], F32) for _ in range(2)]
            assert ar.off == FFN_END
            if not FB:
                FB.update(h=nb("h"), sq=nb("sq"), rs=nb("rs"), wgu=[nb("wgu") for _ in range(2)],
                          wd=[nb("wd") for _ in range(2)], sg=[nb("sg") for _ in range(2)])
            B_h, B_sq, B_rs, B_wgu, B_wd, B_sg = FB["h"], FB["sq"], FB["rs"], FB["wgu"], FB["wd"], FB["sg"]
            gi = 0
            yi = 0
            wslot = 0
            dslot = 0
            for half in ([0, 1], [2, 3, 4]):
                h0 = BLKS[half[0]][0]
                if half[0] == 0 and not skip_norm01:
                    for bi in half:
                        norm_block(bi, gcol0, sq, rs, B_sq, B_rs)
                for jg in range(11):
                    if half[0] == 0 and jg in (3, 5, 7):
                        norm_block({3: 2, 5: 3, 7: 4}[jg], gcol0, sq, rs, B_sq, B_rs)
                    if half[0] == 2 and next_gcol is not None and jg in (4, 7):
                        norm_block({4: 0, 7: 1}[jg], next_gcol, sq, rs, B_sq, B_rs)
                    s = wslot % 2
                    wslot += 1
                    dma("pool", wg[s][:], wgu[l, :, jg * 256:(jg + 1) * 256].rearrange("(c p) f -> p c f", p=128),
                        [], [B_wgu[s]], B_wgu[s])
                    dma("pool", wu[s][:], wgu[l, :, 2816 + jg * 256:2816 + (jg + 1) * 256].rearrange("(c p) f -> p c f", p=128),
                        [], [B_wgu[s]], B_wgu[s])
                    for fc in range(2):
                        j = 2 * jg + fc
                        for bi in half:
                            c0, n = BLKS[bi]
                            gb, ub = gi % 2, 2 + gi % 2
                            sgi = gi % 2
                            gi += 1
                            for c in range(8):
                                mm(psb[gb][:, 0:n], wg[s][:, c, fc * 128:(fc + 1) * 128], hn[:, c, c0:c0 + n],
                                   c == 0, c == 7, [B_wgu[s], B_hn[bi]], [PB[gb]])
                            for c in range(8):
                                mm(psb[ub][:, 0:n], wu[s][:, c, fc * 128:(fc + 1) * 128], hn[:, c, c0:c0 + n],
                                   c == 0, c == 7, [B_wgu[s], B_hn[bi]], [PB[ub]])
                            act(sg[sgi][:, 0:n], psb[gb][:, 0:n], AF.Silu, [PB[gb]], [B_sg[sgi]])
                            tt("dve", h[:, j, c0 - h0:c0 - h0 + n], psb[ub][:, 0:n], sg[sgi][:, 0:n], ALU.mult,
                               [PB[ub], B_sg[sgi]], [B_h])
                for o in range(8):
                    s = dslot % 2
                    dslot += 1
                    dma("pool", wd[s][:], wdn[l, :, o * 128:(o + 1) * 128].rearrange("(j p) o -> p j o", p=128),
                        [], [B_wd[s]], B_wd[s])
                    for bi in half:
                        c0, n = BLKS[bi]
                        yb = 4 + yi % 2
                        yi += 1
                        for j in range(22):
                            mm(psb[yb][:, 0:n], wd[s][:, j, :], h[:, j, c0 - h0:c0 - h0 + n], j == 0, j == 21,
                               [B_wd[s], B_h], [PB[yb]])
                        stt("dve", xT[:, o, c0:c0 + n], psb[yb][:, 0:n], 0.5, xT[:, o, c0:c0 + n], ALU.mult, ALU.add,
                            [PB[yb], B_x[bi]], [B_x[bi]])
            if end_barrier:
                P.barrier()

        def final_phase():
            ar.reset(FFN_END)
            st2 = [ar.alloc([128, 1024], F32) for _ in range(2)]
            B_st2 = [nb("st2") for _ in range(2)]
            for i in range(NTILE):
                r0, nr = tile_rows(i)
                s = i % 2
                blk = min(i // 4, 4)
                for hf in range(2):
                    bk = (2 * i + hf) % 4
                    for k in range(4):
                        c = 4 * hf + k
                        tr(psb[bk][0:nr, k * 128:(k + 1) * 128], xT[:, c, r0:r0 + nr], ident[:, :],
                           [B_x[blk], B_const], [PB[bk]])
                    cp(ev_eng(), st2[s][0:nr, hf * 512:(hf + 1) * 512], psb[bk][0:nr, :], [PB[bk]], [B_st2[s]])
                dst = o_yp[r0:r0 + nr, :] if i < 16 else o_ys[:, :]
                dma("sp", dst, st2[s][0:nr, :], [B_st2[s]], [], B_st2[s], is_out=True)


        def sw_pipeline(items, stage_fns, lag=1):
            n = len(items)
            K = len(stage_fns)
            for s in range(n + (K - 1) * lag):
                for k in range(K):
                    idx = s - k * lag
                    if 0 <= idx < n:
                        stage_fns[k](items[idx], idx)
                yield

        def drain(*gens):
            gens = [g for g in gens if g is not None]
            while gens:
                for g in list(gens):
                    try:
                        next(g)
                    except StopIteration:
                        gens.remove(g)

        def norm_all(gcol0, blocks=(0, 1, 2, 3, 4)):
            m = ar.mark()
            drain(norm_all_gen(gcol0, blocks))
            P.barrier()
            ar.reset(m)

        def norm_all_gen(gcol0, blocks=(0, 1, 2, 3, 4)):
            sq = ar.alloc([128, 8, 512], BF16)
            rs = ar.alloc([128, 512], F32)
            B_sq, B_rs = nb("sq"), nb("rs")
            for bi in blocks:
                norm_block(bi, gcol0, sq, rs, B_sq, B_rs)
                yield

        class Chain:
            def __init__(self):
                self.sqc = [ar.alloc([128, 512], BF16) for _ in range(3)]
                self.rsc = [ar.alloc([128, 512], F32) for _ in range(2)]
                self.B_sqc = [nb("sqc") for _ in range(3)]
                self.B_rsc = [nb("rsc") for _ in range(2)]

            def item(self, pmm, rows, n, bd_lhsT, gcol, dests, PJ, SSB, post=None):
                r0, r1 = rows

                def s0(idx):
                    b = PJ[idx % len(PJ)]
                    sl = idx % 3
                    pmm(b)
                    act(self.sqc[sl][r0:r1, 0:n], psb[b][r0:r1, 0:n], AF.Square, [PB[b]], [self.B_sqc[sl]])

                def s1(idx):
                    b = PJ[idx % len(PJ)]
                    sl = idx % 3
                    rl = idx % 2
                    ssb = SSB[idx % len(SSB)]
                    mm(psb[ssb][r0:r1, 0:n], bd_lhsT, self.sqc[sl][r0:r1, 0:n], True, True,
                       [self.B_sqc[sl], B_const], [PB[ssb]])
                    rstd_from(self.rsc[rl][r0:r1, 0:n], psb[ssb][r0:r1, 0:n], [PB[ssb]], self.B_rsc[rl])
                    for d in dests:
                        ap, bf = d[0], d[1]
                        a, b_ = d[2] if len(d) > 2 else (r0, r1)
                        g = d[3] if len(d) > 3 else gcol
                        stt("dve", ap, psb[b][a:b_, 0:n], g, self.rsc[rl][a:b_, 0:n], ALU.mult, ALU.mult,
                            [PB[b], self.B_rsc[rl], B_const], [bf])
                    if post is not None:
                        post()

                return (s0, s1)

        def simple_item(f0, f1, PJ):
            return (lambda idx: f0(PJ[idx % len(PJ)]), lambda idx: f1(PJ[idx % len(PJ)]))

        def run_items(items, lag=1):
            return sw_pipeline(items, [lambda it, i: it[0](i), lambda it, i: it[1](i)], lag)

        def out_proj(mix, nk, wo, B_mix, B_wo, banks=(0, 1, 2)):
            cnt = 0
            for bi in range(5):
                c0, n = BLKS[bi]
                for o in range(8):
                    bk = banks[cnt % len(banks)]
                    cnt += 1
                    for k in range(nk):
                        mm(psb[bk][:, 0:n], wo[:, k, o * 128:(o + 1) * 128], mix[:, k, c0:c0 + n], k == 0, k == nk - 1,
                           [B_wo, B_mix], [PB[bk]])
                    tt("dve", xT[:, o, c0:c0 + n], psb[bk][:, 0:n], xT[:, o, c0:c0 + n], ALU.add,
                       [PB[bk], B_x[bi]], [B_x[bi]])

        NPB = 6

        class AttnScratch:
            def __init__(self, need_pexp=True):
                self.pexp = [ar.alloc([128, 512], F32) for _ in range(3)] if need_pexp else None
                self.pb = [ar.alloc([128, 512], BF16) for _ in range(NPB)]
                self.B_pexp = [nb("pexp") for _ in range(3)]
                self.B_pb = [nb("pb") for _ in range(NPB)]
                self.rcp = ar.alloc([128, 512], F32)
                self.rcp2 = self.rcp
                self.B_rcp, self.B_rcp2 = nb("rcp"), nb("rcp2")

        def attention_gen(A, groups, scale, SB, OB, lag=3):
            items = []
            ob0 = 0
            for gi, g in enumerate(groups):
                g["ob0"] = ob0
                ob0 += g["nO"]
                for ti, t in enumerate(g["tiles"]):
                    items.append((gi, t, ti == len(g["tiles"]) - 1))
            started = set()

            def s0(it, idx):
                gi, t, last = it
                sl3, sl4 = idx % 3, idx % NPB
                sbk = SB[idx % len(SB)]
                nk, W = t["nk"], t["W"]
                ov = psb[sbk][0:nk, 0:W]
                pb_v = A.pb[sl4][0:nk, 0:W]
                view = t.get("view")
                if view is not None:
                    ov, pb_v = view(ov), view(pb_v)
                mm(ov, t["KT"], t["qrhs"], True, True, t["R"], [PB[sbk]])
                if t.get("Gap") is None:
                    act(pb_v, ov, AF.Exp, [PB[sbk]], [A.B_pb[sl4]], scale=scale)
                else:
                    pe_v = A.pexp[sl3][0:nk, 0:W]
                    if view is not None:
                        pe_v = view(pe_v)
                    act(pe_v, ov, AF.Exp, [PB[sbk]], [A.B_pexp[sl3]], scale=scale)
                    tt("pool" if idx % 3 == 2 else "dve", pb_v, pe_v, t["Gap"], ALU.mult,
                       [A.B_pexp[sl3], B_G], [A.B_pb[sl4]])

            def s1(it, idx):
                gi, t, last = it
                g = groups[gi]
                sl4 = idx % NPB
                obs = [OB[(g["ob0"] + j) % len(OB)] for j in range(g["nO"])]
                for (j, lo, hi, VEap, nkr, plo, phi, Rk) in t["pv"]:
                    first = (gi, j) not in started
                    started.add((gi, j))
                    mm(psb[obs[j]][:, lo:hi], VEap, A.pb[sl4][0:nkr, plo:phi], first, False,
                       [A.B_pb[sl4]] + list(Rk), [PB[obs[j]]])
                if last:
                    g["finish"](obs)

            return sw_pipeline(items, [s0, s1], lag)

        def attn_finish(A, ob, ncols, esink_ap, dest, B_dest, R_extra=(), use_dve=False):
            if use_dve:
                recip(A.rcp2[0:64, 0:ncols], psb[ob][64:128, 0:ncols], [PB[ob]], [A.B_rcp2])
            else:
                if esink_ap is not None:
                    act(A.rcp[64:128, 0:ncols], psb[ob][64:128, 0:ncols], AF.Ln, [PB[ob], B_G], [A.B_rcp], bias=esink_ap)
                else:
                    act(A.rcp[64:128, 0:ncols], psb[ob][64:128, 0:ncols], AF.Ln, [PB[ob]], [A.B_rcp])
                act(A.rcp2[0:64, 0:ncols], A.rcp[64:128, 0:ncols], AF.Exp, [A.B_rcp], [A.B_rcp2], scale=-1.0)
            tt("dve", dest, psb[ob][0:64, 0:ncols], A.rcp2[0:64, 0:ncols], ALU.mult,
               [PB[ob], A.B_rcp2] + list(R_extra), [B_dest])

        B_G = nb("G")

        class OutStage:
            def __init__(self, width=256):
                self.ostg = [ar.alloc([128, width], F32) for _ in range(2)]
                self.B = [nb("ostg") for _ in range(2)]
                self.cnt = 0

            def store_rows(self, src_bank, nr, ncols, dst):
                s = self.cnt % 2
                self.cnt += 1
                cp(ev_eng(), self.ostg[s][0:nr, 0:ncols], psb[src_bank][0:nr, 0:ncols], [PB[src_bank]], [self.B[s]])
                dma("sp", dst, self.ostg[s][0:nr, 0:ncols], [self.B[s]], [], self.B[s], is_out=True)

        def mix_even():
            ar.reset()
            G_A = ar.alloc([128, 8, 256], BF16)
            G_B = ar.alloc([128, 8, 640], BF16)
            esink = ar.alloc([128, 8], F32)
            mG = ar.mark()
            gnorm = norm_all_gen(8, (2, 3, 4))
            t5s = ar.alloc([128, 8], F32)
            tcb = ar.alloc([128, 2, 8], F32)
            oha = ar.alloc([128, 384], F32)
            ohb = ar.alloc([128, 2, 768], F32)
            visa = ar.alloc([128, 256], F32)
            visb = ar.alloc([128, 640], F32)
            sraw = ar.alloc([128, 8], F32)
            lbA = [ar.alloc([128, 128], F32) for _ in range(8)]
            lbB = [ar.alloc([128, 2, 128], F32) for _ in range(8)]
            rrA = [ar.alloc([128, 384], F32) for _ in range(8)]
            rrB = [ar.alloc([128, 768], F32) for _ in range(8)]
            gpA = [ar.alloc([128, 256], F32) for _ in range(4)]
            gpB = [ar.alloc([128, 640], F32) for _ in range(4)]
            B_gc = nb("gconst")
            B_lbA = [nb("lbA") for _ in range(8)]
            B_lbB = [nb("lbB") for _ in range(8)]
            B_rrA = [nb("rrA") for _ in range(8)]
            B_rrB = [nb("rrB") for _ in range(8)]
            B_gpA = [nb("gpA") for _ in range(4)]
            B_gpB = [nb("gpB") for _ in range(4)]
            B_gscr = [nb("gscr") for _ in range(16)]
            dma("sp", t5s[0:32, :], t5_d, [], [B_gc], B_gc)
            for kc in range(2):
                dma_nc("sp", tcb[:, kc, :], brel_d[:, kc * 128:(kc + 1) * 128].rearrange("h p -> p h"), [], [B_gc], B_gc)
            dma("sp", oha[0:32, :], oha_d, [], [B_gc], B_gc)
            dma("sp", ohb[:], ohb_d, [], [B_gc], B_gc)
            dma("sp", visa[:], visa_d, [], [B_gc], B_gc)
            dma("sp", visb[:], visb_d, [], [B_gc], B_gc)
            dma("sp", sraw[:], sinks_d.partition_broadcast(128), [], [B_gc], B_gc)
            act(esink[:], sraw[:], AF.Exp, [B_gc], [B_G])
            next(gnorm)
            for h in range(8):
                cp("dve", lbA[h][0:32, :], t5s[0:32, h:h + 1].to_broadcast([32, 128]), [B_gc], [B_lbA[h]])
                for kc in range(2):
                    cp("dve", lbB[h][:, kc, :], tcb[:, kc, h:h + 1].to_broadcast([128, 128]), [B_gc], [B_lbB[h]])
            next(gnorm)
            for h in range(8):
                ba = (0, 3)[h % 2]
                bb = (1, 4)[h % 2]
                bc = (2, 5)[h % 2]
                mm(psb[ba][:, 0:384], lbA[h][0:32, :], oha[0:32, 0:384], True, True, [B_lbA[h], B_gc], [PB[ba]])
                act(rrA[h][:, :], psb[ba][:, 0:384], AF.Exp, [PB[ba]], [B_rrA[h]])
                for kc in range(2):
                    mm(psb[bb][:, 0:256], lbB[h][:, kc, :], ohb[:, kc, 0:256], kc == 0, kc == 1, [B_lbB[h], B_gc], [PB[bb]])
                act(rrB[h][:, 0:256], psb[bb][:, 0:256], AF.Exp, [PB[bb]], [B_rrB[h]])
                act(rrB[h][:, 256:768], psb[bb][:, 255:256].to_broadcast([128, 512]), AF.Exp, [PB[bb]], [B_rrB[h]])
                scrA = bass.AP(gscr.tensor, h * 128 * 768, [[384, 128], [1, 384]])
                dma("sp", scrA, rrA[h][:, :], [B_rrA[h]], [B_gscr[h]], B_gscr[h])
                scrB = bass.AP(gscr.tensor, (8 + h) * 128 * 768, [[768, 128], [1, 768]])
                dma("sp", scrB, rrB[h][:, :], [B_rrB[h]], [B_gscr[8 + h]], B_gscr[8 + h])
            next(gnorm)
            for h in range(8):
                s = h % 4
                skA = bass.AP(gscr.tensor, h * 128 * 768 + 127, [[383, 128], [1, 256]])
                dma("sp", gpA[s][:, :], skA, [B_gscr[h]], [B_gpA[s]], B_gpA[s])
                skB = bass.AP(gscr.tensor, (8 + h) * 128 * 768 + 127, [[767, 128], [1, 640]])
                dma("sp", gpB[s][:, :], skB, [B_gscr[8 + h]], [B_gpB[s]], B_gpB[s])
                kvh, g = h // 4, h % 4
                slot = kvh * 4 + (g % 2) * 2 + g // 2
                tt("dve", G_A[:, slot, :], gpA[s][:, :], visa[:], ALU.mult, [B_gpA[s], B_gc], [B_G])
                tt("dve", G_B[:, h, :], gpB[s][:, :], visb[:], ALU.mult, [B_gpB[s], B_gc], [B_G])
            P.barrier()
            ar.reset(mG)
            CH = Chain()
            AS = AttnScratch()
            k32 = [ar.alloc([128, 512], F32) for _ in range(1)]
            B_k32 = [nb("k32") for _ in range(1)]
            OSt = OutStage(256)
            mS = ar.mark()
            PJ, SSB = (0, 1, 2), (3, 4)

            def k_out(k32cols, B_src, nr, dst, bank):
                tr(psb[bank][0:nr, 0:128], k32cols, ident[:, :], [B_src, B_const], [PB[bank]])
                OSt.store_rows(bank, nr, 128, dst)

            wsrc = lambda a, b: evin_d[:, a:b].rearrange("(c p) f -> p c f", p=128)

            wbuf = ar.alloc([128, 8, 896], BF16)
            B_w = nb("wA")
            aqT = ar.alloc([128, 4, T], BF16)
            akT = ar.alloc([128, 2, T], BF16)
            aVE = ar.alloc([128, NTILE, 2, 128], BF16)
            cstg = ar.alloc([128, 256], F32)
            akcT = ar.alloc([128, 2, 128], BF16)
            aVEc = ar.alloc([128, 2, 128], BF16)
            mixA = ar.alloc([128, 4, T], BF16)
            B_aq, B_ak, B_aVE, B_cstg, B_akc, B_aVEc, B_mixA = (nb("aq"), nb("ak"), nb("aVE"), nb("cstg"),
                                                               nb("akc"), nb("aVEc"), nb("mixA"))
            dma("pool", wbuf[:, :, 0:512], wsrc(0, 512), [], [B_w], B_w)
            dma("pool", wbuf[:, :, 512:640], wsrc(512, 640), [], [B_w], B_w)
            dma("pool", wbuf[:, :, 640:704], wsrc(576, 640), [], [B_w], B_w)
            dma("pool", wbuf[:, :, 704:768], wsrc(512, 576), [], [B_w], B_w)
            dma("pool", wbuf[:, :, 768:896], wsrc(640, 768), [], [B_w], B_w)
            dma("pool", aVEc[:, :, 0:64], cav_d.rearrange("k (h d) -> k h d", h=2), [], [B_aVEc], B_aVEc)
            memset("pool", aVE[:, :, :, 64:128], 1.0, [B_aVE])
            memset("pool", aVEc[:, :, 64:128], 1.0, [B_aVEc])
            dma("sp", cstg[:, 0:128], cak_d, [], [B_cstg], B_cstg)
            dma("sp", cstg[:, 128:192], cak_d[:, 64:128], [], [B_cstg], B_cstg)
            dma("sp", cstg[:, 192:256], cak_d[:, 0:64], [], [B_cstg], B_cstg)
            for sel in range(2):
                tr(psb[6 + sel][:, 0:128], cstg[:, sel * 128:(sel + 1) * 128], ident[:, :], [B_cstg, B_const], [PB[6 + sel]])
                cp("act", akcT[:, sel, :], psb[6 + sel][:, 0:128], [PB[6 + sel]], [B_akc])
            dma("sp", o_aks[0:64, :], cak_d[64:128, :], [], [], nb("oc"), is_out=True)
            dma("sp", o_avs[0:64, :], cav_d[64:128, :], [], [], nb("oc"), is_out=True)
            dma("sp", o_bks[0:448, :], cbk_d[64:512, :], [], [], nb("oc"), is_out=True)
            dma("sp", o_bvs[0:448, :], cbv_d[64:512, :], [], [], nb("oc"), is_out=True)
            items = []
            for bi in range(5):
                c0, n = BLKS[bi]
                for oc in range(6):
                    def pmm(b, oc=oc, bi=bi, c0=c0, n=n):
                        for c in range(8):
                            mm(psb[b][:, 0:n], wbuf[:, c, oc * 128:(oc + 1) * 128], hn[:, c, c0:c0 + n], c == 0, c == 7,
                               [B_w, B_hn[bi]], [PB[b]])
                    post = None
                    if oc < 4:
                        dests = [(aqT[:, oc, c0:c0 + n], B_aq)]
                        g = vec[:, 48:49]
                    else:
                        dests = [(akT[:, oc - 4, c0:c0 + n], B_ak)]
                        g = vec[:, 49:50]
                        if oc == 4 and bi >= 3:
                            dests.append((k32[0][:, 0:n], B_k32[0]))
                            if bi == 3:
                                post = lambda: k_out(k32[0][:, 384:512], B_k32[0], 128, o_akp[:, :], 6)
                            else:
                                post = lambda: k_out(k32[0][:, 0:64], B_k32[0], 64, o_aks[64:128, :], 6)
                    items.append(CH.item(pmm, (0, 128), n, cmat[:, CM_BD64, :], g, dests, PJ, SSB, post))
                for i in range(NTILE):
                    r0, nr = tile_rows(i)
                    if not (c0 <= r0 < c0 + n):
                        continue

                    def f0(b, i=i, r0=r0, nr=nr, bi=bi):
                        for c in range(8):
                            mm(psb[b][0:nr, 0:128], hn[:, c, r0:r0 + nr], wbuf[:, c, 768:896], c == 0, c == 7,
                               [B_w, B_hn[bi]], [PB[b]])

                    def f1(b, i=i, nr=nr):
                        cp("dve", aVE[0:nr, i, :, 0:64], psb[b][0:nr, 0:128].rearrange("p (k d) -> p k d", k=2),
                           [PB[b]], [B_aVE])
                        if i == 15:
                            OSt.store_rows(b, 128, 128, o_avp[:, :])
                        if i == 16:
                            OSt.store_rows(b, 64, 128, o_avs[64:128, :])
                    items.append(simple_item(f0, f1, PJ))
            drain(run_items(items))
            woA = wbuf[:, :, :].rearrange("p c f -> p (c f)")[:, 0:4096].rearrange("p (k o) -> p k o", k=4)
            dma("pool", woA, evout_d[0:512, :].rearrange("(k p) o -> p k o", p=128), [], [B_w], B_w)
            ACR = [(0, 128), (0, 256), (128, 384), (256, 512), (384, 512)]
            groups = []
            for kvh in range(2):
                for par in range(2):
                    sel = 0 if kvh == par else 1
                    pr = slice(par * 64, par * 64 + 64)
                    slot0 = kvh * 4 + par * 2
                    qgroups = [(i * 512, 512, [(4 * i - 1 + r, ACR[r], None) for r in range(5) if 4 * i - 1 + r >= 0])
                               for i in range(4)]
                    qgroups.append((2048, 64, [("cache", (0, 64), 128), ("new", (0, 64), 0)]))
                    for q0, qn, kbl in qgroups:
                        tiles = []
                        for kb, (ca, cb_), yoff in kbl:
                            ncols = cb_ - ca
                            if kb == "cache":
                                KT, nk, VEs, y0 = akcT[pr, sel, :], 128, aVEc[:, kvh, :], yoff + ca
                                Rk = [B_akc, B_aVEc]
                            elif kb == "new":
                                KT, nk, VEs, y0 = akT[pr, sel, 2048:2112], 64, aVE[0:64, 16, kvh, :], ca
                                Rk = [B_ak, B_aVE]
                            else:
                                KT, nk, VEs = akT[pr, sel, kb * 128:(kb + 1) * 128], 128, aVE[:, kb, kvh, :]
                                r = kb - (4 * (q0 // 512) - 1)
                                y0 = ca - 128 * (r - 1)
                                Rk = [B_ak, B_aVE]
                            tiles.append(dict(
                                KT=KT, qrhs=aqT[pr, kvh * 2:kvh * 2 + 2, q0 + ca:q0 + cb_], nk=nk, W=2 * ncols,
                                Gap=G_A[0:nk, slot0:slot0 + 2, y0:y0 + ncols],
                                view=lambda a: a.rearrange("p (c n) -> p c n", c=2),
                                R=[B_aq] + Rk,
                                pv=[(cc, ca, cb_, VEs, nk, cc * ncols, (cc + 1) * ncols, Rk) for cc in range(2)]))

                        def fin(obs, kvh=kvh, par=par, q0=q0, qn=qn):
                            for cc in range(2):
                                g = 2 * cc + par
                                h = kvh * 4 + g
                                chunk = kvh * 2 + g // 2
                                drow = slice((g % 2) * 64, (g % 2) * 64 + 64)
                                attn_finish(AS, obs[cc], qn, esink[64:128, h:h + 1], mixA[drow, chunk, q0:q0 + qn], B_mixA)
                        groups.append(dict(tiles=tiles, nO=2, finish=fin))
            drain(attention_gen(AS, groups, 0.125, (3, 4, 5), (0, 1, 2, 6, 7, 0, 1, 2, 6, 7)[0:4]))
            out_proj(mixA, 4, woA, B_mixA, B_w)
            P.barrier()

            BCR = [(0, 128), (0, 256), (0, 384), (0, 512), (0, 512), (128, 512), (256, 512), (384, 512)]
            for hb in range(2):
                ar.reset(mS)
                wbuf = ar.alloc([128, 8, 768], BF16)
                B_w = nb("wB")
                bqT = ar.alloc([128, 2, T], BF16)
                bkT = ar.alloc([128, 2, T], BF16)
                bVE = ar.alloc([128, NTILE, 4, 128], BF16)
                cstg = ar.alloc([128, 4, 256], F32)
                bkcT = ar.alloc([128, 2, 512], BF16)
                bVEc = ar.alloc([128, 4, 4, 128], BF16)
                mixB = ar.alloc([128, 2, T], BF16)
                kb32 = ar.alloc([128, 512], F32)
                B_kb32 = nb("kb32")
                B_bq, B_bk, B_bVE, B_cstg, B_bkc, B_bVEc, B_mixB = (nb("bq"), nb("bk"), nb("bVE"), nb("cstgb"),
                                                                   nb("bkc"), nb("bVEc"), nb("mixB"))
                for k3, base in enumerate((768, 1280, 1792)):
                    dma("pool", wbuf[:, :, k3 * 256:(k3 + 1) * 256], wsrc(base + hb * 256, base + hb * 256 + 256),
                        [], [B_w], B_w)
                for t4 in range(4):
                    dma("pool", bVEc[:, t4, :, 0:64],
                        cbv_d[t4 * 128:(t4 + 1) * 128, hb * 256:(hb + 1) * 256].rearrange("p (h d) -> p h d", h=4),
                        [], [B_bVEc], B_bVEc)
                memset("pool", bVE[:, :, :, 64:128], 1.0, [B_bVE])
                memset("pool", bVEc[:, :, :, 64:128], 1.0, [B_bVEc])
                dma("sp", cstg[:], cbk_d[:, hb * 256:(hb + 1) * 256].rearrange("(t p) f -> p t f", p=128),
                    [], [B_cstg], B_cstg)
                for t4 in range(4):
                    for cl in range(2):
                        bk = 6 + (t4 * 2 + cl) % 2
                        tr(psb[bk][:, 0:128], cstg[:, t4, cl * 128:(cl + 1) * 128], ident[:, :], [B_cstg, B_const], [PB[bk]])
                        cp(ev_eng(), bkcT[:, cl, t4 * 128:(t4 + 1) * 128], psb[bk][:, 0:128], [PB[bk]], [B_bkc])
                items = []
                for bi in range(5):
                    c0, n = BLKS[bi]
                    for oc in range(4):
                        def pmm(b, oc=oc, bi=bi, c0=c0, n=n):
                            for c in range(8):
                                mm(psb[b][:, 0:n], wbuf[:, c, oc * 128:(oc + 1) * 128], hn[:, c, c0:c0 + n], c == 0, c == 7,
                                   [B_w, B_hn[bi]], [PB[b]])
                        post = None
                        if oc < 2:
                            dests = [(bqT[:, oc, c0:c0 + n], B_bq)]
                            g = vec[:, 50:51]
                        else:
                            dests = [(bkT[:, oc - 2, c0:c0 + n], B_bk)]
                            g = vec[:, 51:52]
                            if bi >= 3:
                                kk = k32[0] if oc == 2 else kb32
                                Bkk = B_k32[0] if oc == 2 else B_kb32
                                dests.append((kk[:, 0:n], Bkk))
                                fcol = hb * 256 + (oc - 2) * 128
                                if bi == 3:
                                    def post(kk=kk, Bkk=Bkk, fcol=fcol):
                                        for t4 in range(4):
                                            k_out(kk[:, t4 * 128:(t4 + 1) * 128], Bkk, 128,
                                                  o_bkp[t4 * 128:(t4 + 1) * 128, fcol:fcol + 128], 6 + t4 % 2)
                                else:
                                    def post(kk=kk, Bkk=Bkk, fcol=fcol):
                                        k_out(kk[:, 0:64], Bkk, 64, o_bks[448:512, fcol:fcol + 128], 6)
                        items.append(CH.item(pmm, (0, 128), n, cmat[:, CM_BD64, :], g, dests, PJ, SSB, post))
                    for i in range(NTILE):
                        r0, nr = tile_rows(i)
                        if not (c0 <= r0 < c0 + n):
                            continue

                        def f0(b, i=i, r0=r0, nr=nr, bi=bi):
                            for c in range(8):
                                mm(psb[b][0:nr, 0:256], hn[:, c, r0:r0 + nr], wbuf[:, c, 512:768], c == 0, c == 7,
                                   [B_w, B_hn[bi]], [PB[b]])

                        def f1(b, i=i, nr=nr, hb=hb):
                            cp(ev_eng(), bVE[0:nr, i, :, 0:64], psb[b][0:nr, 0:256].rearrange("p (k d) -> p k d", k=4),
                               [PB[b]], [B_bVE])
                            if 12 <= i < 16:
                                OSt.store_rows(b, 128, 256, o_bvp[(i - 12) * 128:(i - 11) * 128, hb * 256:(hb + 1) * 256])
                            if i == 16:
                                OSt.store_rows(b, 64, 256, o_bvs[448:512, hb * 256:(hb + 1) * 256])
                        items.append(simple_item(f0, f1, PJ))
                drain(run_items(items))
                woB = wbuf[:, :, :].rearrange("p c f -> p (c f)")[:, 0:2048].rearrange("p (k o) -> p k o", k=2)
                dma("pool", woB, evout_d[512 + hb * 256:512 + (hb + 1) * 256, :].rearrange("(k p) o -> p k o", p=128),
                    [], [B_w], B_w)
                groups = []
                for hl in range(4):
                    h = hb * 4 + hl
                    cl = hl // 2
                    pr = slice((hl % 2) * 64, (hl % 2) * 64 + 64)
                    qgroups = [(i * 512, 512, [(4 * i - 4 + r, BCR[r], 512 - 128 * r) for r in range(8) if 4 * i - 4 + r >= 0])
                               for i in range(4)]
                    qgroups.append((2048, 64, [("c%d" % m, (0, 64), 512 - 128 * m) for m in range(4)] + [("new", (0, 64), 0)]))
                    for q0, qn, kbl in qgroups:
                        tiles = []
                        for kb, (ca, cb_), yoff in kbl:
                            ncols = cb_ - ca
                            y0 = yoff + ca
                            if isinstance(kb, str) and kb[0] == "c":
                                m = int(kb[1:])
                                KT, nk, VEs = bkcT[pr, cl, m * 128:(m + 1) * 128], 128, bVEc[:, m, hl, :]
                                Rk = [B_bkc, B_bVEc]
                            elif kb == "new":
                                KT, nk, VEs = bkT[pr, cl, 2048:2112], 64, bVE[0:64, 16, hl, :]
                                Rk = [B_bk, B_bVE]
                            else:
                                KT, nk, VEs = bkT[pr, cl, kb * 128:(kb + 1) * 128], 128, bVE[:, kb, hl, :]
                                Rk = [B_bk, B_bVE]
                            tiles.append(dict(KT=KT, qrhs=bqT[pr, cl, q0 + ca:q0 + cb_], nk=nk, W=ncols,
                                              Gap=G_B[0:nk, h, y0:y0 + ncols], R=[B_bq] + Rk,
                                              pv=[(0, ca, cb_, VEs, nk, 0, ncols, Rk)]))

                        def fin(obs, pr=pr, cl=cl, q0=q0, qn=qn):
                            attn_finish(AS, obs[0], qn, None, mixB[pr, cl, q0:q0 + qn], B_mixB)
                        groups.append(dict(tiles=tiles, nO=1, finish=fin))
                drain(attention_gen(AS, groups, 0.125, (3, 4, 5), (0, 1, 2, 6, 7)))
                out_proj(mixB, 2, woB, B_mixB, B_w)
                P.barrier()

        def mix_odd():
            ar.reset()
            CH = Chain()
            AS = AttnScratch(need_pexp=False)
            OSt = OutStage(128)
            mS = ar.mark()
            osrc = lambda a, b: odin_d[:, a:b].rearrange("(c p) f -> p c f", p=128)
            mixC = ar.alloc([128, 4, T], BF16)
            woC = ar.alloc([128, 4, 1024], BF16)
            wC = [ar.alloc([128, 8, 384], BF16) for _ in range(2)]
            ub = [ar.alloc([128, 516], F32) for _ in range(3)]
            ccs = [ar.alloc([128, 512], F32) for _ in range(2)]
            t1 = [ar.alloc([128, 512], F32) for _ in range(2)]
            cvs = ar.alloc([128, 4, 2], F32)
            B_mixC, B_woC, B_cvs = nb("mixC"), nb("woC"), nb("cvs")
            B_wC = [nb("wC") for _ in range(2)]
            B_ub = [nb("ub") for _ in range(3)]
            B_ccs = [nb("ccs") for _ in range(2)]
            B_t1 = [nb("t1") for _ in range(2)]

            def load_wC(c):
                s = c % 2
                for k3 in range(3):
                    dma("pool", wC[s][:, :, k3 * 128:(k3 + 1) * 128], osrc(k3 * 512 + c * 128, k3 * 512 + (c + 1) * 128),
                        [], [B_wC[s]], B_wC[s])
            load_wC(0)
            load_wC(1)
            dma("pool", woC[:], odout_d[0:512, :].rearrange("(k p) o -> p k o", p=128), [], [B_woC], B_woC)
            for c in range(4):
                dma_nc("sp", cvs[:, c, :], ccv_d[:, c * 128:(c + 1) * 128].rearrange("j p -> p j"), [], [B_cvs], B_cvs)
            drain(norm_all_gen(32, (2, 3, 4)))
            items = []
            cnt = 0
            for c in range(4):
                for bi in range(5):
                    c0, n = BLKS[bi]
                    us = cnt % 3
                    ups = (cnt - 1) % 3
                    es_ = cnt % 2
                    cnt += 1

                    def f0(bset, c=c, bi=bi, c0=c0, n=n):
                        s = c % 2
                        for k3 in range(3):
                            for kc in range(8):
                                mm(psb[bset[k3]][:, 0:n], wC[s][:, kc, k3 * 128:(k3 + 1) * 128], hn[:, kc, c0:c0 + n],
                                   kc == 0, kc == 7, [B_wC[s], B_hn[bi]], [PB[bset[k3]]])
                        if bi == 4 and c + 2 < 4:
                            load_wC(c + 2)

                    def f1(bset, c=c, bi=bi, c0=c0, n=n, us=us, ups=ups, es_=es_):
                        u = ub[us]
                        if bi == 0:
                            memset("dve", u[:, 0:2], 0.0, [B_ub[us]])
                        elif bi == 4:
                            cp("dve", u[:, 0:2], cvs[:, c, :], [B_cvs], [B_ub[us]])
                        else:
                            cp("dve", u[:, 0:2], ub[ups][:, 512:514], [B_ub[ups]], [B_ub[us]])
                        cp("act", ccs[es_][:, 0:n], psb[bset[1]][:, 0:n], [PB[bset[1]]], [B_ccs[es_]])
                        tt("dve", u[:, 2:2 + n], psb[bset[2]][:, 0:n], ccs[es_][:, 0:n], ALU.mult,
                           [PB[bset[2]], B_ccs[es_]], [B_ub[us]])
                        ts("dve", t1[es_][:, 0:n], u[:, 2:2 + n], vec[:, 58 + 8 + c:58 + 8 + c + 1], None, ALU.mult, None,
                           [B_ub[us], B_const], [B_t1[es_]])
                        stt("dve", t1[es_][:, 0:n], u[:, 1:1 + n], vec[:, 58 + 4 + c:58 + 4 + c + 1], t1[es_][:, 0:n],
                            ALU.mult, ALU.add, [B_ub[us], B_const, B_t1[es_]], [B_t1[es_]])
                        stt("dve", t1[es_][:, 0:n], u[:, 0:n], vec[:, 58 + c:58 + c + 1], t1[es_][:, 0:n],
                            ALU.mult, ALU.add, [B_ub[us], B_const, B_t1[es_]], [B_t1[es_]])
                        tt("dve", mixC[:, c, c0:c0 + n], psb[bset[0]][:, 0:n], t1[es_][:, 0:n], ALU.mult,
                           [PB[bset[0]], B_t1[es_]], [B_mixC])
                        if bi == 3:
                            dma_nc("sp", o_cp[:, c * 128:(c + 1) * 128].rearrange("j p -> p j"), u[:, 512:514],
                                   [B_ub[us]], [], nb("ocp"), is_out=True)
                        if bi == 4:
                            dma_nc("sp", o_cs[:, c * 128:(c + 1) * 128].rearrange("j p -> p j"), u[:, 64:66],
                                   [B_ub[us]], [], nb("ocs"), is_out=True)
                    items.append((lambda idx, f0=f0: f0(((0, 1, 2), (3, 4, 5))[idx % 2]),
                                  lambda idx, f1=f1: f1(((0, 1, 2), (3, 4, 5))[idx % 2])))
            drain(run_items(items))
            out_proj(mixC, 4, woC, B_mixC, B_woC, banks=(6, 7))
            P.barrier()
            ar.reset(mS)
            wkvb = ar.alloc([128, 1024], BF16)
            QsT = ar.alloc([128, 8, 64], BF16)
            KnT = ar.alloc([128, 8, 64], BF16)
            VE16 = ar.alloc([128, 8, 128], BF16)
            mKeep2 = ar.mark()
            ckvT = ar.alloc([128, T], BF16)
            kpeT = ar.alloc([128, T], BF16)
            VE = ar.alloc([128, 16, 8, 128], BF16)
            mKeep = ar.mark()
            qan = ar.alloc([128, 2, T], BF16)
            tabq = ar.alloc([128, T], F32)
            wD = ar.alloc([128, 8, 448], BF16)
            wqb = ar.alloc([128, 2, 8, 128], BF16)
            zk = [ar.alloc([128, 512], F32) for _ in range(2)]
            zk2 = ar.alloc([128, 512], F32)
            c32 = ar.alloc([128, 512], F32)
            zb = [ar.alloc([128, 512], BF16) for _ in range(2)]
            sq2 = ar.alloc([128, 512], BF16)
            (B_ckvT, B_kpeT, B_VE, B_wkvb, B_QsT, B_KnT, B_qan, B_tabq, B_wD, B_wqb, B_zk2, B_c32, B_sq2, B_VE16) = \
                [nb(x) for x in ("ckvT", "kpeT", "VE", "wkvb", "QsT", "KnT", "qan", "tabq", "wD", "wqb", "zk2",
                                 "c32", "sq2", "VE16")]
            B_zk = [nb("zk") for _ in range(2)]
            B_zb = [nb("zb") for _ in range(2)]
            dma("pool", wD[:, :, 0:416], osrc(1536, 1952), [], [B_wD], B_wD)
            dma("pool", wD[:, :, 416:432], osrc(1936, 1952), [], [B_wD], B_wD)
            dma("pool", wD[:, :, 432:448], osrc(1920, 1936), [], [B_wD], B_wD)
            dma("pool", wkvb[:], wkvb_d, [], [B_wkvb], B_wkvb)
            for kc in range(2):
                dma("pool", wqb[:, kc, :, :], wqb_d[kc * 128:(kc + 1) * 128, :, :], [], [B_wqb], B_wqb)
            dma("sp", tabq[:], tabq_d, [], [B_tabq], B_tabq)
            memset("pool", VE[:, :, :, 64:128], 1.0, [B_VE])
            memset("pool", VE16[:, :, 64:128], 1.0, [B_VE16])
            wkv_v = wkvb[:, 512:1024]

            def VEt(i):
                return VE[:, i] if i < 16 else VE16

            for bi in range(5):
                c0, n = BLKS[bi]
                for kc2 in range(2):
                    for kc in range(8):
                        mm(psb[kc2][:, 0:n], wD[:, kc, kc2 * 128:(kc2 + 1) * 128], hn[:, kc, c0:c0 + n], kc == 0, kc == 7,
                           [B_wD, B_hn[bi]], [PB[kc2]])
                for kc in range(8):
                    mm(psb[3][:, 0:n], wD[:, kc, 256:384], hn[:, kc, c0:c0 + n], kc == 0, kc == 7, [B_wD, B_hn[bi]], [PB[3]])
                for kc in range(8):
                    mm(psb[4][64:128, 0:n], wD[:, kc, 384:448], hn[:, kc, c0:c0 + n], kc == 0, kc == 7, [B_wD, B_hn[bi]], [PB[4]])
                act(CH.sqc[0][:, 0:n], psb[0][:, 0:n], AF.Square, [PB[0]], [CH.B_sqc[0]])
                act(sq2[:, 0:n], psb[1][:, 0:n], AF.Square, [PB[1]], [B_sq2])
                act(CH.sqc[1][:, 0:n], psb[3][:, 0:n], AF.Square, [PB[3]], [CH.B_sqc[1]])
                act(CH.sqc[2][64:96, 0:n], psb[4][64:96, 0:n], AF.Square, [PB[4]], [CH.B_sqc[2]])
                mm(psb[2][:, 0:n], cmat[:, CM_ONES256, :], CH.sqc[0][:, 0:n], True, False, [CH.B_sqc[0], B_const], [PB[2]])
                mm(psb[2][:, 0:n], cmat[:, CM_ONES256, :], sq2[:, 0:n], False, True, [B_sq2, B_const], [PB[2]])
                mm(psb[6][:, 0:n], cmat[:, CM_ONES128, :], CH.sqc[1][:, 0:n], True, True, [CH.B_sqc[1], B_const], [PB[6]])
                mm(psb[5][64:96, 0:n], cmat[64:96, CM_BD3, 64:96], CH.sqc[2][64:96, 0:n], True, True, [CH.B_sqc[2], B_const], [PB[5]])
                rstd_from(CH.rsc[0][:, 0:n], psb[2][:, 0:n], [PB[2]], CH.B_rsc[0])
                for kc2 in range(2):
                    stt("dve", qan[:, kc2, c0:c0 + n], psb[kc2][:, 0:n], vec[:, 52 + kc2:53 + kc2], CH.rsc[0][:, 0:n],
                        ALU.mult, ALU.mult, [PB[kc2], CH.B_rsc[0], B_const], [B_qan])
                rstd_from(CH.rsc[1][:, 0:n], psb[6][:, 0:n], [PB[6]], CH.B_rsc[1])
                stt("dve", c32[:, 0:n], psb[3][:, 0:n], vec[:, 54:55], CH.rsc[1][:, 0:n], ALU.mult, ALU.mult,
                    [PB[3], CH.B_rsc[1], B_const], [B_c32])
                cp("pool", ckvT[:, c0:c0 + n], c32[:, 0:n], [B_c32], [B_ckvT])
                rstd_from(zk2[64:96, 0:n], psb[5][64:96, 0:n], [PB[5]], B_zk2)
                stt("dve", zk[0][64:128, 0:n], psb[4][64:128, 0:n], vec[64:128, 56:57], tabq[64:128, c0:c0 + n],
                    ALU.mult, ALU.mult, [PB[4], B_const, B_tabq], [B_zk[0]])
                cp("act", zk[1][64:96, 0:n], zk[0][96:128, 0:n], [B_zk[0]], [B_zk[1]])
                tt("dve", zk[1][64:96, 0:n], zk[0][64:96, 0:n], zk[1][64:96, 0:n], ALU.add, [B_zk[0], B_zk[1]], [B_zk[1]])
                tt("dve", zk[0][64:96, 0:n], zk[1][64:96, 0:n], zk2[64:96, 0:n], ALU.mult, [B_zk[1], B_zk2], [B_zk[0]])
                cp("pool", kpeT[64:96, c0:c0 + n], zk[0][64:96, 0:n], [B_zk[0]], [B_kpeT])
                for i in range(NTILE):
                    r0, nr = tile_rows(i)
                    if not (c0 <= r0 < c0 + n):
                        continue
                    bk = 6 + i % 2
                    tr(psb[bk][0:nr, 0:128], c32[:, r0 - c0:r0 - c0 + nr], ident[:, :], [B_c32, B_const], [PB[bk]])
                    tr(psb[bk][0:nr, 128:160], zk[0][64:96, r0 - c0:r0 - c0 + nr], ident[64:96, 64:96], [B_zk[0], B_const], [PB[bk]])
                    s = OSt.cnt % 2
                    OSt.cnt += 1
                    cp(ev_eng(), OSt.ostg[s][0:nr, 0:128], psb[bk][0:nr, 0:128], [PB[bk]], [OSt.B[s]])
                    dma("sp", o_ckvp[r0:r0 + nr, :] if i < 16 else o_ckvs[:, :], OSt.ostg[s][0:nr, 0:128], [OSt.B[s]], [],
                        OSt.B[s], is_out=True)
                    s = OSt.cnt % 2
                    OSt.cnt += 1
                    cp(ev_eng(), OSt.ostg[s][0:nr, 0:32], psb[bk][0:nr, 128:160], [PB[bk]], [OSt.B[s]])
                    dma("sp", o_kpep[r0:r0 + nr, :] if i < 16 else o_kpes[:, :], OSt.ostg[s][0:nr, 0:32], [OSt.B[s]], [],
                        OSt.B[s], is_out=True)
                for i in range(NTILE):
                    r0, nr = tile_rows(i)
                    if not (c0 <= r0 < c0 + n):
                        continue
                    bk = i % 2
                    mm(psb[bk][0:nr, 0:512], ckvT[:, r0:r0 + nr], wkv_v, True, True, [B_ckvT, B_wkvb], [PB[bk]])
                    cp("dve", VEt(i)[0:nr, :, 0:64], psb[bk][0:nr, 0:512].rearrange("p (h d) -> p h d", h=8),
                       [PB[bk]], [B_VE if i < 16 else B_VE16])
            P.barrier()
            mixD = hn
            B_mixD = nb("mixD")
            B_Qh = [nb("Qh") for _ in range(2)]
            B_Kh = [nb("Kh") for _ in range(2)]
            sc_d = float(96.0 ** -0.5)
            PJ, SSB, AMB = (0, 1), (2,), 3

            def head_chain_gen(h):
                s = h % 2
                QhT = hn[:, 4 + s, :]
                KhT = hn[:, 6 + s, :]
                items = []
                for bi in range(5):
                    c0, n = BLKS[bi]

                    def q0f(idx, c0=c0, n=n):
                        b = PJ[idx % 2]
                        for kc in range(2):
                            mm(psb[b][:, 0:n], wqb[:, kc, h, :], qan[:, kc, c0:c0 + n], kc == 0, kc == 1, [B_wqb, B_qan], [PB[b]])
                        act(CH.sqc[idx % 3][:, 0:n], psb[b][:, 0:n], AF.Square, [PB[b]], [CH.B_sqc[idx % 3]])

                    def q1f(idx, c0=c0, n=n):
                        b = PJ[idx % 2]
                        sl, rl, zl = idx % 3, idx % 2, (idx // 2) % 2
                        mm(psb[SSB[0]][:, 0:n], cmat[:, CM_BD3, :], CH.sqc[sl][:, 0:n], True, True, [CH.B_sqc[sl], B_const], [PB[SSB[0]]])
                        rstd_from(CH.rsc[rl][:, 0:n], psb[SSB[0]][:, 0:n], [PB[SSB[0]]], CH.B_rsc[rl])
                        stt("dve", zk[zl][:, 0:n], psb[b][:, 0:n], vec[:, 55:56], tabq[:, c0:c0 + n], ALU.mult, ALU.mult,
                            [PB[b], B_const, B_tabq], [B_zk[zl]])
                        tt("dve", zb[zl][:, 0:n], zk[zl][:, 0:n], CH.rsc[rl][:, 0:n], ALU.mult, [B_zk[zl], CH.B_rsc[rl]], [B_zb[zl]])

                    def q2f(idx, bi=bi, c0=c0, n=n):
                        zl = (idx // 2) % 2
                        mm(psb[AMB][0:96, 0:n], cmat[:, CM_AMAT, 0:96], zb[zl][:, 0:n], True, True, [B_zb[zl], B_const], [PB[AMB]])
                        if bi < 4:
                            cp("dve", QhT[0:96, c0:c0 + n], psb[AMB][0:96, 0:n], [PB[AMB]], [B_Qh[s]])
                        else:
                            cp("dve", QsT[0:96, h, :], psb[AMB][0:96, 0:n], [PB[AMB]], [B_QsT])
                    items.append((q0f, q1f, q2f))

                    def kpm(b, c0=c0, n=n):
                        mm(psb[b][0:64, 0:n], wkvb[:, h * 64:(h + 1) * 64], ckvT[:, c0:c0 + n], True, True, [B_wkvb, B_ckvT], [PB[b]])
                    if bi < 4:
                        dests = [(KhT[0:64, c0:c0 + n], B_Kh[s])]
                        post = lambda c0=c0, n=n: cp("pool", KhT[64:96, c0:c0 + n], kpeT[64:96, c0:c0 + n], [B_kpeT], [B_Kh[s]])
                    else:
                        dests = [(KnT[0:64, h, :], B_KnT)]
                        post = lambda c0=c0, n=n: cp("pool", KnT[64:96, h, :], kpeT[64:96, c0:c0 + n], [B_kpeT], [B_KnT])
                    k0f, k1f = CH.item(kpm, (0, 64), n, cmat[0:64, CM_BD64, 0:64], vec[0:64, 57:58], dests, PJ, SSB, post)
                    items.append((k0f, k1f, lambda idx: None))
                return sw_pipeline(items, [lambda it, i: it[0](i), lambda it, i: it[1](i), lambda it, i: it[2](i)], 1)

            def head_attn_gen(h):
                s = h % 2
                QhT = hn[:, 4 + s, :]
                KhT = hn[:, 6 + s, :]
                groups = []
                for i in range(4):
                    q0 = 512 * i
                    tiles = []
                    for kb in range(4 * i + 4):
                        ca = max(0, 128 * kb - 512 * i)
                        ncols = 512 - ca
                        diag = kb >= 4 * i
                        if diag:
                            pv = [(0, ca, ca + 64, VE[0:64, kb, h, :], 64, 0, 64, [B_VE])]
                            if ncols > 64:
                                pv.append((0, ca + 64, 512, VE[:, kb, h, :], 128, 64, ncols, [B_VE]))
                        else:
                            pv = [(0, ca, 512, VE[:, kb, h, :], 128, 0, ncols, [B_VE])]
                        tiles.append(dict(KT=KhT[0:96, kb * 128:(kb + 1) * 128], qrhs=QhT[0:96, q0 + ca:q0 + 512], nk=128,
                                          W=ncols, Gap=None, R=[B_Kh[s], B_Qh[s]], pv=pv))

                    def fin(obs, q0=q0):
                        drow = slice((h % 2) * 64, (h % 2) * 64 + 64)
                        attn_finish(AS, obs[0], 512, None, mixD[drow, h // 2, q0:q0 + 512], B_mixD, use_dve=True)
                    groups.append(dict(tiles=tiles, nO=1, finish=fin))
                return attention_gen(AS, groups, sc_d, (4, 5), (6, 7))

            drain(head_chain_gen(0))
            for h in range(8):
                drain(head_attn_gen(h), head_chain_gen(h + 1) if h < 7 else None)
            P.barrier()
            ar.reset(mKeep2)
            woD = ar.alloc([128, 4, 1024], BF16)
            B_woD = nb("woD")
            dma("pool", woD[:], odout_d[512:1024, :].rearrange("(k p) o -> p k o", p=128), [], [B_woD], B_woD)
            NS = 3
            cst = [ar.alloc([128, 4, 128], F32) for _ in range(NS)]
            kst = [ar.alloc([128, 4, 32], F32) for _ in range(NS)]
            ckc = [ar.alloc([128, 512], BF16) for _ in range(NS)]
            kpc = [ar.alloc([128, 512], BF16) for _ in range(NS)]
            KcT = [ar.alloc([128, 8, 512], BF16) for _ in range(NS)]
            VEc = [ar.alloc([128, 4, 8, 128], BF16) for _ in range(NS)]
            B_cst = [nb("cst") for _ in range(NS)]
            B_kst = [nb("kst") for _ in range(NS)]
            B_ckc = [nb("ckc") for _ in range(NS)]
            B_kpc = [nb("kpc") for _ in range(NS)]
            B_KcT = [nb("KcT") for _ in range(NS)]
            B_VEc = [nb("VEc") for _ in range(NS)]
            for s in range(NS):
                memset("pool", VEc[s][:, :, :, 64:128], 1.0, [B_VEc[s]])
            OS = 7
            PJs, SSBs = (0, 1, 2), (3,)

            def load_grp(g):
                s = g % NS
                dma("sp", cst[s][:], ckv_d[g * 512:(g + 1) * 512, :].rearrange("(t p) l -> p t l", p=128), [], [B_cst[s]], B_cst[s])
                dma("sp", kst[s][:], ckp_d[g * 512:(g + 1) * 512, :].rearrange("(t p) l -> p t l", p=128), [], [B_kst[s]], B_kst[s])

            def prep_gen(g):
                s = g % NS
                if g + 2 < 8:
                    load_grp(g + 2)
                for t4 in range(4):
                    tr(psb[0][:, t4 * 128:(t4 + 1) * 128], cst[s][:, t4, :], ident[:, :], [B_cst[s], B_const], [PB[0]])
                cp("act", ckc[s][:, :], psb[0][:, :], [PB[0]], [B_ckc[s]])
                for t4 in range(4):
                    tr(psb[1][0:32, t4 * 128:(t4 + 1) * 128], kst[s][:, t4, :], ident[:, :], [B_kst[s], B_const], [PB[1]])
                cp("dve", kpc[s][64:96, :], psb[1][0:32, :], [PB[1]], [B_kpc[s]])
                yield
                items = []
                for t4 in range(4):
                    def f0(b, t4=t4):
                        mm(psb[b][:, 0:512], ckc[s][:, t4 * 128:(t4 + 1) * 128], wkv_v, True, True, [B_ckc[s], B_wkvb], [PB[b]])

                    def f1(b, t4=t4):
                        cp("dve", VEc[s][:, t4, :, 0:64], psb[b][:, 0:512].rearrange("p (h d) -> p h d", h=8), [PB[b]], [B_VEc[s]])
                    items.append(simple_item(f0, f1, PJs))
                for hp in range(4):
                    def kpm(b, hp=hp):
                        mm(psb[b][:, 0:512], wkvb[:, hp * 128:(hp + 1) * 128], ckc[s][:, :], True, True, [B_wkvb, B_ckc[s]], [PB[b]])
                    dests = [(KcT[s][0:64, 2 * hp, :], B_KcT[s], (0, 64), vec[0:64, 57:58]),
                             (KcT[s][0:64, 2 * hp + 1, :], B_KcT[s], (64, 128), vec[64:128, 57:58])]

                    items.append(CH.item(kpm, (0, 128), 512, cmat[:, CM_BD64, :], vec[:, 57:58], dests, PJs, SSBs, None))
                for _ in run_items(items):
                    yield

            started = [False]

            def tiles_gen(g):
                s = g % NS
                items = list(range(4))

                def s0(t4, idx):
                    sl4 = (4 * g + t4) % 4
                    sbk = (4, 5)[(4 * g + t4) % 2]
                    mm(psb[sbk][:, 0:512].rearrange("p (h q) -> p h q", h=8), kpc[s][64:96, t4 * 128:(t4 + 1) * 128],
                       QsT[64:96, :, :], True, False, [B_kpc[s], B_QsT], [PB[sbk]])
                    for h in range(8):
                        mm(psb[sbk][:, h * 64:(h + 1) * 64], KcT[s][0:64, h, t4 * 128:(t4 + 1) * 128], QsT[0:64, h, :], False, False,
                           [B_KcT[s], B_QsT], [PB[sbk]])
                    act(AS.pb[sl4][:, 0:512], psb[sbk][:, 0:512], AF.Exp, [PB[sbk]], [AS.B_pb[sl4]], scale=sc_d)

                def s1(t4, idx):
                    sl4 = (4 * g + t4) % 4
                    for h in range(8):
                        mm(psb[OS][:, h * 64:(h + 1) * 64], VEc[s][:, t4, h, :], AS.pb[sl4][:, h * 64:(h + 1) * 64],
                           not started[0], False, [AS.B_pb[sl4], B_VEc[s]], [PB[OS]])
                        started[0] = True
                return sw_pipeline(items, [s0, s1], 1)

            load_grp(0)
            load_grp(1)
            drain(prep_gen(0))
            drain(prep_gen(1))
            for g in range(8):
                drain(tiles_gen(g), prep_gen(g + 2) if g + 2 < 8 else None)
            sbk = 4
            for h in range(8):
                mm(psb[sbk][0:64, h * 64:(h + 1) * 64], KnT[0:96, h, :], QsT[0:96, h, :], True, True, [B_KnT, B_QsT], [PB[sbk]])
            act(AS.pb[0][0:64, 0:512], psb[sbk][0:64, 0:512], AF.Exp, [PB[sbk]], [AS.B_pb[0]], scale=sc_d)
            for h in range(8):
                mm(psb[OS][:, h * 64:(h + 1) * 64], VE16[0:64, h, :], AS.pb[0][0:64, h * 64:(h + 1) * 64], False, False,
                   [AS.B_pb[0], B_VE16], [PB[OS]])
            act(AS.rcp[64:128, 0:512], psb[OS][64:128, 0:512], AF.Ln, [PB[OS]], [AS.B_rcp])
            act(AS.rcp2[0:64, 0:512], AS.rcp[64:128, 0:512], AF.Exp, [AS.B_rcp], [AS.B_rcp2], scale=-1.0)
            for h in range(8):
                drow = slice((h % 2) * 64, (h % 2) * 64 + 64)
                tt("dve", mixD[drow, h // 2, 2048:2112], psb[OS][0:64, h * 64:(h + 1) * 64], AS.rcp2[0:64, h * 64:(h + 1) * 64],
                   ALU.mult, [PB[OS], AS.B_rcp2], [B_mixD])
            out_proj(mixD, 4, woD, B_mixD, B_woD)
            P.barrier()

        stages = ["ffn1_0", "mix_0", "ffn2_0", "ffn1_1", "mix_1", "ffn2_1"]
        for st in stages:
            if stop is not None and st == stop:
                break
            if st == "ffn1_0":
                ffn(0, 0, next_gcol=8)
            elif st == "ffn2_0":
                ffn(0, 1, next_gcol=24, end_barrier=False)
            elif st == "ffn1_1":
                ffn(1, 0, skip_norm01=True, next_gcol=32)
            elif st == "ffn2_1":
                ffn(1, 1, end_barrier=False)
            elif st == "mix_0":
                mix_even()
            elif st == "mix_1":
                mix_odd()
        final_phase()
        P.finalize_and_emit()
    return nc


OUT_NAMES = ["o_yp", "o_ys", "o_akp", "o_avp", "o_bkp", "o_bvp", "o_cp", "o_ckvp", "o_kpep",
             "o_aks", "o_avs", "o_bks", "o_bvs", "o_cs", "o_ckvs", "o_kpes"]


def make_in_maps(inputs, cores):
    f = lambda a: np.ascontiguousarray(np.asarray(a, dtype=np.float32))
    consts = host_consts()
    vec = pack_vec(inputs)
    wqb = f(inputs["d_w_q_b"][0]).reshape(256, 8, 96)
    nope, rope = wqb[:, :, 0:64], wqb[:, :, 64:96]
    ropesw = np.concatenate([rope[:, :, 16:32], rope[:, :, 0:16]], axis=2)
    wqb_p = np.ascontiguousarray(np.concatenate([nope, rope, ropesw], axis=2))
    wkv = f(inputs["d_w_kv_b"][0]).reshape(128, 8, 128)
    wkvb_p = np.ascontiguousarray(np.concatenate([wkv[:, :, 0:64].reshape(128, 512), wkv[:, :, 64:128].reshape(128, 512)], axis=1))
    shared = {
        "ff1_w_gu": f(inputs["ff1_w_gu"]), "ff2_w_gu": f(inputs["ff2_w_gu"]),
        "ff1_w_down": f(inputs["ff1_w_down"]), "ff2_w_down": f(inputs["ff2_w_down"]),
        "ev_w_in": f(inputs["ev_w_in"][0]), "ev_w_out": f(inputs["ev_w_out"][0]),
        "od_w_in": f(inputs["od_w_in"][0]), "od_w_out": f(inputs["od_w_out"][0]),
        "wqb": wqb_p, "wkvb": wkvb_p, "vec": vec,
        "t5": f(inputs["t5_bias_table"]), "brel": f(inputs["b_rel_bias"][0]),
        "sinks": f(inputs["a_sinks"]).reshape(1, 8),
    }
    shared.update(consts)
    maps = []
    for b in cores:
        m = dict(shared)
        m["xp"] = f(inputs["x_prompt"][b])
        m["xs"] = f(inputs["x_sample"][b])
        m["cak"] = f(inputs["cache_a_k"][0, b]).reshape(128, 128)
        m["cav"] = f(inputs["cache_a_v"][0, b]).reshape(128, 128)
        m["cbk"] = f(inputs["cache_b_k"][0, b]).reshape(512, 512)
        m["cbv"] = f(inputs["cache_b_v"][0, b]).reshape(512, 512)
        m["ccv"] = f(inputs["state_c_conv"][0, b])
        m["ckv"] = f(inputs["cache_d_ckv"][0, b])
        m["ckp"] = f(inputs["cache_d_kpe"][0, b])
        maps.append(m)
    return maps


def assemble(results):
    def st(name, shape):
        return np.stack([np.asarray(r[name], np.float32).reshape(shape) for r in results])[None] \
            if name not in ("o_yp", "o_ys") else np.stack([np.asarray(r[name], np.float32) for r in results])

    return (
        st("o_yp", None), st("o_ys", None),
        st("o_akp", (128, 2, 64)), st("o_avp", (128, 2, 64)),
        st("o_bkp", (512, 8, 64)), st("o_bvp", (512, 8, 64)),
        st("o_cp", (2, 512)), st("o_ckvp", (2048, 128)), st("o_kpep", (2048, 32)),
        st("o_aks", (128, 2, 64)), st("o_avs", (128, 2, 64)),
        st("o_bks", (512, 8, 64)), st("o_bvs", (512, 8, 64)),
        st("o_cs", (2, 512)), st("o_ckvs", (64, 128)), st("o_kpes", (64, 32)),
    )


def kernel(**inputs):
    nc = build(stop=os.environ.get("MK_STOP"))
    maps = make_in_maps(inputs, list(range(8)))
    res = run_bass_kernel_spmd(nc, maps, core_ids=list(range(8)))
    return assemble(res.results)
```

```python
import contextlib
import os
import numpy as np
import ml_dtypes
import concourse.bass as bass
import concourse.mybir as mybir
from concourse.bass_utils import run_bass_kernel_spmd

F32 = mybir.dt.float32
BF16 = mybir.dt.bfloat16
ALU = mybir.AluOpType
AF = mybir.ActivationFunctionType

EPS = 1e-6
T = 2112
BLKS = [(0, 512), (512, 512), (1024, 512), (1536, 512), (2048, 64)]
NTILE = 17


def tile_rows(i):
    return (i * 128, 128) if i < 16 else (2048, 64)


class Buf:
    __slots__ = ("name", "writers", "readers", "sem", "semcount", "psum")

    def __init__(self, name, psum=False):
        self.name = name
        self.writers = []
        self.readers = []
        self.sem = None
        self.semcount = 0
        self.psum = psum


class Op:
    __slots__ = ("eng", "fn", "deps", "is_dma", "sem", "val", "signals", "idx")

    def __init__(self, eng, fn):
        self.eng = eng
        self.fn = fn
        self.deps = []
        self.is_dma = False
        self.sem = None
        self.val = 0
        self.signals = False


ENGS = ("pe", "act", "dve", "pool", "sp")
STRICT_SAME_ENGINE = False


class Prog:
    def __init__(self, nc):
        self.nc = nc
        self.ops = []
        self.out_dmas = []
        self.last = {}
        self.pending_dma = []

    def add(self, eng, fn, reads=(), writes=(), dma_buf=None, is_out=False):
        op = Op(eng, fn)
        op.idx = len(self.ops)
        deps = []
        for b in reads:
            deps.extend((d, 0) for d in b.writers)
            if b.psum:
                deps.extend((d, 2) for d in b.readers if d.eng != eng)
        for b in writes:
            deps.extend((d, 1) for d in b.writers)
            deps.extend((d, 1) for d in b.readers)
        seen = set()
        for d, kind in deps:
            if d is op or id(d) in seen:
                continue
            if d.eng == eng and not d.is_dma:
                if eng == "pe" or (kind != 0 and not STRICT_SAME_ENGINE):
                    continue
            seen.add(id(d))
            op.deps.append(d)
        for b in writes:
            b.writers = [op]
            b.readers = []
        for b in reads:
            if b not in writes:
                b.readers.append(op)
        if dma_buf is not None:
            op.is_dma = True
            op.signals = True
            op.sem = dma_buf
            self.pending_dma.append(op)
        self.ops.append(op)
        if not op.is_dma:
            self.last[eng] = op
        if is_out:
            self.out_dmas.append(op)
        return op

    def barrier(self):
        lasts = [o for o in self.last.values() if not o.is_dma]
        dmas = list(self.pending_dma)
        self.pending_dma = []
        for e in ENGS:
            op = Op(e, None)
            op.idx = len(self.ops)
            op.deps = [o for o in lasts if o.eng != e] + dmas
            self.ops.append(op)

    def finalize_and_emit(self):
        nc = self.nc
        fin = Op("sp", None)
        fin.deps = list(self.out_dmas)
        fin.idx = len(self.ops)
        self.ops.append(fin)
        for op in self.ops:
            for d in op.deps:
                d.signals = True
        with contextlib.ExitStack() as es:
            engsem = {}
            for e in ("pe", "act", "dve", "pool"):
                engsem[e] = es.enter_context(nc.semaphore("c_" + e))
            counters = {e: 0 for e in engsem}
            for op in self.ops:
                if op.is_dma:
                    b = op.sem
                    if b.sem is None:
                        b.sem = es.enter_context(nc.semaphore("d_" + b.name))
                        b.semcount = 0
                    b.semcount += 16
                    op.sem = b.sem
                    op.val = b.semcount
                elif op.signals and op.eng in engsem:
                    counters[op.eng] += 1
                    op.sem = engsem[op.eng]
                    op.val = counters[op.eng]
            block = es.enter_context(nc.Block())
            engobj = {"pe": "tensor", "act": "scalar", "dve": "vector",
                      "pool": "gpsimd", "sp": "sync"}

            def make(ename):
                def body(eng):
                    known = {}
                    for op in self.ops:
                        if op.eng != ename:
                            continue
                        need = {}
                        for d in op.deps:
                            if d.sem is None:
                                continue
                            k = id(d.sem)
                            if k not in need or need[k][1] < d.val:
                                need[k] = (d.sem, d.val)
                        for k, (s, v) in need.items():
                            if known.get(k, 0) >= v:
                                continue
                            eng.wait_ge(s, v)
                            known[k] = v
                        if op.fn is None:
                            continue
                        ins = op.fn(eng)
                        if op.signals:
                            ins.then_inc(op.sem, 16 if op.is_dma else 1)
                return body

            for ename in ENGS:
                getattr(block, engobj[ename])(make(ename))


class Arena:
    def __init__(self, ap, words):
        self.ap = ap
        self.words = words
        self.off = 0
        self.n = 0

    def mark(self):
        return self.off

    def reset(self, m=0):
        self.off = m

    def alloc(self, shape, dt, name=None):
        n = int(np.prod(shape[1:]))
        nw = n if dt == F32 else (n + 1) // 2
        assert self.off + nw <= self.words, (name, self.off, nw, self.words)
        v = self.ap[:, self.off:self.off + nw]
        if dt != F32:
            v = v.bitcast(dt)[:, 0:n]
        self.off += nw
        if len(shape) == 3:
            v = v.rearrange("p (a b) -> p a b", a=shape[1])
        elif len(shape) == 4:
            v = v.rearrange("p (a b c) -> p a b c", a=shape[1], b=shape[2])
        self.n += 1
        return v


def _t5_bucket_np(rel):
    import jax.numpy as jnp
    import math
    import jax
    with jax.default_device(jax.devices("cpu")[0]):
        return _t5_bucket_cpu(rel)


def _t5_bucket_cpu(rel):
    import jax.numpy as jnp
    import math
    rel = jnp.asarray(rel, dtype=jnp.int32)
    nb = 16
    max_exact = 8
    n = -rel
    ret = jnp.where(n < 0, nb, 0)
    n = jnp.abs(n)
    nf = jnp.maximum(n, 1).astype(jnp.float32)
    large = max_exact + (jnp.log(nf / max_exact) / math.log(128 / max_exact) * (nb - max_exact)).astype(jnp.int32)
    large = jnp.minimum(large, nb - 1)
    return np.asarray(ret + jnp.where(n < max_exact, n, large))


def host_consts():
    c = {}
    c["ident"] = np.eye(128, dtype=np.float32)
    cm = np.zeros((128, 6, 128), np.float32)
    cm[:, 0, :] = 1.0 / 1024
    cm[0:64, 1, 0:64] = 1.0 / 64
    cm[64:128, 1, 64:128] = 1.0 / 64
    cm[:, 2, :] = 1.0 / 256
    cm[:, 3, :] = 1.0 / 128
    cm[0:64, 4, 0:64] = 1.0 / 64
    cm[64:96, 4, 64:96] = 1.0 / 32
    cm[96:128, 4, 96:128] = 1.0 / 32
    for k in range(128):
        m = k if k < 96 else k - 32
        cm[k, 5, m] = 1.0
    c["cmat"] = cm.astype(ml_dtypes.bfloat16)
    j = np.arange(384)
    bk = _t5_bucket_np(127 - j)
    oha = np.zeros((32, 384), np.float32)
    oha[bk, j] = 1.0
    c["oh_a"] = oha
    j = np.arange(768)
    idx = np.clip(127 - j, -128, 128) + 128
    ohb = np.zeros((256, 768), np.float32)
    ohb[idx, j] = 1.0
    c["oh_b"] = np.ascontiguousarray(ohb.reshape(2, 128, 768).transpose(1, 0, 2))
    p = np.arange(128)[:, None] // 64
    ya = np.arange(256)[None, :] // 64
    c["vis_a"] = ((ya - p >= 0) & (ya - p <= 2)).astype(np.float32)
    yb = np.arange(640)[None, :] // 64
    c["vis_b"] = ((yb - p >= 0) & (yb - p <= 8)).astype(np.float32)
    pos = np.concatenate([np.arange(2048), 4096 + np.arange(64)]).astype(np.float32)
    half = 16
    inv = (1.0 / (np.float32(10000.0) ** (np.arange(half, dtype=np.float32) / np.float32(half)))).astype(np.float32)
    ang = (pos[None, :] * inv[:, None]).astype(np.float32)
    cos = np.cos(ang).astype(np.float32)
    sin = np.sin(ang).astype(np.float32)
    tab = np.ones((128, T), np.float32)
    tab[64:80] = cos
    tab[80:96] = cos
    tab[96:112] = -sin
    tab[112:128] = sin
    c["tabq"] = tab
    sk = np.zeros((128, T), np.float32)
    sk[64:80] = -sin
    sk[80:96] = sin
    c["sink"] = sk
    return c


NVEC = 70


def pack_vec(inp):
    v = np.zeros((128, NVEC), np.float32)

    def col8(a):
        return np.asarray(a, np.float32).reshape(8, 128).T

    for l in range(2):
        v[:, 24 * l + 0:24 * l + 8] = col8(inp["ff1_norm"][l])
        v[:, 24 * l + 8:24 * l + 16] = col8(inp["mix_norm"][l])
        v[:, 24 * l + 16:24 * l + 24] = col8(inp["ff2_norm"][l])
    for k, nm in enumerate(["a_q_norm", "a_k_norm", "b_q_norm", "b_k_norm"]):
        g = np.asarray(inp[nm][0], np.float32)
        v[:, 48 + k] = np.concatenate([g, g])
    v[:, 52:54] = np.asarray(inp["d_q_a_norm"][0], np.float32).reshape(2, 128).T
    v[:, 54] = np.asarray(inp["d_kv_a_norm"][0], np.float32)
    qn = np.asarray(inp["d_q_nope_norm"][0], np.float32)
    qr = np.asarray(inp["d_q_rope_norm"][0], np.float32)
    kn = np.asarray(inp["d_k_nope_norm"][0], np.float32)
    kr = np.asarray(inp["d_k_rope_norm"][0], np.float32)
    sw = lambda a: np.concatenate([a[16:], a[:16]])
    v[:, 55] = np.concatenate([qn, qr, sw(qr)])
    v[0:64, 56] = kn
    v[64:96, 56] = kr
    v[96:128, 56] = sw(kr)
    v[:, 57] = np.concatenate([kn, kn])
    cw = np.asarray(inp["c_conv_w"][0], np.float32)
    for jj in range(3):
        v[:, 58 + 4 * jj:58 + 4 * jj + 4] = cw[jj].reshape(4, 128).T
    return v


def build(stop=None):
    nc = bass.Bass("TRN2", target_bir_lowering=False)

    def din(name, shape, dt=F32):
        return nc.dram_tensor(name, list(shape), dt, kind="ExternalInput").ap()

    def dout(name, shape):
        return nc.dram_tensor(name, list(shape), F32, kind="ExternalOutput").ap()

    xp_d = din("xp", [2048, 1024])
    xs_d = din("xs", [64, 1024])
    cak_d = din("cak", [128, 128])
    cav_d = din("cav", [128, 128])
    cbk_d = din("cbk", [512, 512])
    cbv_d = din("cbv", [512, 512])
    ccv_d = din("ccv", [2, 512])
    ckv_d = din("ckv", [4096, 128])
    ckp_d = din("ckp", [4096, 32])
    wgu_d = [din("ff1_w_gu", [2, 1024, 5632]), din("ff2_w_gu", [2, 1024, 5632])]
    wdn_d = [din("ff1_w_down", [2, 2816, 1024]), din("ff2_w_down", [2, 2816, 1024])]
    evin_d = din("ev_w_in", [1024, 2304])
    evout_d = din("ev_w_out", [1024, 1024])
    odin_d = din("od_w_in", [1024, 1952])
    odout_d = din("od_w_out", [1024, 1024])
    wqb_d = din("wqb", [256, 8, 128])
    wkvb_d = din("wkvb", [128, 1024])
    vec_d = din("vec", [128, NVEC])
    t5_d = din("t5", [32, 8])
    brel_d = din("brel", [8, 257])
    sinks_d = din("sinks", [1, 8])
    ident_d = din("ident", [128, 128])
    cmat_d = din("cmat", [128, 6, 128], BF16)
    oha_d = din("oh_a", [32, 384])
    ohb_d = din("oh_b", [128, 2, 768])
    visa_d = din("vis_a", [128, 256])
    visb_d = din("vis_b", [128, 640])
    tabq_d = din("tabq", [128, T])
    sink_d = din("sink", [128, T])

    o_yp = dout("o_yp", [2048, 1024])
    o_ys = dout("o_ys", [64, 1024])
    o_akp = dout("o_akp", [128, 128])
    o_avp = dout("o_avp", [128, 128])
    o_bkp = dout("o_bkp", [512, 512])
    o_bvp = dout("o_bvp", [512, 512])
    o_cp = dout("o_cp", [2, 512])
    o_ckvp = dout("o_ckvp", [2048, 128])
    o_kpep = dout("o_kpep", [2048, 32])
    o_aks = dout("o_aks", [128, 128])
    o_avs = dout("o_avs", [128, 128])
    o_bks = dout("o_bks", [512, 512])
    o_bvs = dout("o_bvs", [512, 512])
    o_cs = dout("o_cs", [2, 512])
    o_ckvs = dout("o_ckvs", [64, 128])
    o_kpes = dout("o_kpes", [64, 32])
    gscr = nc.dram_tensor("gscr", [16, 128, 768], F32, kind="Internal").ap()

    P = Prog(nc)
    es = contextlib.ExitStack()
    with es:
        def sbt(name, shape, dt):
            return es.enter_context(nc.sbuf_tensor("s_" + name, shape, dt))

        xT = sbt("xT", [128, 8, T], F32)
        hn = sbt("hn", [128, 8, T], BF16)
        ident = sbt("ident", [128, 128], F32)
        cmat = sbt("cmat", [128, 6, 128], BF16)
        vec = sbt("vec", [128, NVEC], F32)
        AW = 27000
        arena_t = sbt("arena", [128, AW], F32)
        ar = Arena(arena_t, AW)
        psb = [es.enter_context(nc.psum_tensor("ps%d" % i, [128, 512], F32)) for i in range(8)]
        PB = [Buf("ps%d" % i, psum=True) for i in range(8)]

        bufn = [0]

        def nb(name="b"):
            bufn[0] += 1
            return Buf("%s%d" % (name, bufn[0]))

        B_x = [nb("x") for _ in BLKS]
        B_hn = [nb("hn") for _ in BLKS]
        B_const = nb("const")
        B_hnall = None

        def mm(out, lhsT, rhs, start, stop, R, W):
            P.add("pe", lambda e: e.matmul(out, lhsT=lhsT, rhs=rhs, start=start, stop=stop,
                                           skip_group_check=True), reads=R, writes=W)

        def tr(out, in_, idn, R, W):
            P.add("pe", lambda e: e.transpose(out=out, in_=in_, identity=idn), reads=R, writes=W)

        def act(out, in_, func, R, W, scale=1.0, bias=0.0):
            P.add("act", lambda e: e.activation(out=out, in_=in_, func=func, bias=bias, scale=scale),
                  reads=R, writes=W)

        def cp(eng, out, in_, R, W):
            if eng == "act":
                P.add("act", lambda e: e.copy(out=out, in_=in_), reads=R, writes=W)
            else:
                P.add(eng, lambda e: e.tensor_copy(out=out, in_=in_), reads=R, writes=W)

        def tt(eng, out, in0, in1, op, R, W):
            P.add(eng, lambda e: e.tensor_tensor(out=out, in0=in0, in1=in1, op=op), reads=R, writes=W)

        def ts(eng, out, in0, s1, s2, op0, op1, R, W):
            if s2 is None:
                P.add(eng, lambda e: e.tensor_scalar(out=out, in0=in0, scalar1=s1, scalar2=None, op0=op0),
                      reads=R, writes=W)
            else:
                P.add(eng, lambda e: e.tensor_scalar(out=out, in0=in0, scalar1=s1, scalar2=s2, op0=op0, op1=op1),
                      reads=R, writes=W)

        def stt(eng, out, in0, scalar, in1, op0, op1, R, W):
            P.add(eng, lambda e: e.scalar_tensor_tensor(out=out, in0=in0, scalar=scalar, in1=in1, op0=op0, op1=op1),
                  reads=R, writes=W)

        def recip(out, in_, R, W):
            P.add("dve", lambda e: e.reciprocal(out=out, in_=in_), reads=R, writes=W)

        def rstd_from(out, ss_psum, R, B_out):
            act(out, ss_psum, AF.Ln, R, [B_out], bias=EPS)
            act(out, out, AF.Exp, [B_out], [B_out], scale=-0.5)

        def memset(eng, ap, val, W):
            P.add(eng, lambda e: e.memset(ap, val), writes=W)

        def dma(q, out, in_, R, W, buf, is_out=False):
            P.add(q, lambda e: e.dma_start(out=out, in_=in_), reads=R, writes=W, dma_buf=buf, is_out=is_out)

        def dma_nc(q, out, in_, R, W, buf, is_out=False):
            P.add(q, lambda e: e.dma_start(out=out, in_=in_, allow_slow_non_contiguous=True),
                  reads=R, writes=W, dma_buf=buf, is_out=is_out)

        rr = [0]

        def ev_eng():
            rr[0] += 1
            return "act" if rr[0] % 2 else "dve"

        dma("sp", ident[:], ident_d, [], [B_const], B_const)
        dma("sp", cmat[:], cmat_d, [], [B_const], B_const)
        dma("sp", vec[:], vec_d, [], [B_const], B_const)
        CM_ONES1024, CM_BD64, CM_ONES256, CM_ONES128, CM_BD3, CM_AMAT = range(6)

        FFN_END = 22464
        ar.reset(FFN_END)
        stg = [ar.alloc([128, 1024], F32) for _ in range(2)]
        B_stg = [nb("stg") for _ in range(2)]
        for i in range(NTILE):
            r0, nr = tile_rows(i)
            s = i % 2
            src = xp_d[r0:r0 + nr, :] if i < 16 else xs_d[:, :]
            dma("sp", stg[s][0:nr, :], src, [], [B_stg[s]], B_stg[s])
            blk = min(i // 4, 4)
            for hf in range(2):
                bk = (2 * i + hf) % 4
                for k in range(4):
                    c = 4 * hf + k
                    tr(psb[bk][:, k * 128:k * 128 + nr], stg[s][0:nr, c * 128:(c + 1) * 128], ident[0:nr, 0:nr],
                       [B_stg[s], B_const], [PB[bk]])
                src_ps = psb[bk][:, :].rearrange("p (c t) -> p c t", c=4)[:, :, 0:nr]
                cp(ev_eng(), xT[:, 4 * hf:4 * hf + 4, r0:r0 + nr], src_ps, [PB[bk]], [B_x[blk]])
        pass

        def norm_block(bi, gcol0, sq, rs, B_sq, B_rs, part="ab"):
            c0, n = BLKS[bi]
            if "a" in part:
                P.add("pool", lambda e: e.tensor_tensor(out=sq[:, :, 0:n], in0=xT[:, :, c0:c0 + n],
                                                        in1=xT[:, :, c0:c0 + n], op=ALU.mult),
                      reads=[B_x[bi]], writes=[B_sq])
            if "b" not in part:
                return
            for c in range(8):
                mm(psb[6][:, 0:n], cmat[:, CM_ONES1024, :], sq[:, c, 0:n], c == 0, c == 7, [B_sq, B_const], [PB[6]])
            rstd_from(rs[:, 0:n], psb[6][:, 0:n], [PB[6]], B_rs)
            for c in range(8):
                stt("dve", hn[:, c, c0:c0 + n], xT[:, c, c0:c0 + n], vec[:, gcol0 + c:gcol0 + c + 1], rs[:, 0:n],
                    ALU.mult, ALU.mult, [B_x[bi], B_rs, B_const], [B_hn[bi]])

        FB = {}

        def ffn(l, which, skip_norm01=False, next_gcol=None, end_barrier=True):
            wgu = wgu_d[which]
            wdn = wdn_d[which]
            gcol0 = 24 * l + (0 if which == 0 else 16)
            ar.reset()
            h = ar.alloc([128, 22, 1088], BF16)
            wg = [ar.alloc([128, 8, 256], BF16) for _ in range(2)]
            wu = [ar.alloc([128, 8, 256], BF16) for _ in range(2)]
            wd = [ar.alloc([128, 22, 128], BF16) for _ in range(2)]
            sq = ar.alloc([128, 8, 512], BF16)
            rs = ar.alloc([128, 512], F32)
            sg = [ar.alloc([128, 512], F32) for _ in range(2)]
            assert ar.off == FFN_END
            if not FB:
                FB.update(h=nb("h"), sq=nb("sq"), rs=nb("rs"), wgu=[nb("wgu") for _ in range(2)],
                          wd=[nb("wd") for _ in range(2)], sg=[nb("sg") for _ in range(2)])
            B_h, B_sq, B_rs, B_wgu, B_wd, B_sg = FB["h"], FB["sq"], FB["rs"], FB["wgu"], FB["wd"], FB["sg"]
            gi = 0
            yi = 0
            wslot = 0
            dslot = 0
            for half in ([0, 1], [2, 3, 4]):
                h0 = BLKS[half[0]][0]
                if half[0] == 0 and not skip_norm01:
                    for bi in half:
                        norm_block(bi, gcol0, sq, rs, B_sq, B_rs)
                for jg in range(11):
                    if half[0] == 0 and jg in (3, 5, 7):
                        norm_block({3: 2, 5: 3, 7: 4}[jg], gcol0, sq, rs, B_sq, B_rs)
                    if half[0] == 2 and next_gcol is not None and jg in (4, 7):
                        norm_block({4: 0, 7: 1}[jg], next_gcol, sq, rs, B_sq, B_rs)
                    s = wslot % 2
                    wslot += 1
                    dma("pool", wg[s][:], wgu[l, :, jg * 256:(jg + 1) * 256].rearrange("(c p) f -> p c f", p=128),
                        [], [B_wgu[s]], B_wgu[s])
                    dma("pool", wu[s][:], wgu[l, :, 2816 + jg * 256:2816 + (jg + 1) * 256].rearrange("(c p) f -> p c f", p=128),
                        [], [B_wgu[s]], B_wgu[s])
                    for fc in range(2):
                        j = 2 * jg + fc
                        for bi in half:
                            c0, n = BLKS[bi]
                            gb, ub = gi % 2, 2 + gi % 2
                            sgi = gi % 2
                            gi += 1
                            for c in range(8):
                                mm(psb[gb][:, 0:n], wg[s][:, c, fc * 128:(fc + 1) * 128], hn[:, c, c0:c0 + n],
                                   c == 0, c == 7, [B_wgu[s], B_hn[bi]], [PB[gb]])
                            for c in range(8):
                                mm(psb[ub][:, 0:n], wu[s][:, c, fc * 128:(fc + 1) * 128], hn[:, c, c0:c0 + n],
                                   c == 0, c == 7, [B_wgu[s], B_hn[bi]], [PB[ub]])
                            act(sg[sgi][:, 0:n], psb[gb][:, 0:n], AF.Silu, [PB[gb]], [B_sg[sgi]])
                            tt("dve", h[:, j, c0 - h0:c0 - h0 + n], psb[ub][:, 0:n], sg[sgi][:, 0:n], ALU.mult,
                               [PB[ub], B_sg[sgi]], [B_h])
                for o in range(8):
                    s = dslot % 2
                    dslot += 1
                    dma("pool", wd[s][:], wdn[l, :, o * 128:(o + 1) * 128].rearrange("(j p) o -> p j o", p=128),
                        [], [B_wd[s]], B_wd[s])
                    for bi in half:
                        c0, n = BLKS[bi]
                        yb = 4 + yi % 2
                        yi += 1
                        for j in range(22):
                            mm(psb[yb][:, 0:n], wd[s][:, j, :], h[:, j, c0 - h0:c0 - h0 + n], j == 0, j == 21,
                               [B_wd[s], B_h], [PB[yb]])
                        stt("dve", xT[:, o, c0:c0 + n], psb[yb][:, 0:n], 0.5, xT[:, o, c0:c0 + n], ALU.mult, ALU.add,
                            [PB[yb], B_x[bi]], [B_x[bi]])
            if end_barrier:
                P.barrier()

        def final_phase():
            ar.reset(FFN_END)
            st2 = [ar.alloc([128, 1024], F32) for _ in range(2)]
            B_st2 = [nb("st2") for _ in range(2)]
            for i in range(NTILE):
                r0, nr = tile_rows(i)
                s = i % 2
                blk = min(i // 4, 4)
                for hf in range(2):
                    bk = (2 * i + hf) % 4
                    for k in range(4):
                        c = 4 * hf + k
                        tr(psb[bk][0:nr, k * 128:(k + 1) * 128], xT[:, c, r0:r0 + nr], ident[:, :],
                           [B_x[blk], B_const], [PB[bk]])
                    cp(ev_eng(), st2[s][0:nr, hf * 512:(hf + 1) * 512], psb[bk][0:nr, :], [PB[bk]], [B_st2[s]])
                dst = o_yp[r0:r0 + nr, :] if i < 16 else o_ys[:, :]
                dma("sp", dst, st2[s][0:nr, :], [B_st2[s]], [], B_st2[s], is_out=True)


        def sw_pipeline(items, stage_fns, lag=1):
            n = len(items)
            K = len(stage_fns)
            for s in range(n + (K - 1) * lag):
                for k in range(K):
                    idx = s - k * lag
                    if 0 <= idx < n:
                        stage_fns[k](items[idx], idx)
                yield

        def drain(*gens):
            gens = [g for g in gens if g is not None]
            while gens:
                for g in list(gens):
                    try:
                        next(g)
                    except StopIteration:
                        gens.remove(g)

        def norm_all(gcol0, blocks=(0, 1, 2, 3, 4)):
            m = ar.mark()
            drain(norm_all_gen(gcol0, blocks))
            P.barrier()
            ar.reset(m)

        def norm_all_gen(gcol0, blocks=(0, 1, 2, 3, 4)):
            sq = ar.alloc([128, 8, 512], BF16)
            rs = ar.alloc([128, 512], F32)
            B_sq, B_rs = nb("sq"), nb("rs")
            for bi in blocks:
                norm_block(bi, gcol0, sq, rs, B_sq, B_rs)
                yield

        class Chain:
            def __init__(self):
                self.sqc = [ar.alloc([128, 512], BF16) for _ in range(3)]
                self.rsc = [ar.alloc([128, 512], F32) for _ in range(2)]
                self.B_sqc = [nb("sqc") for _ in range(3)]
                self.B_rsc = [nb("rsc") for _ in range(2)]

            def item(self, pmm, rows, n, bd_lhsT, gcol, dests, PJ, SSB, post=None):
                r0, r1 = rows

                def s0(idx):
                    b = PJ[idx % len(PJ)]
                    sl = idx % 3
                    pmm(b)
                    act(self.sqc[sl][r0:r1, 0:n], psb[b][r0:r1, 0:n], AF.Square, [PB[b]], [self.B_sqc[sl]])

                def s1(idx):
                    b = PJ[idx % len(PJ)]
                    sl = idx % 3
                    rl = idx % 2
                    ssb = SSB[idx % len(SSB)]
                    mm(psb[ssb][r0:r1, 0:n], bd_lhsT, self.sqc[sl][r0:r1, 0:n], True, True,
                       [self.B_sqc[sl], B_const], [PB[ssb]])
                    rstd_from(self.rsc[rl][r0:r1, 0:n], psb[ssb][r0:r1, 0:n], [PB[ssb]], self.B_rsc[rl])
                    for d in dests:
                        ap, bf = d[0], d[1]
                        a, b_ = d[2] if len(d) > 2 else (r0, r1)
                        g = d[3] if len(d) > 3 else gcol
                        stt("dve", ap, psb[b][a:b_, 0:n], g, self.rsc[rl][a:b_, 0:n], ALU.mult, ALU.mult,
                            [PB[b], self.B_rsc[rl], B_const], [bf])
                    if post is not None:
                        post()

                return (s0, s1)

        def simple_item(f0, f1, PJ):
            return (lambda idx: f0(PJ[idx % len(PJ)]), lambda idx: f1(PJ[idx % len(PJ)]))

        def run_items(items, lag=1):
            return sw_pipeline(items, [lambda it, i: it[0](i), lambda it, i: it[1](i)], lag)

        def out_proj(mix, nk, wo, B_mix, B_wo, banks=(0, 1, 2)):
            cnt = 0
            for bi in range(5):
                c0, n = BLKS[bi]
                for o in range(8):
                    bk = banks[cnt % len(banks)]
                    cnt += 1
                    for k in range(nk):
                        mm(psb[bk][:, 0:n], wo[:, k, o * 128:(o + 1) * 128], mix[:, k, c0:c0 + n], k == 0, k == nk - 1,
                           [B_wo, B_mix], [PB[bk]])
                    tt("dve", xT[:, o, c0:c0 + n], psb[bk][:, 0:n], xT[:, o, c0:c0 + n], ALU.add,
                       [PB[bk], B_x[bi]], [B_x[bi]])

        NPB = 6

        class AttnScratch:
            def __init__(self, need_pexp=True):
                self.pexp = [ar.alloc([128, 512], F32) for _ in range(3)] if need_pexp else None
                self.pb = [ar.alloc([128, 512], BF16) for _ in range(NPB)]
                self.B_pexp = [nb("pexp") for _ in range(3)]
                self.B_pb = [nb("pb") for _ in range(NPB)]
                self.rcp = ar.alloc([128, 512], F32)
                self.rcp2 = self.rcp
                self.B_rcp, self.B_rcp2 = nb("rcp"), nb("rcp2")

        def attention_gen(A, groups, scale, SB, OB, lag=4):
            items = []
            ob0 = 0
            for gi, g in enumerate(groups):
                g["ob0"] = ob0
                ob0 += g["nO"]
                for ti, t in enumerate(g["tiles"]):
                    items.append((gi, t, ti == len(g["tiles"]) - 1))
            started = set()

            def s0(it, idx):
                gi, t, last = it
                sl3, sl4 = idx % 3, idx % NPB
                sbk = SB[idx % len(SB)]
                nk, W = t["nk"], t["W"]
                ov = psb[sbk][0:nk, 0:W]
                pb_v = A.pb[sl4][0:nk, 0:W]
                view = t.get("view")
                if view is not None:
                    ov, pb_v = view(ov), view(pb_v)
                mm(ov, t["KT"], t["qrhs"], True, True, t["R"], [PB[sbk]])
                if t.get("Gap") is None:
                    act(pb_v, ov, AF.Exp, [PB[sbk]], [A.B_pb[sl4]], scale=scale)
                else:
                    pe_v = A.pexp[sl3][0:nk, 0:W]
                    if view is not None:
                        pe_v = view(pe_v)
                    act(pe_v, ov, AF.Exp, [PB[sbk]], [A.B_pexp[sl3]], scale=scale)
                    tt("pool" if idx % 3 == 2 else "dve", pb_v, pe_v, t["Gap"], ALU.mult,
                       [A.B_pexp[sl3], B_G], [A.B_pb[sl4]])

            def s1(it, idx):
                gi, t, last = it
                g = groups[gi]
                sl4 = idx % NPB
                obs = [OB[(g["ob0"] + j) % len(OB)] for j in range(g["nO"])]
                for (j, lo, hi, VEap, nkr, plo, phi, Rk) in t["pv"]:
                    first = (gi, j) not in started
                    started.add((gi, j))
                    mm(psb[obs[j]][:, lo:hi], VEap, A.pb[sl4][0:nkr, plo:phi], first, False,
                       [A.B_pb[sl4]] + list(Rk), [PB[obs[j]]])
                if last:
                    g["finish"](obs)

            return sw_pipeline(items, [s0, s1], lag)

        def attn_finish(A, ob, ncols, esink_ap, dest, B_dest, R_extra=(), use_dve=False):
            if use_dve:
                recip(A.rcp2[0:64, 0:ncols], psb[ob][64:128, 0:ncols], [PB[ob]], [A.B_rcp2])
            else:
                if esink_ap is not None:
                    act(A.rcp[64:128, 0:ncols], psb[ob][64:128, 0:ncols], AF.Ln, [PB[ob], B_G], [A.B_rcp], bias=esink_ap)
                else:
                    act(A.rcp[64:128, 0:ncols], psb[ob][64:128, 0:ncols], AF.Ln, [PB[ob]], [A.B_rcp])
                act(A.rcp2[0:64, 0:ncols], A.rcp[64:128, 0:ncols], AF.Exp, [A.B_rcp], [A.B_rcp2], scale=-1.0)
            tt("dve", dest, psb[ob][0:64, 0:ncols], A.rcp2[0:64, 0:ncols], ALU.mult,
               [PB[ob], A.B_rcp2] + list(R_extra), [B_dest])

        B_G = nb("G")

        class OutStage:
            def __init__(self, width=256):
                self.ostg = [ar.alloc([128, width], F32) for _ in range(2)]
                self.B = [nb("ostg") for _ in range(2)]
                self.cnt = 0

            def store_rows(self, src_bank, nr, ncols, dst):
                s = self.cnt % 2
                self.cnt += 1
                cp(ev_eng(), self.ostg[s][0:nr, 0:ncols], psb[src_bank][0:nr, 0:ncols], [PB[src_bank]], [self.B[s]])
                dma("sp", dst, self.ostg[s][0:nr, 0:ncols], [self.B[s]], [], self.B[s], is_out=True)

        def mix_even():
            ar.reset()
            G_A = ar.alloc([128, 8, 256], BF16)
            G_B = ar.alloc([128, 8, 640], BF16)
            esink = ar.alloc([128, 8], F32)
            mG = ar.mark()
            gnorm = norm_all_gen(8, (2, 3, 4))
            t5s = ar.alloc([128, 8], F32)
            tcb = ar.alloc([128, 2, 8], F32)
            oha = ar.alloc([128, 384], F32)
            ohb = ar.alloc([128, 2, 768], F32)
            visa = ar.alloc([128, 256], F32)
            visb = ar.alloc([128, 640], F32)
            sraw = ar.alloc([128, 8], F32)
            lbA = [ar.alloc([128, 128], F32) for _ in range(8)]
            lbB = [ar.alloc([128, 2, 128], F32) for _ in range(8)]
            rrA = [ar.alloc([128, 384], F32) for _ in range(8)]
            rrB = [ar.alloc([128, 768], F32) for _ in range(8)]
            gpA = [ar.alloc([128, 256], F32) for _ in range(4)]
            gpB = [ar.alloc([128, 640], F32) for _ in range(4)]
            B_gc = nb("gconst")
            B_lbA = [nb("lbA") for _ in range(8)]
            B_lbB = [nb("lbB") for _ in range(8)]
            B_rrA = [nb("rrA") for _ in range(8)]
            B_rrB = [nb("rrB") for _ in range(8)]
            B_gpA = [nb("gpA") for _ in range(4)]
            B_gpB = [nb("gpB") for _ in range(4)]
            B_gscr = [nb("gscr") for _ in range(16)]
            dma("sp", t5s[0:32, :], t5_d, [], [B_gc], B_gc)
            for kc in range(2):
                dma_nc("sp", tcb[:, kc, :], brel_d[:, kc * 128:(kc + 1) * 128].rearrange("h p -> p h"), [], [B_gc], B_gc)
            dma("sp", oha[0:32, :], oha_d, [], [B_gc], B_gc)
            dma("sp", ohb[:], ohb_d, [], [B_gc], B_gc)
            dma("sp", visa[:], visa_d, [], [B_gc], B_gc)
            dma("sp", visb[:], visb_d, [], [B_gc], B_gc)
            dma("sp", sraw[:], sinks_d.partition_broadcast(128), [], [B_gc], B_gc)
            act(esink[:], sraw[:], AF.Exp, [B_gc], [B_G])
            next(gnorm)
            for h in range(8):
                cp("dve", lbA[h][0:32, :], t5s[0:32, h:h + 1].to_broadcast([32, 128]), [B_gc], [B_lbA[h]])
                for kc in range(2):
                    cp("dve", lbB[h][:, kc, :], tcb[:, kc, h:h + 1].to_broadcast([128, 128]), [B_gc], [B_lbB[h]])
            next(gnorm)
            for h in range(8):
                ba = (0, 3)[h % 2]
                bb = (1, 4)[h % 2]
                bc = (2, 5)[h % 2]
                mm(psb[ba][:, 0:384], lbA[h][0:32, :], oha[0:32, 0:384], True, True, [B_lbA[h], B_gc], [PB[ba]])
                act(rrA[h][:, :], psb[ba][:, 0:384], AF.Exp, [PB[ba]], [B_rrA[h]])
                for kc in range(2):
                    mm(psb[bb][:, 0:256], lbB[h][:, kc, :], ohb[:, kc, 0:256], kc == 0, kc == 1, [B_lbB[h], B_gc], [PB[bb]])
                act(rrB[h][:, 0:256], psb[bb][:, 0:256], AF.Exp, [PB[bb]], [B_rrB[h]])
                act(rrB[h][:, 256:768], psb[bb][:, 255:256].to_broadcast([128, 512]), AF.Exp, [PB[bb]], [B_rrB[h]])
                scrA = bass.AP(gscr.tensor, h * 128 * 768, [[384, 128], [1, 384]])
                dma("sp", scrA, rrA[h][:, :], [B_rrA[h]], [B_gscr[h]], B_gscr[h])
                scrB = bass.AP(gscr.tensor, (8 + h) * 128 * 768, [[768, 128], [1, 768]])
                dma("sp", scrB, rrB[h][:, :], [B_rrB[h]], [B_gscr[8 + h]], B_gscr[8 + h])
            next(gnorm)
            for h in range(8):
                s = h % 4
                skA = bass.AP(gscr.tensor, h * 128 * 768 + 127, [[383, 128], [1, 256]])
                dma("sp", gpA[s][:, :], skA, [B_gscr[h]], [B_gpA[s]], B_gpA[s])
                skB = bass.AP(gscr.tensor, (8 + h) * 128 * 768 + 127, [[767, 128], [1, 640]])
                dma("sp", gpB[s][:, :], skB, [B_gscr[8 + h]], [B_gpB[s]], B_gpB[s])
                kvh, g = h // 4, h % 4
                slot = kvh * 4 + (g % 2) * 2 + g // 2
                tt("dve", G_A[:, slot, :], gpA[s][:, :], visa[:], ALU.mult, [B_gpA[s], B_gc], [B_G])
                tt("dve", G_B[:, h, :], gpB[s][:, :], visb[:], ALU.mult, [B_gpB[s], B_gc], [B_G])
            P.barrier()
            ar.reset(mG)
            CH = Chain()
            AS = AttnScratch()
            k32 = [ar.alloc([128, 512], F32) for _ in range(1)]
            B_k32 = [nb("k32") for _ in range(1)]
            OSt = OutStage(256)
            mS = ar.mark()
            PJ, SSB = (0, 1, 2), (3, 4)

            def k_out(k32cols, B_src, nr, dst, bank):
                tr(psb[bank][0:nr, 0:128], k32cols, ident[:, :], [B_src, B_const], [PB[bank]])
                OSt.store_rows(bank, nr, 128, dst)

            wsrc = lambda a, b: evin_d[:, a:b].rearrange("(c p) f -> p c f", p=128)

            wbuf = ar.alloc([128, 8, 896], BF16)
            B_w = nb("wA")
            aqT = ar.alloc([128, 4, T], BF16)
            akT = ar.alloc([128, 2, T], BF16)
            aVE = ar.alloc([128, NTILE, 2, 128], BF16)
            cstg = ar.alloc([128, 256], F32)
            akcT = ar.alloc([128, 2, 128], BF16)
            aVEc = ar.alloc([128, 2, 128], BF16)
            mixA = ar.alloc([128, 4, T], BF16)
            B_aq, B_ak, B_aVE, B_cstg, B_akc, B_aVEc, B_mixA = (nb("aq"), nb("ak"), nb("aVE"), nb("cstg"),
                                                               nb("akc"), nb("aVEc"), nb("mixA"))
            dma("pool", wbuf[:, :, 0:512], wsrc(0, 512), [], [B_w], B_w)
            dma("pool", wbuf[:, :, 512:640], wsrc(512, 640), [], [B_w], B_w)
            dma("pool", wbuf[:, :, 640:704], wsrc(576, 640), [], [B_w], B_w)
            dma("pool", wbuf[:, :, 704:768], wsrc(512, 576), [], [B_w], B_w)
            dma("pool", wbuf[:, :, 768:896], wsrc(640, 768), [], [B_w], B_w)
            dma("pool", aVEc[:, :, 0:64], cav_d.rearrange("k (h d) -> k h d", h=2), [], [B_aVEc], B_aVEc)
            memset("pool", aVE[:, :, :, 64:128], 1.0, [B_aVE])
            memset("pool", aVEc[:, :, 64:128], 1.0, [B_aVEc])
            dma("sp", cstg[:, 0:128], cak_d, [], [B_cstg], B_cstg)
            dma("sp", cstg[:, 128:192], cak_d[:, 64:128], [], [B_cstg], B_cstg)
            dma("sp", cstg[:, 192:256], cak_d[:, 0:64], [], [B_cstg], B_cstg)
            for sel in range(2):
                tr(psb[6 + sel][:, 0:128], cstg[:, sel * 128:(sel + 1) * 128], ident[:, :], [B_cstg, B_const], [PB[6 + sel]])
                cp("act", akcT[:, sel, :], psb[6 + sel][:, 0:128], [PB[6 + sel]], [B_akc])
            dma("sp", o_aks[0:64, :], cak_d[64:128, :], [], [], nb("oc"), is_out=True)
            dma("sp", o_avs[0:64, :], cav_d[64:128, :], [], [], nb("oc"), is_out=True)
            dma("sp", o_bks[0:448, :], cbk_d[64:512, :], [], [], nb("oc"), is_out=True)
            dma("sp", o_bvs[0:448, :], cbv_d[64:512, :], [], [], nb("oc"), is_out=True)
            items = []
            for bi in range(5):
                c0, n = BLKS[bi]
                for oc in range(6):
                    def pmm(b, oc=oc, bi=bi, c0=c0, n=n):
                        for c in range(8):
                            mm(psb[b][:, 0:n], wbuf[:, c, oc * 128:(oc + 1) * 128], hn[:, c, c0:c0 + n], c == 0, c == 7,
                               [B_w, B_hn[bi]], [PB[b]])
                    post = None
                    if oc < 4:
                        dests = [(aqT[:, oc, c0:c0 + n], B_aq)]
                        g = vec[:, 48:49]
                    else:
                        dests = [(akT[:, oc - 4, c0:c0 + n], B_ak)]
                        g = vec[:, 49:50]
                        if oc == 4 and bi >= 3:
                            dests.append((k32[0][:, 0:n], B_k32[0]))
                            if bi == 3:
                                post = lambda: k_out(k32[0][:, 384:512], B_k32[0], 128, o_akp[:, :], 6)
                            else:
                                post = lambda: k_out(k32[0][:, 0:64], B_k32[0], 64, o_aks[64:128, :], 6)
                    items.append(CH.item(pmm, (0, 128), n, cmat[:, CM_BD64, :], g, dests, PJ, SSB, post))
                for i in range(NTILE):
                    r0, nr = tile_rows(i)
                    if not (c0 <= r0 < c0 + n):
                        continue

                    def f0(b, i=i, r0=r0, nr=nr, bi=bi):
                        for c in range(8):
                            mm(psb[b][0:nr, 0:128], hn[:, c, r0:r0 + nr], wbuf[:, c, 768:896], c == 0, c == 7,
                               [B_w, B_hn[bi]], [PB[b]])

                    def f1(b, i=i, nr=nr):
                        cp("dve", aVE[0:nr, i, :, 0:64], psb[b][0:nr, 0:128].rearrange("p (k d) -> p k d", k=2),
                           [PB[b]], [B_aVE])
                        if i == 15:
                            OSt.store_rows(b, 128, 128, o_avp[:, :])
                        if i == 16:
                            OSt.store_rows(b, 64, 128, o_avs[64:128, :])
                    items.append(simple_item(f0, f1, PJ))
            drain(run_items(items))
            woA = wbuf[:, :, :].rearrange("p c f -> p (c f)")[:, 0:4096].rearrange("p (k o) -> p k o", k=4)
            dma("pool", woA, evout_d[0:512, :].rearrange("(k p) o -> p k o", p=128), [], [B_w], B_w)
            ACR = [(0, 128), (0, 256), (128, 384), (256, 512), (384, 512)]
            groups = []
            for kvh in range(2):
                for par in range(2):
                    sel = 0 if kvh == par else 1
                    pr = slice(par * 64, par * 64 + 64)
                    slot0 = kvh * 4 + par * 2
                    qgroups = [(i * 512, 512, [(4 * i - 1 + r, ACR[r], None) for r in range(5) if 4 * i - 1 + r >= 0])
                               for i in range(4)]
                    qgroups.append((2048, 64, [("cache", (0, 64), 128), ("new", (0, 64), 0)]))
                    for q0, qn, kbl in qgroups:
                        tiles = []
                        for kb, (ca, cb_), yoff in kbl:
                            ncols = cb_ - ca
                            if kb == "cache":
                                KT, nk, VEs, y0 = akcT[pr, sel, :], 128, aVEc[:, kvh, :], yoff + ca
                                Rk = [B_akc, B_aVEc]
                            elif kb == "new":
                                KT, nk, VEs, y0 = akT[pr, sel, 2048:2112], 64, aVE[0:64, 16, kvh, :], ca
                                Rk = [B_ak, B_aVE]
                            else:
                                KT, nk, VEs = akT[pr, sel, kb * 128:(kb + 1) * 128], 128, aVE[:, kb, kvh, :]
                                r = kb - (4 * (q0 // 512) - 1)
                                y0 = ca - 128 * (r - 1)
                                Rk = [B_ak, B_aVE]
                            tiles.append(dict(
                                KT=KT, qrhs=aqT[pr, kvh * 2:kvh * 2 + 2, q0 + ca:q0 + cb_], nk=nk, W=2 * ncols,
                                Gap=G_A[0:nk, slot0:slot0 + 2, y0:y0 + ncols],
                                view=lambda a: a.rearrange("p (c n) -> p c n", c=2),
                                R=[B_aq] + Rk,
                                pv=[(cc, ca, cb_, VEs, nk, cc * ncols, (cc + 1) * ncols, Rk) for cc in range(2)]))

                        def fin(obs, kvh=kvh, par=par, q0=q0, qn=qn):
                            for cc in range(2):
                                g = 2 * cc + par
                                h = kvh * 4 + g
                                chunk = kvh * 2 + g // 2
                                drow = slice((g % 2) * 64, (g % 2) * 64 + 64)
                                attn_finish(AS, obs[cc], qn, esink[64:128, h:h + 1], mixA[drow, chunk, q0:q0 + qn], B_mixA)
                        groups.append(dict(tiles=tiles, nO=2, finish=fin))
            drain(attention_gen(AS, groups, 0.125, (3, 4, 5), (0, 1, 2, 6, 7, 0, 1, 2, 6, 7)[0:4]))
            out_proj(mixA, 4, woA, B_mixA, B_w)
            P.barrier()

            BCR = [(0, 128), (0, 256), (0, 384), (0, 512), (0, 512), (128, 512), (256, 512), (384, 512)]
            for hb in range(2):
                ar.reset(mS)
                wbuf = ar.alloc([128, 8, 768], BF16)
                B_w = nb("wB")
                bqT = ar.alloc([128, 2, T], BF16)
                bkT = ar.alloc([128, 2, T], BF16)
                bVE = ar.alloc([128, NTILE, 4, 128], BF16)
                cstg = ar.alloc([128, 4, 256], F32)
                bkcT = ar.alloc([128, 2, 512], BF16)
                bVEc = ar.alloc([128, 4, 4, 128], BF16)
                mixB = ar.alloc([128, 2, T], BF16)
                kb32 = ar.alloc([128, 512], F32)
                B_kb32 = nb("kb32")
                B_bq, B_bk, B_bVE, B_cstg, B_bkc, B_bVEc, B_mixB = (nb("bq"), nb("bk"), nb("bVE"), nb("cstgb"),
                                                                   nb("bkc"), nb("bVEc"), nb("mixB"))
                for k3, base in enumerate((768, 1280, 1792)):
                    dma("pool", wbuf[:, :, k3 * 256:(k3 + 1) * 256], wsrc(base + hb * 256, base + hb * 256 + 256),
                        [], [B_w], B_w)
                for t4 in range(4):
                    dma("pool", bVEc[:, t4, :, 0:64],
                        cbv_d[t4 * 128:(t4 + 1) * 128, hb * 256:(hb + 1) * 256].rearrange("p (h d) -> p h d", h=4),
                        [], [B_bVEc], B_bVEc)
                memset("pool", bVE[:, :, :, 64:128], 1.0, [B_bVE])
                memset("pool", bVEc[:, :, :, 64:128], 1.0, [B_bVEc])
                dma("sp", cstg[:], cbk_d[:, hb * 256:(hb + 1) * 256].rearrange("(t p) f -> p t f", p=128),
                    [], [B_cstg], B_cstg)
                for t4 in range(4):
                    for cl in range(2):
                        bk = 6 + (t4 * 2 + cl) % 2
                        tr(psb[bk][:, 0:128], cstg[:, t4, cl * 128:(cl + 1) * 128], ident[:, :], [B_cstg, B_const], [PB[bk]])
                        cp(ev_eng(), bkcT[:, cl, t4 * 128:(t4 + 1) * 128], psb[bk][:, 0:128], [PB[bk]], [B_bkc])
                items = []
                for bi in range(5):
                    c0, n = BLKS[bi]
                    for oc in range(4):
                        def pmm(b, oc=oc, bi=bi, c0=c0, n=n):
                            for c in range(8):
                                mm(psb[b][:, 0:n], wbuf[:, c, oc * 128:(oc + 1) * 128], hn[:, c, c0:c0 + n], c == 0, c == 7,
                                   [B_w, B_hn[bi]], [PB[b]])
                        post = None
                        if oc < 2:
                            dests = [(bqT[:, oc, c0:c0 + n], B_bq)]
                            g = vec[:, 50:51]
                        else:
                            dests = [(bkT[:, oc - 2, c0:c0 + n], B_bk)]
                            g = vec[:, 51:52]
                            if bi >= 3:
                                kk = k32[0] if oc == 2 else kb32
                                Bkk = B_k32[0] if oc == 2 else B_kb32
                                dests.append((kk[:, 0:n], Bkk))
                                fcol = hb * 256 + (oc - 2) * 128
                                if bi == 3:
                                    def post(kk=kk, Bkk=Bkk, fcol=fcol):
                                        for t4 in range(4):
                                            k_out(kk[:, t4 * 128:(t4 + 1) * 128], Bkk, 128,
                                                  o_bkp[t4 * 128:(t4 + 1) * 128, fcol:fcol + 128], 6 + t4 % 2)
                                else:
                                    def post(kk=kk, Bkk=Bkk, fcol=fcol):
                                        k_out(kk[:, 0:64], Bkk, 64, o_bks[448:512, fcol:fcol + 128], 6)
                        items.append(CH.item(pmm, (0, 128), n, cmat[:, CM_BD64, :], g, dests, PJ, SSB, post))
                    for i in range(NTILE):
                        r0, nr = tile_rows(i)
                        if not (c0 <= r0 < c0 + n):
                            continue

                        def f0(b, i=i, r0=r0, nr=nr, bi=bi):
                            for c in range(8):
                                mm(psb[b][0:nr, 0:256], hn[:, c, r0:r0 + nr], wbuf[:, c, 512:768], c == 0, c == 7,
                                   [B_w, B_hn[bi]], [PB[b]])

                        def f1(b, i=i, nr=nr, hb=hb):
                            cp(ev_eng(), bVE[0:nr, i, :, 0:64], psb[b][0:nr, 0:256].rearrange("p (k d) -> p k d", k=4),
                               [PB[b]], [B_bVE])
                            if 12 <= i < 16:
                                OSt.store_rows(b, 128, 256, o_bvp[(i - 12) * 128:(i - 11) * 128, hb * 256:(hb + 1) * 256])
                            if i == 16:
                                OSt.store_rows(b, 64, 256, o_bvs[448:512, hb * 256:(hb + 1) * 256])
                        items.append(simple_item(f0, f1, PJ))
                drain(run_items(items))
                woB = wbuf[:, :, :].rearrange("p c f -> p (c f)")[:, 0:2048].rearrange("p (k o) -> p k o", k=2)
                dma("pool", woB, evout_d[512 + hb * 256:512 + (hb + 1) * 256, :].rearrange("(k p) o -> p k o", p=128),
                    [], [B_w], B_w)
                groups = []
                for hl in range(4):
                    h = hb * 4 + hl
                    cl = hl // 2
                    pr = slice((hl % 2) * 64, (hl % 2) * 64 + 64)
                    qgroups = [(i * 512, 512, [(4 * i - 4 + r, BCR[r], 512 - 128 * r) for r in range(8) if 4 * i - 4 + r >= 0])
                               for i in range(4)]
                    qgroups.append((2048, 64, [("c%d" % m, (0, 64), 512 - 128 * m) for m in range(4)] + [("new", (0, 64), 0)]))
                    for q0, qn, kbl in qgroups:
                        tiles = []
                        for kb, (ca, cb_), yoff in kbl:
                            ncols = cb_ - ca
                            y0 = yoff + ca
                            if isinstance(kb, str) and kb[0] == "c":
                                m = int(kb[1:])
                                KT, nk, VEs = bkcT[pr, cl, m * 128:(m + 1) * 128], 128, bVEc[:, m, hl, :]
                                Rk = [B_bkc, B_bVEc]
                            elif kb == "new":
                                KT, nk, VEs = bkT[pr, cl, 2048:2112], 64, bVE[0:64, 16, hl, :]
                                Rk = [B_bk, B_bVE]
                            else:
                                KT, nk, VEs = bkT[pr, cl, kb * 128:(kb + 1) * 128], 128, bVE[:, kb, hl, :]
                                Rk = [B_bk, B_bVE]
                            tiles.append(dict(KT=KT, qrhs=bqT[pr, cl, q0 + ca:q0 + cb_], nk=nk, W=ncols,
                                              Gap=G_B[0:nk, h, y0:y0 + ncols], R=[B_bq] + Rk,
                                              pv=[(0, ca, cb_, VEs, nk, 0, ncols, Rk)]))

                        def fin(obs, pr=pr, cl=cl, q0=q0, qn=qn):
                            attn_finish(AS, obs[0], qn, None, mixB[pr, cl, q0:q0 + qn], B_mixB)
                        groups.append(dict(tiles=tiles, nO=1, finish=fin))
                drain(attention_gen(AS, groups, 0.125, (3, 4, 5), (0, 1, 2, 6, 7)))
                out_proj(mixB, 2, woB, B_mixB, B_w)
                P.barrier()

        def mix_odd():
            ar.reset()
            CH = Chain()
            AS = AttnScratch(need_pexp=False)
            OSt = OutStage(128)
            mS = ar.mark()
            osrc = lambda a, b: odin_d[:, a:b].rearrange("(c p) f -> p c f", p=128)
            mixC = ar.alloc([128, 4, T], BF16)
            woC = ar.alloc([128, 4, 1024], BF16)
            wC = [ar.alloc([128, 8, 384], BF16) for _ in range(2)]
            ub = [ar.alloc([128, 516], F32) for _ in range(3)]
            ccs = [ar.alloc([128, 512], F32) for _ in range(2)]
            t1 = [ar.alloc([128, 512], F32) for _ in range(2)]
            cvs = ar.alloc([128, 4, 2], F32)
            B_mixC, B_woC, B_cvs = nb("mixC"), nb("woC"), nb("cvs")
            B_wC = [nb("wC") for _ in range(2)]
            B_ub = [nb("ub") for _ in range(3)]
            B_ccs = [nb("ccs") for _ in range(2)]
            B_t1 = [nb("t1") for _ in range(2)]

            def load_wC(c):
                s = c % 2
                for k3 in range(3):
                    dma("pool", wC[s][:, :, k3 * 128:(k3 + 1) * 128], osrc(k3 * 512 + c * 128, k3 * 512 + (c + 1) * 128),
                        [], [B_wC[s]], B_wC[s])
            load_wC(0)
            load_wC(1)
            dma("pool", woC[:], odout_d[0:512, :].rearrange("(k p) o -> p k o", p=128), [], [B_woC], B_woC)
            for c in range(4):
                dma_nc("sp", cvs[:, c, :], ccv_d[:, c * 128:(c + 1) * 128].rearrange("j p -> p j"), [], [B_cvs], B_cvs)
            drain(norm_all_gen(32, (2, 3, 4)))
            items = []
            cnt = 0
            for c in range(4):
                for bi in range(5):
                    c0, n = BLKS[bi]
                    us = cnt % 3
                    ups = (cnt - 1) % 3
                    es_ = cnt % 2
                    cnt += 1

                    def f0(bset, c=c, bi=bi, c0=c0, n=n):
                        s = c % 2
                        for k3 in range(3):
                            for kc in range(8):
                                mm(psb[bset[k3]][:, 0:n], wC[s][:, kc, k3 * 128:(k3 + 1) * 128], hn[:, kc, c0:c0 + n],
                                   kc == 0, kc == 7, [B_wC[s], B_hn[bi]], [PB[bset[k3]]])
                        if bi == 4 and c + 2 < 4:
                            load_wC(c + 2)

                    def f1(bset, c=c, bi=bi, c0=c0, n=n, us=us, ups=ups, es_=es_):
                        u = ub[us]
                        if bi == 0:
                            memset("dve", u[:, 0:2], 0.0, [B_ub[us]])
                        elif bi == 4:
                            cp("dve", u[:, 0:2], cvs[:, c, :], [B_cvs], [B_ub[us]])
                        else:
                            cp("dve", u[:, 0:2], ub[ups][:, 512:514], [B_ub[ups]], [B_ub[us]])
                        cp("act", ccs[es_][:, 0:n], psb[bset[1]][:, 0:n], [PB[bset[1]]], [B_ccs[es_]])
                        tt("dve", u[:, 2:2 + n], psb[bset[2]][:, 0:n], ccs[es_][:, 0:n], ALU.mult,
                           [PB[bset[2]], B_ccs[es_]], [B_ub[us]])
                        ts("dve", t1[es_][:, 0:n], u[:, 2:2 + n], vec[:, 58 + 8 + c:58 + 8 + c + 1], None, ALU.mult, None,
                           [B_ub[us], B_const], [B_t1[es_]])
                        stt("dve", t1[es_][:, 0:n], u[:, 1:1 + n], vec[:, 58 + 4 + c:58 + 4 + c + 1], t1[es_][:, 0:n],
                            ALU.mult, ALU.add, [B_ub[us], B_const, B_t1[es_]], [B_t1[es_]])
                        stt("dve", t1[es_][:, 0:n], u[:, 0:n], vec[:, 58 + c:58 + c + 1], t1[es_][:, 0:n],
                            ALU.mult, ALU.add, [B_ub[us], B_const, B_t1[es_]], [B_t1[es_]])
                        tt("dve", mixC[:, c, c0:c0 + n], psb[bset[0]][:, 0:n], t1[es_][:, 0:n], ALU.mult,
                           [PB[bset[0]], B_t1[es_]], [B_mixC])
                        if bi == 3:
                            dma_nc("sp", o_cp[:, c * 128:(c + 1) * 128].rearrange("j p -> p j"), u[:, 512:514],
                                   [B_ub[us]], [], nb("ocp"), is_out=True)
                        if bi == 4:
                            dma_nc("sp", o_cs[:, c * 128:(c + 1) * 128].rearrange("j p -> p j"), u[:, 64:66],
                                   [B_ub[us]], [], nb("ocs"), is_out=True)
                    items.append((lambda idx, f0=f0: f0(((0, 1, 2), (3, 4, 5))[idx % 2]),
                                  lambda idx, f1=f1: f1(((0, 1, 2), (3, 4, 5))[idx % 2])))
            drain(run_items(items))
            out_proj(mixC, 4, woC, B_mixC, B_woC, banks=(6, 7))
            P.barrier()
            ar.reset(mS)
            wkvb = ar.alloc([128, 1024], BF16)
            QsT = ar.alloc([128, 8, 64], BF16)
            KnT = ar.alloc([128, 8, 64], BF16)
            VE16 = ar.alloc([128, 8, 128], BF16)
            mKeep2 = ar.mark()
            ckvT = ar.alloc([128, T], BF16)
            kpeT = ar.alloc([128, T], BF16)
            VE = ar.alloc([128, 16, 8, 128], BF16)
            mKeep = ar.mark()
            qan = ar.alloc([128, 2, T], BF16)
            tabq = ar.alloc([128, T], F32)
            wD = ar.alloc([128, 8, 448], BF16)
            wqb = ar.alloc([128, 2, 8, 128], BF16)
            zk = [ar.alloc([128, 512], F32) for _ in range(2)]
            zk2 = ar.alloc([128, 512], F32)
            c32 = ar.alloc([128, 512], F32)
            zb = [ar.alloc([128, 512], BF16) for _ in range(2)]
            sq2 = ar.alloc([128, 512], BF16)
            (B_ckvT, B_kpeT, B_VE, B_wkvb, B_QsT, B_KnT, B_qan, B_tabq, B_wD, B_wqb, B_zk2, B_c32, B_sq2, B_VE16) = \
                [nb(x) for x in ("ckvT", "kpeT", "VE", "wkvb", "QsT", "KnT", "qan", "tabq", "wD", "wqb", "zk2",
                                 "c32", "sq2", "VE16")]
            B_zk = [nb("zk") for _ in range(2)]
            B_zb = [nb("zb") for _ in range(2)]
            dma("pool", wD[:, :, 0:416], osrc(1536, 1952), [], [B_wD], B_wD)
            dma("pool", wD[:, :, 416:432], osrc(1936, 1952), [], [B_wD], B_wD)
            dma("pool", wD[:, :, 432:448], osrc(1920, 1936), [], [B_wD], B_wD)
            dma("pool", wkvb[:], wkvb_d, [], [B_wkvb], B_wkvb)
            for kc in range(2):
                dma("pool", wqb[:, kc, :, :], wqb_d[kc * 128:(kc + 1) * 128, :, :], [], [B_wqb], B_wqb)
            dma("sp", tabq[:], tabq_d, [], [B_tabq], B_tabq)
            memset("pool", VE[:, :, :, 64:128], 1.0, [B_VE])
            memset("pool", VE16[:, :, 64:128], 1.0, [B_VE16])
            wkv_v = wkvb[:, 512:1024]

            def VEt(i):
                return VE[:, i] if i < 16 else VE16

            for bi in range(5):
                c0, n = BLKS[bi]
                for kc2 in range(2):
                    for kc in range(8):
                        mm(psb[kc2][:, 0:n], wD[:, kc, kc2 * 128:(kc2 + 1) * 128], hn[:, kc, c0:c0 + n], kc == 0, kc == 7,
                           [B_wD, B_hn[bi]], [PB[kc2]])
                for kc in range(8):
                    mm(psb[3][:, 0:n], wD[:, kc, 256:384], hn[:, kc, c0:c0 + n], kc == 0, kc == 7, [B_wD, B_hn[bi]], [PB[3]])
                for kc in range(8):
                    mm(psb[4][64:128, 0:n], wD[:, kc, 384:448], hn[:, kc, c0:c0 + n], kc == 0, kc == 7, [B_wD, B_hn[bi]], [PB[4]])
                act(CH.sqc[0][:, 0:n], psb[0][:, 0:n], AF.Square, [PB[0]], [CH.B_sqc[0]])
                act(sq2[:, 0:n], psb[1][:, 0:n], AF.Square, [PB[1]], [B_sq2])
                act(CH.sqc[1][:, 0:n], psb[3][:, 0:n], AF.Square, [PB[3]], [CH.B_sqc[1]])
                act(CH.sqc[2][64:96, 0:n], psb[4][64:96, 0:n], AF.Square, [PB[4]], [CH.B_sqc[2]])
                mm(psb[2][:, 0:n], cmat[:, CM_ONES256, :], CH.sqc[0][:, 0:n], True, False, [CH.B_sqc[0], B_const], [PB[2]])
                mm(psb[2][:, 0:n], cmat[:, CM_ONES256, :], sq2[:, 0:n], False, True, [B_sq2, B_const], [PB[2]])
                mm(psb[6][:, 0:n], cmat[:, CM_ONES128, :], CH.sqc[1][:, 0:n], True, True, [CH.B_sqc[1], B_const], [PB[6]])
                mm(psb[5][64:96, 0:n], cmat[64:96, CM_BD3, 64:96], CH.sqc[2][64:96, 0:n], True, True, [CH.B_sqc[2], B_const], [PB[5]])
                rstd_from(CH.rsc[0][:, 0:n], psb[2][:, 0:n], [PB[2]], CH.B_rsc[0])
                for kc2 in range(2):
                    stt("dve", qan[:, kc2, c0:c0 + n], psb[kc2][:, 0:n], vec[:, 52 + kc2:53 + kc2], CH.rsc[0][:, 0:n],
                        ALU.mult, ALU.mult, [PB[kc2], CH.B_rsc[0], B_const], [B_qan])
                rstd_from(CH.rsc[1][:, 0:n], psb[6][:, 0:n], [PB[6]], CH.B_rsc[1])
                stt("dve", c32[:, 0:n], psb[3][:, 0:n], vec[:, 54:55], CH.rsc[1][:, 0:n], ALU.mult, ALU.mult,
                    [PB[3], CH.B_rsc[1], B_const], [B_c32])
                cp("pool", ckvT[:, c0:c0 + n], c32[:, 0:n], [B_c32], [B_ckvT])
                rstd_from(zk2[64:96, 0:n], psb[5][64:96, 0:n], [PB[5]], B_zk2)
                stt("dve", zk[0][64:128, 0:n], psb[4][64:128, 0:n], vec[64:128, 56:57], tabq[64:128, c0:c0 + n],
                    ALU.mult, ALU.mult, [PB[4], B_const, B_tabq], [B_zk[0]])
                cp("act", zk[1][64:96, 0:n], zk[0][96:128, 0:n], [B_zk[0]], [B_zk[1]])
                tt("dve", zk[1][64:96, 0:n], zk[0][64:96, 0:n], zk[1][64:96, 0:n], ALU.add, [B_zk[0], B_zk[1]], [B_zk[1]])
                tt("dve", zk[0][64:96, 0:n], zk[1][64:96, 0:n], zk2[64:96, 0:n], ALU.mult, [B_zk[1], B_zk2], [B_zk[0]])
                cp("pool", kpeT[64:96, c0:c0 + n], zk[0][64:96, 0:n], [B_zk[0]], [B_kpeT])
                for i in range(NTILE):
                    r0, nr = tile_rows(i)
                    if not (c0 <= r0 < c0 + n):
                        continue
                    bk = 6 + i % 2
                    tr(psb[bk][0:nr, 0:128], c32[:, r0 - c0:r0 - c0 + nr], ident[:, :], [B_c32, B_const], [PB[bk]])
                    tr(psb[bk][0:nr, 128:160], zk[0][64:96, r0 - c0:r0 - c0 + nr], ident[64:96, 64:96], [B_zk[0], B_const], [PB[bk]])
                    s = OSt.cnt % 2
                    OSt.cnt += 1
                    cp(ev_eng(), OSt.ostg[s][0:nr, 0:128], psb[bk][0:nr, 0:128], [PB[bk]], [OSt.B[s]])
                    dma("sp", o_ckvp[r0:r0 + nr, :] if i < 16 else o_ckvs[:, :], OSt.ostg[s][0:nr, 0:128], [OSt.B[s]], [],
                        OSt.B[s], is_out=True)
                    s = OSt.cnt % 2
                    OSt.cnt += 1
                    cp(ev_eng(), OSt.ostg[s][0:nr, 0:32], psb[bk][0:nr, 128:160], [PB[bk]], [OSt.B[s]])
                    dma("sp", o_kpep[r0:r0 + nr, :] if i < 16 else o_kpes[:, :], OSt.ostg[s][0:nr, 0:32], [OSt.B[s]], [],
                        OSt.B[s], is_out=True)
                for i in range(NTILE):
                    r0, nr = tile_rows(i)
                    if not (c0 <= r0 < c0 + n):
                        continue
                    bk = i % 2
                    mm(psb[bk][0:nr, 0:512], ckvT[:, r0:r0 + nr], wkv_v, True, True, [B_ckvT, B_wkvb], [PB[bk]])
                    cp("dve", VEt(i)[0:nr, :, 0:64], psb[bk][0:nr, 0:512].rearrange("p (h d) -> p h d", h=8),
                       [PB[bk]], [B_VE if i < 16 else B_VE16])
            P.barrier()
            mixD = hn
            B_mixD = nb("mixD")
            B_Qh = [nb("Qh") for _ in range(2)]
            B_Kh = [nb("Kh") for _ in range(2)]
            sc_d = float(96.0 ** -0.5)
            PJ, SSB, AMB = (0, 1), (2,), 3

            def head_chain_gen(h):
                s = h % 2
                QhT = hn[:, 4 + s, :]
                KhT = hn[:, 6 + s, :]
                items = []
                for bi in range(5):
                    c0, n = BLKS[bi]

                    def q0f(idx, c0=c0, n=n):
                        b = PJ[idx % 2]
                        for kc in range(2):
                            mm(psb[b][:, 0:n], wqb[:, kc, h, :], qan[:, kc, c0:c0 + n], kc == 0, kc == 1, [B_wqb, B_qan], [PB[b]])
                        act(CH.sqc[idx % 3][:, 0:n], psb[b][:, 0:n], AF.Square, [PB[b]], [CH.B_sqc[idx % 3]])

                    def q1f(idx, c0=c0, n=n):
                        b = PJ[idx % 2]
                        sl, rl, zl = idx % 3, idx % 2, (idx // 2) % 2
                        mm(psb[SSB[0]][:, 0:n], cmat[:, CM_BD3, :], CH.sqc[sl][:, 0:n], True, True, [CH.B_sqc[sl], B_const], [PB[SSB[0]]])
                        rstd_from(CH.rsc[rl][:, 0:n], psb[SSB[0]][:, 0:n], [PB[SSB[0]]], CH.B_rsc[rl])
                        stt("dve", zk[zl][:, 0:n], psb[b][:, 0:n], vec[:, 55:56], tabq[:, c0:c0 + n], ALU.mult, ALU.mult,
                            [PB[b], B_const, B_tabq], [B_zk[zl]])
                        tt("dve", zb[zl][:, 0:n], zk[zl][:, 0:n], CH.rsc[rl][:, 0:n], ALU.mult, [B_zk[zl], CH.B_rsc[rl]], [B_zb[zl]])

                    def q2f(idx, bi=bi, c0=c0, n=n):
                        zl = (idx // 2) % 2
                        mm(psb[AMB][0:96, 0:n], cmat[:, CM_AMAT, 0:96], zb[zl][:, 0:n], True, True, [B_zb[zl], B_const], [PB[AMB]])
                        if bi < 4:
                            cp("dve", QhT[0:96, c0:c0 + n], psb[AMB][0:96, 0:n], [PB[AMB]], [B_Qh[s]])
                        else:
                            cp("dve", QsT[0:96, h, :], psb[AMB][0:96, 0:n], [PB[AMB]], [B_QsT])
                    items.append((q0f, q1f, q2f))

                    def kpm(b, c0=c0, n=n):
                        mm(psb[b][0:64, 0:n], wkvb[:, h * 64:(h + 1) * 64], ckvT[:, c0:c0 + n], True, True, [B_wkvb, B_ckvT], [PB[b]])
                    if bi < 4:
                        dests = [(KhT[0:64, c0:c0 + n], B_Kh[s])]
                        post = lambda c0=c0, n=n: cp("pool", KhT[64:96, c0:c0 + n], kpeT[64:96, c0:c0 + n], [B_kpeT], [B_Kh[s]])
                    else:
                        dests = [(KnT[0:64, h, :], B_KnT)]
                        post = lambda c0=c0, n=n: cp("pool", KnT[64:96, h, :], kpeT[64:96, c0:c0 + n], [B_kpeT], [B_KnT])
                    k0f, k1f = CH.item(kpm, (0, 64), n, cmat[0:64, CM_BD64, 0:64], vec[0:64, 57:58], dests, PJ, SSB, post)
                    items.append((k0f, k1f, lambda idx: None))
                return sw_pipeline(items, [lambda it, i: it[0](i), lambda it, i: it[1](i), lambda it, i: it[2](i)], 1)

            def head_attn_gen(h):
                s = h % 2
                QhT = hn[:, 4 + s, :]
                KhT = hn[:, 6 + s, :]
                groups = []
                for i in range(4):
                    q0 = 512 * i
                    tiles = []
                    for kb in range(4 * i + 4):
                        ca = max(0, 128 * kb - 512 * i)
                        ncols = 512 - ca
                        diag = kb >= 4 * i
                        if diag:
                            pv = [(0, ca, ca + 64, VE[0:64, kb, h, :], 64, 0, 64, [B_VE])]
                            if ncols > 64:
                                pv.append((0, ca + 64, 512, VE[:, kb, h, :], 128, 64, ncols, [B_VE]))
                        else:
                            pv = [(0, ca, 512, VE[:, kb, h, :], 128, 0, ncols, [B_VE])]
                        tiles.append(dict(KT=KhT[0:96, kb * 128:(kb + 1) * 128], qrhs=QhT[0:96, q0 + ca:q0 + 512], nk=128,
                                          W=ncols, Gap=None, R=[B_Kh[s], B_Qh[s]], pv=pv))

                    def fin(obs, q0=q0):
                        drow = slice((h % 2) * 64, (h % 2) * 64 + 64)
                        attn_finish(AS, obs[0], 512, None, mixD[drow, h // 2, q0:q0 + 512], B_mixD, use_dve=True)
                    groups.append(dict(tiles=tiles, nO=1, finish=fin))
                return attention_gen(AS, groups, sc_d, (4, 5), (6, 7))

            drain(head_chain_gen(0))
            for h in range(8):
                drain(head_attn_gen(h), head_chain_gen(h + 1) if h < 7 else None)
            P.barrier()
            ar.reset(mKeep2)
            woD = ar.alloc([128, 4, 1024], BF16)
            B_woD = nb("woD")
            dma("pool", woD[:], odout_d[512:1024, :].rearrange("(k p) o -> p k o", p=128), [], [B_woD], B_woD)
            NS = 3
            cst = [ar.alloc([128, 4, 128], F32) for _ in range(NS)]
            kst = [ar.alloc([128, 4, 32], F32) for _ in range(NS)]
            ckc = [ar.alloc([128, 512], BF16) for _ in range(NS)]
            kpc = [ar.alloc([128, 512], BF16) for _ in range(NS)]
            KcT = [ar.alloc([128, 8, 512], BF16) for _ in range(NS)]
            VEc = [ar.alloc([128, 4, 8, 128], BF16) for _ in range(NS)]
            B_cst = [nb("cst") for _ in range(NS)]
            B_kst = [nb("kst") for _ in range(NS)]
            B_ckc = [nb("ckc") for _ in range(NS)]
            B_kpc = [nb("kpc") for _ in range(NS)]
            B_KcT = [nb("KcT") for _ in range(NS)]
            B_VEc = [nb("VEc") for _ in range(NS)]
            for s in range(NS):
                memset("pool", VEc[s][:, :, :, 64:128], 1.0, [B_VEc[s]])
            OS = 7
            PJs, SSBs = (0, 1, 2), (3,)

            def load_grp(g):
                s = g % NS
                dma("sp", cst[s][:], ckv_d[g * 512:(g + 1) * 512, :].rearrange("(t p) l -> p t l", p=128), [], [B_cst[s]], B_cst[s])
                dma("sp", kst[s][:], ckp_d[g * 512:(g + 1) * 512, :].rearrange("(t p) l -> p t l", p=128), [], [B_kst[s]], B_kst[s])

            def prep_gen(g):
                s = g % NS
                if g + 2 < 8:
                    load_grp(g + 2)
                for t4 in range(4):
                    tr(psb[0][:, t4 * 128:(t4 + 1) * 128], cst[s][:, t4, :], ident[:, :], [B_cst[s], B_const], [PB[0]])
                cp("act", ckc[s][:, :], psb[0][:, :], [PB[0]], [B_ckc[s]])
                for t4 in range(4):
                    tr(psb[1][0:32, t4 * 128:(t4 + 1) * 128], kst[s][:, t4, :], ident[:, :], [B_kst[s], B_const], [PB[1]])
                cp("dve", kpc[s][64:96, :], psb[1][0:32, :], [PB[1]], [B_kpc[s]])
                yield
                items = []
                for t4 in range(4):
                    def f0(b, t4=t4):
                        mm(psb[b][:, 0:512], ckc[s][:, t4 * 128:(t4 + 1) * 128], wkv_v, True, True, [B_ckc[s], B_wkvb], [PB[b]])

                    def f1(b, t4=t4):
                        cp("dve", VEc[s][:, t4, :, 0:64], psb[b][:, 0:512].rearrange("p (h d) -> p h d", h=8), [PB[b]], [B_VEc[s]])
                    items.append(simple_item(f0, f1, PJs))
                for hp in range(4):
                    def kpm(b, hp=hp):
                        mm(psb[b][:, 0:512], wkvb[:, hp * 128:(hp + 1) * 128], ckc[s][:, :], True, True, [B_wkvb, B_ckc[s]], [PB[b]])
                    dests = [(KcT[s][0:64, 2 * hp, :], B_KcT[s], (0, 64), vec[0:64, 57:58]),
                             (KcT[s][0:64, 2 * hp + 1, :], B_KcT[s], (64, 128), vec[64:128, 57:58])]

                    items.append(CH.item(kpm, (0, 128), 512, cmat[:, CM_BD64, :], vec[:, 57:58], dests, PJs, SSBs, None))
                for _ in run_items(items):
                    yield

            started = [False]

            def tiles_gen(g):
                s = g % NS
                items = list(range(4))

                def s0(t4, idx):
                    sl4 = (4 * g + t4) % 4
                    sbk = (4, 5)[(4 * g + t4) % 2]
                    mm(psb[sbk][:, 0:512].rearrange("p (h q) -> p h q", h=8), kpc[s][64:96, t4 * 128:(t4 + 1) * 128],
                       QsT[64:96, :, :], True, False, [B_kpc[s], B_QsT], [PB[sbk]])
                    for h in range(8):
                        mm(psb[sbk][:, h * 64:(h + 1) * 64], KcT[s][0:64, h, t4 * 128:(t4 + 1) * 128], QsT[0:64, h, :], False, False,
                           [B_KcT[s], B_QsT], [PB[sbk]])
                    act(AS.pb[sl4][:, 0:512], psb[sbk][:, 0:512], AF.Exp, [PB[sbk]], [AS.B_pb[sl4]], scale=sc_d)

                def s1(t4, idx):
                    sl4 = (4 * g + t4) % 4
                    for h in range(8):
                        mm(psb[OS][:, h * 64:(h + 1) * 64], VEc[s][:, t4, h, :], AS.pb[sl4][:, h * 64:(h + 1) * 64],
                           not started[0], False, [AS.B_pb[sl4], B_VEc[s]], [PB[OS]])
                        started[0] = True
                return sw_pipeline(items, [s0, s1], 1)

            load_grp(0)
            load_grp(1)
            drain(prep_gen(0))
            drain(prep_gen(1))
            for g in range(8):
                drain(tiles_gen(g), prep_gen(g + 2) if g + 2 < 8 else None)
            sbk = 4
            for h in range(8):
                mm(psb[sbk][0:64, h * 64:(h + 1) * 64], KnT[0:96, h, :], QsT[0:96, h, :], True, True, [B_KnT, B_QsT], [PB[sbk]])
            act(AS.pb[0][0:64, 0:512], psb[sbk][0:64, 0:512], AF.Exp, [PB[sbk]], [AS.B_pb[0]], scale=sc_d)
            for h in range(8):
                mm(psb[OS][:, h * 64:(h + 1) * 64], VE16[0:64, h, :], AS.pb[0][0:64, h * 64:(h + 1) * 64], False, False,
                   [AS.B_pb[0], B_VE16], [PB[OS]])
            act(AS.rcp[64:128, 0:512], psb[OS][64:128, 0:512], AF.Ln, [PB[OS]], [AS.B_rcp])
            act(AS.rcp2[0:64, 0:512], AS.rcp[64:128, 0:512], AF.Exp, [AS.B_rcp], [AS.B_rcp2], scale=-1.0)
            for h in range(8):
                drow = slice((h % 2) * 64, (h % 2) * 64 + 64)
                tt("dve", mixD[drow, h // 2, 2048:2112], psb[OS][0:64, h * 64:(h + 1) * 64], AS.rcp2[0:64, h * 64:(h + 1) * 64],
                   ALU.mult, [PB[OS], AS.B_rcp2], [B_mixD])
            out_proj(mixD, 4, woD, B_mixD, B_woD)
            P.barrier()

        stages = ["ffn1_0", "mix_0", "ffn2_0", "ffn1_1", "mix_1", "ffn2_1"]
        for st in stages:
            if stop is not None and st == stop:
                break
            if st == "ffn1_0":
                ffn(0, 0, next_gcol=8)
            elif st == "ffn2_0":
                ffn(0, 1, next_gcol=24, end_barrier=False)
            elif st == "ffn1_1":
                ffn(1, 0, skip_norm01=True, next_gcol=32)
            elif st == "ffn2_1":
                ffn(1, 1, end_barrier=False)
            elif st == "mix_0":
                mix_even()
            elif st == "mix_1":
                mix_odd()
        final_phase()
        P.finalize_and_emit()
    return nc


OUT_NAMES = ["o_yp", "o_ys", "o_akp", "o_avp", "o_bkp", "o_bvp", "o_cp", "o_ckvp", "o_kpep",
             "o_aks", "o_avs", "o_bks", "o_bvs", "o_cs", "o_ckvs", "o_kpes"]


def make_in_maps(inputs, cores):
    f = lambda a: np.ascontiguousarray(np.asarray(a, dtype=np.float32))
    consts = host_consts()
    vec = pack_vec(inputs)
    wqb = f(inputs["d_w_q_b"][0]).reshape(256, 8, 96)
    nope, rope = wqb[:, :, 0:64], wqb[:, :, 64:96]
    ropesw = np.concatenate([rope[:, :, 16:32], rope[:, :, 0:16]], axis=2)
    wqb_p = np.ascontiguousarray(np.concatenate([nope, rope, ropesw], axis=2))
    wkv = f(inputs["d_w_kv_b"][0]).reshape(128, 8, 128)
    wkvb_p = np.ascontiguousarray(np.concatenate([wkv[:, :, 0:64].reshape(128, 512), wkv[:, :, 64:128].reshape(128, 512)], axis=1))
    shared = {
        "ff1_w_gu": f(inputs["ff1_w_gu"]), "ff2_w_gu": f(inputs["ff2_w_gu"]),
        "ff1_w_down": f(inputs["ff1_w_down"]), "ff2_w_down": f(inputs["ff2_w_down"]),
        "ev_w_in": f(inputs["ev_w_in"][0]), "ev_w_out": f(inputs["ev_w_out"][0]),
        "od_w_in": f(inputs["od_w_in"][0]), "od_w_out": f(inputs["od_w_out"][0]),
        "wqb": wqb_p, "wkvb": wkvb_p, "vec": vec,
        "t5": f(inputs["t5_bias_table"]), "brel": f(inputs["b_rel_bias"][0]),
        "sinks": f(inputs["a_sinks"]).reshape(1, 8),
    }
    shared.update(consts)
    maps = []
    for b in cores:
        m = dict(shared)
        m["xp"] = f(inputs["x_prompt"][b])
        m["xs"] = f(inputs["x_sample"][b])
        m["cak"] = f(inputs["cache_a_k"][0, b]).reshape(128, 128)
        m["cav"] = f(inputs["cache_a_v"][0, b]).reshape(128, 128)
        m["cbk"] = f(inputs["cache_b_k"][0, b]).reshape(512, 512)
        m["cbv"] = f(inputs["cache_b_v"][0, b]).reshape(512, 512)
        m["ccv"] = f(inputs["state_c_conv"][0, b])
        m["ckv"] = f(inputs["cache_d_ckv"][0, b])
        m["ckp"] = f(inputs["cache_d_kpe"][0, b])
        maps.append(m)
    return maps


def assemble(results):
    def st(name, shape):
        return np.stack([np.asarray(r[name], np.float32).reshape(shape) for r in results])[None] \
            if name not in ("o_yp", "o_ys") else np.stack([np.asarray(r[name], np.float32) for r in results])

    return (
        st("o_yp", None), st("o_ys", None),
        st("o_akp", (128, 2, 64)), st("o_avp", (128, 2, 64)),
        st("o_bkp", (512, 8, 64)), st("o_bvp", (512, 8, 64)),
        st("o_cp", (2, 512)), st("o_ckvp", (2048, 128)), st("o_kpep", (2048, 32)),
        st("o_aks", (128, 2, 64)), st("o_avs", (128, 2, 64)),
        st("o_bks", (512, 8, 64)), st("o_bvs", (512, 8, 64)),
        st("o_cs", (2, 512)), st("o_ckvs", (64, 128)), st("o_kpes", (64, 32)),
    )


def kernel(**inputs):
    nc = build(stop=os.environ.get("MK_STOP"))
    maps = make_in_maps(inputs, list(range(8)))
    res = run_bass_kernel_spmd(nc, maps, core_ids=list(range(8)))
    return assemble(res.results)
```
